# Optimizing a Trainium2 kernel written in Bass

```python
import math
import jax
import jax.numpy as jnp
from jax import lax
import numpy as np

D_MODEL = 1024
BATCH = 8
SEQ = 4096
DEPTH = 4

GRID_W = 64
CTX_LEN = 256
N_MIXERS = 3
FFN_HIDDEN = -(-8 * D_MODEL // (3 * 256)) * 256
CONV_WIDTH = 3
MLA_HEADS = D_MODEL // 64
MLA_Q_LORA = 3 * D_MODEL // 4
MLA_KV_LORA = D_MODEL // 4
MLA_NOPE = 64
MLA_ROPE = 32
MLA_V = 64
DIFF_HEADS = 8
DIFF_HEAD_DIM = D_MODEL // (2 * DIFF_HEADS)
ROPE_THETA = 10000.0
NORM_EPS = 1e-6
Q_BLOCK = 128

kernel_name = 'hybrid_conv_mla_diffattn_dit_trunk'


def rms_norm(x, g):
    xf = x.astype(jnp.float32)
    y = xf * lax.rsqrt(jnp.mean(xf * xf, axis=-1, keepdims=True) + NORM_EPS)
    return (y * g.astype(jnp.float32)).astype(x.dtype)


def modulate(h, shift, scale):
    return h * (1.0 + scale) + shift


def swiglu(h, w_gate, w_up, w_down):
    return (jax.nn.silu(h @ w_gate) * (h @ w_up)) @ w_down


def axial_rope_tables(rows, rot_dim):
    axis_dim = rot_dim // 2
    inv_freq = ROPE_THETA ** (-jnp.arange(0, axis_dim, 2, dtype=jnp.float32) / axis_dim)
    row = jnp.repeat(jnp.arange(rows, dtype=jnp.float32), GRID_W)
    col = jnp.tile(jnp.arange(GRID_W, dtype=jnp.float32), rows)
    ang_r = row[:, None] * inv_freq
    ang_c = col[:, None] * inv_freq
    ang = jnp.concatenate([ang_r, ang_r, ang_c, ang_c], axis=-1)
    return jnp.cos(ang), jnp.sin(ang)


def apply_axial_rope(x, cos, sin):
    a = x.shape[-1] // 2
    h = a // 2
    xr, xc = x[..., :a], x[..., a:]
    rot = jnp.concatenate([-xr[..., h:], xr[..., :h], -xc[..., h:], xc[..., :h]], axis=-1)
    return x * cos.astype(x.dtype) + rot * sin.astype(x.dtype)


def sweep_query_blocks(block_fn, *qs):
    b, n = qs[0].shape[:2]
    if n % Q_BLOCK or n == Q_BLOCK:
        return block_fn(*qs)
    nb = n // Q_BLOCK
    blocks = tuple(jnp.swapaxes(a.reshape(b, nb, Q_BLOCK, *a.shape[2:]), 0, 1) for a in qs)
    out = lax.map(lambda blk: block_fn(*blk), blocks)
    out = jnp.swapaxes(out, 0, 1)
    return out.reshape(b, n, *out.shape[3:])


def short_conv_mixer(h, w_in, conv_k, conv_b, w_out):
    b_gate, c_gate, u = jnp.split(h @ w_in, 3, axis=-1)
    z = c_gate * u
    pad = CONV_WIDTH // 2
    y = lax.conv_general_dilated(z, conv_k[:, None, :], window_strides=(1,), padding=((pad, pad),),
                                 dimension_numbers=('NWC', 'WIO', 'NWC'), feature_group_count=D_MODEL)
    y = y + conv_b
    return (b_gate * y) @ w_out


def _mla_project(h, cos, sin, with_q, w_in, q_norm_g, w_q_up, kv_norm_g, w_kv_up,
                 qn_nope_g, qn_rope_g, kn_nope_g, kn_rope_g):
    b, n, _ = h.shape
    proj = h @ (w_in if with_q else w_in[:, MLA_Q_LORA:])
    if with_q:
        q_lat, proj = proj[..., :MLA_Q_LORA], proj[..., MLA_Q_LORA:]
    kv_lat, k_pe = proj[..., :MLA_KV_LORA], proj[..., MLA_KV_LORA:]
    kv = (rms_norm(kv_lat, kv_norm_g) @ w_kv_up).reshape(b, n, MLA_HEADS, MLA_NOPE + MLA_V)
    k_nope = rms_norm(kv[..., :MLA_NOPE], kn_nope_g)
    v = kv[..., MLA_NOPE:]
    k_pe = rms_norm(k_pe, kn_rope_g)
    if cos is not None:
        k_pe = apply_axial_rope(k_pe, cos, sin)
    if not with_q:
        return None, None, k_nope, k_pe, v
    q = (rms_norm(q_lat, q_norm_g) @ w_q_up).reshape(b, n, MLA_HEADS, MLA_NOPE + MLA_ROPE)
    q_nope = rms_norm(q[..., :MLA_NOPE], qn_nope_g)
    q_pe = rms_norm(q[..., MLA_NOPE:], qn_rope_g)
    if cos is not None:
        q_pe = apply_axial_rope(q_pe, cos[:, None, :], sin[:, None, :])
    return q_nope, q_pe, k_nope, k_pe, v


def _mla_block(k_nope, k_pe, v):
    scale = (MLA_NOPE + MLA_ROPE) ** -0.5

    def attend(q_nope, q_pe):
        s = (jnp.einsum('bqhd,bkhd->bhqk', q_nope, k_nope)
             + jnp.einsum('bqhr,bkr->bhqk', q_pe, k_pe))
        p = jax.nn.softmax(s.astype(jnp.float32) * scale, axis=-1).astype(v.dtype)
        return jnp.einsum('bhqk,bkhd->bqhd', p, v)
    return attend


def mla_mixer(a_ctx, a_lat, ctx_out, cos, sin, w_in, q_norm_g, w_q_up, kv_norm_g, w_kv_up,
              qn_nope_g, qn_rope_g, kn_nope_g, kn_rope_g, w_out):
    prm = (w_in, q_norm_g, w_q_up, kv_norm_g, w_kv_up, qn_nope_g, qn_rope_g, kn_nope_g, kn_rope_g)
    cq_n, cq_p, ck_n, ck_p, cv = _mla_project(a_ctx, None, None, ctx_out, *prm)
    lq_n, lq_p, lk_n, lk_p, lv = _mla_project(a_lat, cos, sin, True, *prm)
    b, n = a_lat.shape[:2]
    attend_lat = _mla_block(jnp.concatenate([ck_n, lk_n], axis=1),
                            jnp.concatenate([ck_p, lk_p], axis=1),
                            jnp.concatenate([cv, lv], axis=1))
    o_lat = sweep_query_blocks(attend_lat, lq_n, lq_p).reshape(b, n, MLA_HEADS * MLA_V) @ w_out
    o_ctx = None
    if ctx_out:
        o_ctx = sweep_query_blocks(_mla_block(ck_n, ck_p, cv), cq_n, cq_p)
        o_ctx = o_ctx.reshape(b, a_ctx.shape[1], MLA_HEADS * MLA_V) @ w_out
    return o_ctx, o_lat


def _diff_project(h, cos, sin, with_q, w_in, qn_g, kn_g):
    b, n, _ = h.shape
    proj = h @ (w_in if with_q else w_in[:, D_MODEL:])
    parts = jnp.split(proj, 3 if with_q else 2, axis=-1)
    k = rms_norm(parts[-2].reshape(b, n, DIFF_HEADS, 2, DIFF_HEAD_DIM), kn_g)
    v = parts[-1].reshape(b, n, DIFF_HEADS, 2 * DIFF_HEAD_DIM)
    if cos is not None:
        k = apply_axial_rope(k, cos[:, None, None, :], sin[:, None, None, :])
    if not with_q:
        return None, k, v
    q = rms_norm(parts[0].reshape(b, n, DIFF_HEADS, 2, DIFF_HEAD_DIM), qn_g)
    if cos is not None:
        q = apply_axial_rope(q, cos[:, None, None, :], sin[:, None, None, :])
    return q, k, v


def _diff_block(k, v, lam):
    scale = DIFF_HEAD_DIM ** -0.5
    k1, k2 = k[..., 0, :], k[..., 1, :]

    def attend(q):
        s1 = jnp.einsum('bqhd,bkhd->bhqk', q[..., 0, :], k1).astype(jnp.float32) * scale
        s2 = jnp.einsum('bqhd,bkhd->bhqk', q[..., 1, :], k2).astype(jnp.float32) * scale
        p = jax.nn.softmax(s1, axis=-1) - lam * jax.nn.softmax(s2, axis=-1)
        return jnp.einsum('bhqk,bkhe->bqhe', p.astype(v.dtype), v)
    return attend


def diff_mixer(a_ctx, a_lat, ctx_out, cos, sin, lam_init, w_in, qn_g, kn_g,
               lq1, lk1, lq2, lk2, sub_g, w_out):
    lam = (jnp.exp(jnp.sum(lq1.astype(jnp.float32) * lk1.astype(jnp.float32)))
           - jnp.exp(jnp.sum(lq2.astype(jnp.float32) * lk2.astype(jnp.float32))) + lam_init)
    cq, ck, cv = _diff_project(a_ctx, None, None, ctx_out, w_in, qn_g, kn_g)
    lq, lk, lv = _diff_project(a_lat, cos, sin, True, w_in, qn_g, kn_g)

    def finish(o):
        o = rms_norm(o, sub_g) * (1.0 - lam_init)
        return o.reshape(o.shape[0], o.shape[1], D_MODEL) @ w_out

    attend_lat = _diff_block(jnp.concatenate([ck, lk], axis=1), jnp.concatenate([cv, lv], axis=1), lam)
    o_lat = finish(sweep_query_blocks(attend_lat, lq))
    o_ctx = finish(sweep_query_blocks(_diff_block(ck, cv, lam), cq)) if ctx_out else None
    return o_ctx, o_lat


def setup_inputs(seed: int = 0) -> dict:
    key = jax.random.key(seed)
    ks = iter(jax.random.split(key, 40))

    def nrm(shape, scale):
        return jax.random.normal(next(ks), shape, jnp.float32) * scale

    def gain(shape):
        return 1.0 + 0.02 * jax.random.normal(next(ks), shape, jnp.float32)

    n_a, n_b, n_c = (len(range(k, DEPTH, N_MIXERS)) for k in range(N_MIXERS))
    d, f = D_MODEL, FFN_HIDDEN
    return {
        'x': nrm((BATCH, SEQ, d), 1.0),
        'c': nrm((BATCH, d), 1.0),
        'ctx': nrm((BATCH, CTX_LEN, d), 1.0),
        'c_ctx': nrm((d,), 1.0),
        'norm_mix_g': gain((DEPTH, d)),
        'norm_ffn_g': gain((DEPTH, d)),
        'mod_w': nrm((DEPTH, d, 6 * d), 0.5 * d ** -0.5),
        'mod_b': nrm((DEPTH, 6 * d), 0.02),
        'ffn_w_gate': nrm((DEPTH, d, f), d ** -0.5),
        'ffn_w_up': nrm((DEPTH, d, f), d ** -0.5),
        'ffn_w_down': nrm((DEPTH, f, d), f ** -0.5),
        'conv_w_in': nrm((n_a, d, 3 * d), d ** -0.5),
        'conv_k': nrm((n_a, CONV_WIDTH, d), CONV_WIDTH ** -0.5),
        'conv_b': nrm((n_a, d), 0.02),
        'conv_w_out': nrm((n_a, d, d), d ** -0.5),
        'mla_w_in': nrm((n_b, d, MLA_Q_LORA + MLA_KV_LORA + MLA_ROPE), d ** -0.5),
        'mla_q_norm_g': gain((n_b, MLA_Q_LORA)),
        'mla_w_q_up': nrm((n_b, MLA_Q_LORA, MLA_HEADS * (MLA_NOPE + MLA_ROPE)), MLA_Q_LORA ** -0.5),
        'mla_kv_norm_g': gain((n_b, MLA_KV_LORA)),
        'mla_w_kv_up': nrm((n_b, MLA_KV_LORA, MLA_HEADS * (MLA_NOPE + MLA_V)), MLA_KV_LORA ** -0.5),
        'mla_qn_nope_g': gain((n_b, MLA_NOPE)),
        'mla_qn_rope_g': gain((n_b, MLA_ROPE)),
        'mla_kn_nope_g': gain((n_b, MLA_NOPE)),
        'mla_kn_rope_g': gain((n_b, MLA_ROPE)),
        'mla_w_out': nrm((n_b, MLA_HEADS * MLA_V, d), (MLA_HEADS * MLA_V) ** -0.5),
        'diff_w_in': nrm((n_c, d, 3 * d), d ** -0.5),
        'diff_qn_g': gain((n_c, DIFF_HEAD_DIM)),
        'diff_kn_g': gain((n_c, DIFF_HEAD_DIM)),
        'diff_lq1': nrm((n_c, DIFF_HEAD_DIM), 0.1),
        'diff_lk1': nrm((n_c, DIFF_HEAD_DIM), 0.1),
        'diff_lq2': nrm((n_c, DIFF_HEAD_DIM), 0.1),
        'diff_lk2': nrm((n_c, DIFF_HEAD_DIM), 0.1),
        'diff_sub_g': gain((n_c, 2 * DIFF_HEAD_DIM)),
        'diff_w_out': nrm((n_c, d, d), d ** -0.5),
    }


def reference(x, c, ctx, c_ctx, norm_mix_g, norm_ffn_g, mod_w, mod_b, ffn_w_gate, ffn_w_up, ffn_w_down,
              conv_w_in, conv_k, conv_b, conv_w_out,
              mla_w_in, mla_q_norm_g, mla_w_q_up, mla_kv_norm_g, mla_w_kv_up,
              mla_qn_nope_g, mla_qn_rope_g, mla_kn_nope_g, mla_kn_rope_g, mla_w_out,
              diff_w_in, diff_qn_g, diff_kn_g, diff_lq1, diff_lk1, diff_lq2, diff_lk2, diff_sub_g, diff_w_out):
    n_lat = x.shape[1]
    rows = n_lat // GRID_W
    cos_mla, sin_mla = axial_rope_tables(rows, MLA_ROPE)
    cos_diff, sin_diff = axial_rope_tables(rows, DIFF_HEAD_DIM)
    silu_c = jax.nn.silu(c)
    silu_cctx = jax.nn.silu(c_ctx)
    h_lat, h_ctx = x, ctx
    for i in range(DEPTH):
        kind, slot = i % N_MIXERS, i // N_MIXERS
        last = i == DEPTH - 1
        sh1, sc1, g1, sh2, sc2, g2 = jnp.split((silu_c @ mod_w[i] + mod_b[i])[:, None, :], 6, axis=-1)
        a_lat = modulate(rms_norm(h_lat, norm_mix_g[i]), sh1, sc1)
        need_ctx_in = not (last and kind == 0)
        if need_ctx_in:
            csh1, csc1, cg1, csh2, csc2, cg2 = jnp.split(silu_cctx @ mod_w[i] + mod_b[i], 6, axis=-1)
            a_ctx = modulate(rms_norm(h_ctx, norm_mix_g[i]), csh1, csc1)
        if kind == 0:
            o_lat = short_conv_mixer(a_lat, conv_w_in[slot], conv_k[slot], conv_b[slot], conv_w_out[slot])
            o_ctx = None if last else short_conv_mixer(a_ctx, conv_w_in[slot], conv_k[slot], conv_b[slot],
                                                       conv_w_out[slot])
        elif kind == 1:
            o_ctx, o_lat = mla_mixer(a_ctx, a_lat, not last, cos_mla, sin_mla,
                                     mla_w_in[slot], mla_q_norm_g[slot], mla_w_q_up[slot],
                                     mla_kv_norm_g[slot], mla_w_kv_up[slot],
                                     mla_qn_nope_g[slot], mla_qn_rope_g[slot],
                                     mla_kn_nope_g[slot], mla_kn_rope_g[slot], mla_w_out[slot])
        else:
            lam_init = 0.8 - 0.6 * math.exp(-0.3 * i)
            o_ctx, o_lat = diff_mixer(a_ctx, a_lat, not last, cos_diff, sin_diff, lam_init,
                                      diff_w_in[slot], diff_qn_g[slot], diff_kn_g[slot],
                                      diff_lq1[slot], diff_lk1[slot], diff_lq2[slot], diff_lk2[slot],
                                      diff_sub_g[slot], diff_w_out[slot])
        h_lat = h_lat + g1 * o_lat
        h_lat = h_lat + g2 * swiglu(modulate(rms_norm(h_lat, norm_ffn_g[i]), sh2, sc2),
                                    ffn_w_gate[i], ffn_w_up[i], ffn_w_down[i])
        if not last:
            h_ctx = h_ctx + cg1 * o_ctx
            h_ctx = h_ctx + cg2 * swiglu(modulate(rms_norm(h_ctx, norm_ffn_g[i]), csh2, csc2),
                                         ffn_w_gate[i], ffn_w_up[i], ffn_w_down[i])
    return h_lat
```

```python
import contextlib
import math
import numpy as np
import concourse.bass as bass
import concourse.mybir as mybir
from concourse.bass_utils import run_bass_kernel_spmd

F32 = mybir.dt.float32
BF16 = mybir.dt.bfloat16
AF = mybir.ActivationFunctionType
ALU = mybir.AluOpType
AX = mybir.AxisListType

D = 1024
TCX = 256
TL = 4096
T = TCX + TL
NT = T // 128
FF = 2816
NF = FF // 128
EPS = 1e-6
DEPTH = 4
SAME_ENG_SYNC = True


class Buf:
    __slots__ = ("name", "w", "rc", "rd", "excl")

    def __init__(self, name="", excl=False):
        self.name = name
        self.w = None
        self.rc = {}
        self.rd = {}
        self.excl = excl


class Op:
    __slots__ = ("idx", "eng", "fn", "cdeps", "dwaits", "dma", "need_inc", "val")

    def __init__(self, idx, eng, fn):
        self.idx = idx
        self.eng = eng
        self.fn = fn
        self.cdeps = set()
        self.dwaits = {}
        self.dma = None
        self.need_inc = False
        self.val = 0


class Sched:
    ENGS = ("pe", "act", "dve", "pool", "sp")

    def __init__(self, nc, n_dma_sems=48):
        self.nc = nc
        self.ops = []
        self.n_dma_sems = n_dma_sems
        self.dma_cnt = [0] * n_dma_sems
        self.last_op = {e: None for e in self.ENGS}
        self.barrier = {e: None for e in self.ENGS}

    def _dep_on(self, op, w):
        if w is None:
            return
        if w[0] == "c":
            op.cdeps.add(w[1])
        else:
            s = w[1]
            op.dwaits[s] = max(op.dwaits.get(s, 0), self.dma_cnt[s] * 16)

    def add(self, eng, fn, reads=(), writes=(), dma_sem=None):
        op = Op(len(self.ops), eng, fn)
        if self.barrier[eng] is not None:
            cd, dw = self.barrier[eng]
            op.cdeps.update(cd)
            for s, v in dw.items():
                op.dwaits[s] = max(op.dwaits.get(s, 0), v)
            self.barrier[eng] = None
        rds = [b for b in reads if not b.excl]
        wrs = list(writes) + [b for b in reads if b.excl]
        for b in rds:
            self._dep_on(op, b.w)
        for b in wrs:
            self._dep_on(op, b.w)
            for e, i in b.rc.items():
                op.cdeps.add(i)
            for s in b.rd:
                op.dwaits[s] = max(op.dwaits.get(s, 0), self.dma_cnt[s] * 16)
        if dma_sem is not None:
            self.dma_cnt[dma_sem] += 1
            op.dma = (dma_sem, self.dma_cnt[dma_sem] * 16)
            me = ("d", dma_sem, op.dma[1])
        else:
            me = ("c", op.idx)
        for b in rds:
            if dma_sem is not None:
                b.rd[dma_sem] = op.dma[1]
            else:
                b.rc[eng] = op.idx
        for b in wrs:
            b.w = me
            b.rc = {}
            b.rd = {}
        op.cdeps.discard(op.idx)
        self.ops.append(op)
        self.last_op[eng] = op.idx
        return op

    def full_barrier(self):
        cd = set(i for i in self.last_op.values() if i is not None)
        dw = {s: c * 16 for s, c in enumerate(self.dma_cnt) if c}
        for e in self.ENGS:
            self.barrier[e] = (set(cd), dict(dw))

    def emit(self):
        nc = self.nc
        ops = self.ops

        def skip(y, op):
            return y.eng == op.eng and (y.eng == "pe" or not SAME_ENG_SYNC)

        for op in ops:
            for d in op.cdeps:
                y = ops[d]
                if y.dma is None and not skip(y, op):
                    y.need_inc = True
        cnt = {e: 0 for e in self.ENGS}
        for op in ops:
            if op.dma is None and op.need_inc:
                cnt[op.eng] += 1
                op.val = cnt[op.eng]
        self.stats = dict(cnt)
        by_eng = {e: [op for op in ops if op.eng == e] for e in self.ENGS}
        for e in self.ENGS:
            self.stats["n_" + e] = len(by_eng[e])
        with contextlib.ExitStack() as st:
            esem = {e: st.enter_context(nc.semaphore("s_" + e)) for e in self.ENGS}
            dsem = [st.enter_context(nc.semaphore("d%d" % i)) for i in range(self.n_dma_sems)]
            block = st.enter_context(nc.Block())
            final_dw = {s: c * 16 for s, c in enumerate(self.dma_cnt) if c}

            def run(e_name, eng):
                seen = {}
                for op in by_eng[e_name]:
                    waits = {}
                    for d in op.cdeps:
                        y = ops[d]
                        if y.dma is not None:
                            k = ("d", y.dma[0])
                            waits[k] = max(waits.get(k, 0), y.dma[1])
                        elif not skip(y, op):
                            k = ("e", y.eng)
                            waits[k] = max(waits.get(k, 0), y.val)
                    for s, v in op.dwaits.items():
                        k = ("d", s)
                        waits[k] = max(waits.get(k, 0), v)
                    for k, v in waits.items():
                        if seen.get(k, 0) >= v:
                            continue
                        seen[k] = v
                        eng.wait_ge(esem[k[1]] if k[0] == "e" else dsem[k[1]], v)
                    ins = op.fn(eng)
                    if op.dma is not None:
                        ins.then_inc(dsem[op.dma[0]], 16)
                    elif op.need_inc:
                        ins.then_inc(esem[e_name], 1)
                if e_name == "sp":
                    for s, v in final_dw.items():
                        eng.wait_ge(dsem[s], v)

            @block.tensor
            def _(eng):
                run("pe", eng)

            @block.scalar
            def _(eng):
                run("act", eng)

            @block.vector
            def _(eng):
                run("dve", eng)

            @block.gpsimd
            def _(eng):
                run("pool", eng)

            @block.sync
            def _(eng):
                run("sp", eng)


class Tl:
    __slots__ = ("ap", "b", "sem")

    def __init__(self, ap, b, sem=None):
        self.ap = ap
        self.b = b
        self.sem = sem


class Ring:
    def __init__(self, tiles):
        self.tiles = tiles
        self.i = 0

    def next(self):
        t = self.tiles[self.i % len(self.tiles)]
        self.i += 1
        return t


ARENA_WORDS = 52224


class KB:
    def __init__(self, nc, S, arena, ps):
        self.nc = nc
        self.S = S
        self.A = arena
        self.AB = arena.bitcast(BF16)
        self.ps = ps
        self.psb = [p.bitcast(BF16) for p in ps]
        self.pb = [Buf("psum%d" % i, excl=True) for i in range(8)]
        self.bank_i = 0
        self.bank_ctr = {}
        self.base = 0
        self.ptr = 0
        self.sem_base = 0
        self.sem_i = 0
        self.peak = 0

    def persist(self):
        self.base = self.ptr
        self.sem_base = self.sem_i

    def new_phase(self):
        self.S.full_barrier()
        self.ptr = self.base
        self.sem_i = self.sem_base

    def _alloc(self, words):
        words = (words + 15) // 16 * 16
        off = self.ptr
        self.ptr += words
        self.peak = max(self.peak, self.ptr)
        assert self.ptr <= ARENA_WORDS, "SBUF arena overflow %d" % self.ptr
        return off

    def sem(self):
        s = self.sem_i
        self.sem_i += 1
        assert s < self.S.n_dma_sems
        return s

    @staticmethod
    def _shape(ap, shape):
        if len(shape) == 2:
            return ap
        if len(shape) == 3:
            return ap.rearrange("p (a b) -> p a b", a=shape[1])
        if len(shape) == 4:
            return ap.rearrange("p (a b c) -> p a b c", a=shape[1], b=shape[2])
        raise ValueError(shape)

    def f32(self, shape, name="", dma=False):
        n = int(np.prod(shape[1:]))
        off = self._alloc(n)
        ap = self._shape(self.A[0:shape[0], off:off + n], shape)
        return Tl(ap, Buf(name), self.sem() if dma else None)

    def bf(self, shape, name="", dma=False):
        n = int(np.prod(shape[1:]))
        off = self._alloc((n + 1) // 2)
        ap = self._shape(self.AB[0:shape[0], 2 * off:2 * off + n], shape)
        return Tl(ap, Buf(name), self.sem() if dma else None)

    def ring(self, kind, shape, n, name="", dma=False):
        mk = self.f32 if kind == "f32" else self.bf
        return Ring([mk(shape, "%s%d" % (name, i), dma) for i in range(n)])

    def bank(self):
        i = self.bank_i
        self.bank_i = (i + 1) % 8
        return i

    def bank_of(self, key, banks):
        c = self.bank_ctr.get(key, 0)
        self.bank_ctr[key] = c + 1
        return banks[c % len(banks)]

    def new_phase_keep(self, keep):
        self.S.full_barrier()
        self.ptr = keep
        self.sem_i = self.sem_base

    def dma(self, out_ap, in_ap, reads=(), writes=(), sem=None, q="sp"):
        self.S.add(q, lambda e: e.dma_start(out=out_ap, in_=in_ap), reads=reads, writes=writes, dma_sem=sem)

    def load(self, tl, in_ap, q="sp", sub=None):
        self.dma(tl.ap if sub is None else sub, in_ap, writes=[tl.b], sem=tl.sem, q=q)

    def store(self, out_ap, tl, sub=None, q="sp"):
        self.dma(out_ap, tl.ap if sub is None else sub, reads=[tl.b], sem=tl.sem, q=q)

    def mm(self, bank, out_ap, lhsT, rhs, start, stop, reads, skipchk=False):
        self.S.add("pe", lambda e: e.matmul(out_ap, lhsT=lhsT, rhs=rhs, start=start, stop=stop, skip_group_check=skipchk),
                   reads=reads, writes=[self.pb[bank]])

    def tr(self, bank, out_ap, in_ap, ident, reads):
        self.S.add("pe", lambda e: e.transpose(out=out_ap, in_=in_ap, identity=ident), reads=reads, writes=[self.pb[bank]])

    def act(self, out_ap, in_ap, func, reads, writes, scale=None, bias=None, accum=None):
        kw = {}
        if scale is not None:
            kw["scale"] = scale
        if bias is not None:
            kw["bias"] = bias
        if accum is not None:
            kw["accum_out"] = accum
        self.S.add("act", lambda e: e.activation(out=out_ap, in_=in_ap, func=func, **kw), reads=reads, writes=writes)

    def op(self, eng, fn, reads, writes):
        self.S.add(eng, fn, reads=reads, writes=writes)

    def tt(self, out_ap, in0, in1, op, reads, writes, eng="dve"):
        self.S.add(eng, lambda e: e.tensor_tensor(out=out_ap, in0=in0, in1=in1, op=op), reads=reads, writes=writes)

    def ts(self, out_ap, in0, s1, s2, op0, op1, reads, writes, eng="dve"):
        if s2 is None:
            self.S.add(eng, lambda e: e.tensor_scalar(out=out_ap, in0=in0, scalar1=s1, scalar2=None, op0=op0), reads=reads, writes=writes)
        else:
            self.S.add(eng, lambda e: e.tensor_scalar(out=out_ap, in0=in0, scalar1=s1, scalar2=s2, op0=op0, op1=op1), reads=reads, writes=writes)

    def stt(self, out_ap, in0, scalar, in1, op0, op1, reads, writes, eng="dve"):
        self.S.add(eng, lambda e: e.scalar_tensor_tensor(out=out_ap, in0=in0, scalar=scalar, in1=in1, op0=op0, op1=op1), reads=reads, writes=writes)

    def copy(self, out_ap, in_ap, reads, writes, eng="dve"):
        self.S.add(eng, lambda e: e.tensor_copy(out=out_ap, in_=in_ap), reads=reads, writes=writes)

    def memset(self, tl, val, sub=None, eng="pool"):
        ap = tl.ap if sub is None else sub
        self.S.add(eng, lambda e: e.memset(ap, val), writes=[tl.b])

    def rstd(self, out_tl, ssq_ap, ssq_bufs, n_feat, cst):
        o = out_tl.ap
        self.ts(o, ssq_ap, 1.0 / n_feat, EPS, ALU.mult, ALU.add, reads=ssq_bufs, writes=[out_tl.b])
        mh = cst["mhalf"]
        sh = list(o.shape)
        mhb = mh.ap if len(sh) == 2 and sh[1] == 1 else mh.ap.to_broadcast(sh)
        self.tt(o, o, mhb, ALU.pow, reads=[out_tl.b, mh.b], writes=[out_tl.b], eng="pool")


def prologue(kb, cst, R, src_aps, aT, G, sh, gs_bufs):
    for t, src in enumerate(src_aps):
        hb = R["hbuf"].next()
        kb.load(hb, src)
        jk = R["junk"].next()
        ss = R["ssq"].next()
        kb.act(jk.ap, hb.ap, AF.Square, reads=[hb.b], writes=[jk.b, ss.b], accum=ss.ap)
        rs = R["rs"].next()
        kb.rstd(rs, ss.ap, [ss.b], D, cst)
        nt = R["ntile"].next()
        kb.ts(nt.ap, hb.ap, rs.ap, None, ALU.mult, None, reads=[hb.b, rs.b], writes=[nt.b])
        bk = kb.bank()
        pT = kb.psb[bk][:, 0:1024].rearrange("p (k n) -> p k n", k=8)
        for j in range(8):
            kb.tr(bk, pT[:, j, :], nt.ap[:, j * 128:(j + 1) * 128], cst["identb"].ap, reads=[nt.b, cst["identb"].b])
        for j in range(8):
            kb.act(aT.ap[:, j, t * 128:(t + 1) * 128], pT[:, j, :], AF.Identity, reads=[kb.pb[bk]] + gs_bufs, writes=[aT.b],
                   scale=G[:, j:j + 1], bias=sh[:, j:j + 1])


def prologue_rings(kb, nh=3):
    return {
        "hbuf": kb.ring("f32", [128, 1024], nh, "hbuf", dma=True),
        "junk": kb.ring("bf", [128, 1024], 2, "junk"),
        "ssq": kb.ring("f32", [128, 1], 4, "ssq"),
        "rs": kb.ring("f32", [128, 1], 4, "rs"),
        "ntile": kb.ring("bf", [128, 1024], 2, "ntile"),
    }


def residual_rings(kb):
    return {
        "hres": kb.ring("f32", [128, 1024], 3, "hres", dma=True),
        "rtmp": kb.ring("f32", [128, 512], 2, "rtmp"),
    }


def out_proj_residual(kb, RR, xT, n_k, tile_cols, W, gbc, src_ap, dst_ap):
    hr = RR["hres"].next()
    kb.load(hr, src_ap)
    for nh in range(2):
        bk = kb.bank()
        for k in range(n_k):
            kb.mm(bk, kb.ps[bk][:, :], xT.ap[:, k, tile_cols], W.ap[:, k, nh * 512:(nh + 1) * 512], k == 0, k == n_k - 1,
                  reads=[xT.b, W.b])
        tmp = RR["rtmp"].next()
        kb.tt(tmp.ap, kb.ps[bk][:, :], gbc.ap[:, nh * 512:(nh + 1) * 512], ALU.mult, reads=[kb.pb[bk], gbc.b], writes=[tmp.b])
        hs = hr.ap[:, nh * 512:(nh + 1) * 512]
        kb.tt(hs, hs, tmp.ap, ALU.add, reads=[hr.b, tmp.b], writes=[hr.b], eng="pool")
    kb.store(dst_ap, hr)


def chunks_of(tc, with_ctx=True):
    out = []
    if with_ctx:
        for t0 in range(0, TCX, min(tc, TCX)):
            out.append((1, t0, min(tc, TCX)))
    for t0 in range(TCX, T, tc):
        out.append((0, t0, tc))
    return out


def phase_mods(kb, cst, io, modpp):
    nc = kb.nc
    sT = kb.f32([128, 8, 2], "sT", dma=True)
    kb.load(sT, io["cT"])
    kb.act(sT.ap, sT.ap, AF.Silu, reads=[sT.b], writes=[sT.b])
    ng = kb.f32([128, DEPTH, 2, 8], "ng", dma=True)
    kb.load(ng, io["ng"])
    wring = kb.ring("f32", [128, 8, 512], 2, "modw", dma=True)
    mrows = kb.ring("f32", [2, 6144], 2, "mrow", dma=True)
    brows = kb.ring("f32", [1, 6144], 2, "brow", dma=True)
    for i in range(DEPTH):
        mr = mrows.next()
        br = brows.next()
        kb.load(br, io["modb"][i])
        for n in range(12):
            w = wring.next()
            kb.load(w, io["modw"][i][:, :, n * 512:(n + 1) * 512])
            bk = kb.bank()
            for k in range(8):
                kb.mm(bk, kb.ps[bk][0:2, :], sT.ap[:, k, :], w.ap[:, k, :], k == 0, False, reads=[sT.b, w.b])
            kb.mm(bk, kb.ps[bk][0:2, :], cst["ones"].ap[0:1, 0:2], br.ap[0:1, n * 512:(n + 1) * 512], False, True,
                  reads=[cst["ones"].b, br.b])
            kb.copy(mr.ap[0:2, n * 512:(n + 1) * 512], kb.ps[bk][0:2, :], reads=[kb.pb[bk]], writes=[mr.b])
        kb.store(io["MR"][i], mr)
        bk = kb.bank()
        first = True
        for v, c0 in enumerate((0, 1024, 3072, 4096)):
            for j in range(8):
                col = (v * 8 + j) * 2
                kb.mm(bk, kb.ps[bk][:, col:col + 2], mr.ap[0:2, c0 + j * 128:c0 + (j + 1) * 128], cst["identf"].ap[0:2, 0:2],
                      first, True, reads=[mr.b, cst["identf"].b], skipchk=True)
                first = False
        pp = modpp[i]
        kb.copy(pp.ap.rearrange("p v j t -> p (v j t)"), kb.ps[bk][:, 0:64], reads=[kb.pb[bk]], writes=[pp.b])
        for v, which in ((1, 0), (3, 1)):
            sc = pp.ap[:, v, :, :]
            kb.ts(sc, sc, 1.0, None, ALU.add, None, reads=[pp.b], writes=[pp.b])
            kb.tt(sc, sc, ng.ap[:, i, which, :, None].to_broadcast([128, 8, 2]), ALU.mult, reads=[pp.b, ng.b], writes=[pp.b])


def load_gate_bc(kb, io, layer, which):
    c0 = 2048 if which == 0 else 5120
    out = []
    for typ in range(2):
        g = kb.f32([128, 1024], "gbc%d" % typ, dma=True)
        kb.load(g, io["MR"][layer][typ:typ + 1, c0:c0 + 1024].partition_broadcast(128))
        out.append(g)
    return out


def phase_ffn(kb, cst, io, layer, modpp, src, dst, with_ctx, out_lat=None):
    kb.new_phase()
    gbc = load_gate_bc(kb, io, layer, 1)
    Wd = kb.bf([128, NF, 1024], "Wd", dma=True)
    for q4 in range(2):
        kb.S.add("pool", lambda e, q4=q4: e.dma_start(out=Wd.ap[:, q4 * 11:(q4 + 1) * 11, :], in_=io["wd"][layer][:, q4 * 11:(q4 + 1) * 11, :]),
                 writes=[Wd.b], dma_sem=Wd.sem)
    R = prologue_rings(kb)
    RR = residual_rings(kb)
    TC = 1024
    aTs = kb.ring("bf", [128, 8, TC], 2, "aT")
    hT = kb.bf([128, NF, TC], "hT")
    wgu = kb.ring("bf", [128, 2, 8, 128], 3, "wgu", dma=True)
    sgr = kb.ring("bf", [128, 512], 2, "sg")
    pp = modpp[layer]
    for typ, t0, n in chunks_of(TC, with_ctx):
        ntile = n // 128
        aT = aTs.next()
        prologue(kb, cst, R, [src[t0 + t * 128:t0 + (t + 1) * 128, :] for t in range(ntile)], aT,
                 pp.ap[:, 3, :, typ], pp.ap[:, 2, :, typ], [pp.b])
        halves = [(c, min(512, n - c)) for c in range(0, n, 512)]
        for f in range(NF):
            w = wgu.next()
            kb.load(w, io["wgu"][layer][f], q="pool")
            for c0, cn in halves:
                bg = kb.bank()
                for k in range(8):
                    kb.mm(bg, kb.ps[bg][:, 0:cn], w.ap[:, 0, k, :], aT.ap[:, k, c0:c0 + cn], k == 0, k == 7, reads=[w.b, aT.b])
                bu = kb.bank()
                for k in range(8):
                    kb.mm(bu, kb.ps[bu][:, 0:cn], w.ap[:, 1, k, :], aT.ap[:, k, c0:c0 + cn], k == 0, k == 7, reads=[w.b, aT.b])
                sg = sgr.next()
                kb.act(sg.ap[:, 0:cn], kb.ps[bg][:, 0:cn], AF.Silu, reads=[kb.pb[bg]], writes=[sg.b])
                kb.tt(hT.ap[:, f, c0:c0 + cn], sg.ap[:, 0:cn], kb.ps[bu][:, 0:cn], ALU.mult, reads=[sg.b, kb.pb[bu]], writes=[hT.b])
        for t in range(ntile):
            r0 = t0 + t * 128
            if out_lat is not None:
                d_ap = out_lat[r0 - TCX:r0 - TCX + 128, :]
            else:
                d_ap = dst[r0:r0 + 128, :]
            out_proj_residual(kb, RR, hT, NF, slice(t * 128, (t + 1) * 128), Wd, gbc[typ], src[r0:r0 + 128, :], d_ap)


def phase_conv(kb, cst, io, layer, slot, modpp, src, dst, with_ctx):
    kb.new_phase()
    gbc = load_gate_bc(kb, io, layer, 0)
    Win = kb.bf([128, 8, 3072], "cWin", dma=True)
    for q4 in range(4):
        kb.S.add("pool", lambda e, q4=q4: e.dma_start(out=Win.ap[:, q4 * 2:(q4 + 1) * 2, :], in_=io["cwin"][slot][:, q4 * 2:(q4 + 1) * 2, :]),
                 writes=[Win.b], dma_sem=Win.sem)
    Wout = kb.bf([128, 8, 1024], "cWout", dma=True)
    kb.load(Wout, io["cwout"][slot], q="pool")
    ckb = kb.f32([128, 8, 4], "ckb", dma=True)
    kb.load(ckb, io["ckb"][slot])
    R = prologue_rings(kb)
    RR = residual_rings(kb)
    TC = 512
    aTs = kb.ring("bf", [128, 8, TC], 3, "aT")
    halos = kb.ring("bf", [128, 8, 2], 2, "ahalo")
    zr = kb.ring("f32", [128, TC + 2], 2, "z")
    csbr = kb.ring("f32", [128, TC], 2, "csb")
    chbr = kb.ring("f32", [128, 2], 2, "chb")
    yr = kb.ring("f32", [128, TC], 2, "ycv")
    ybg = kb.bf([128, 8, TC], "ybg")
    pp = modpp[layer]
    chs = chunks_of(TC, with_ctx)
    seqs = []
    if with_ctx:
        seqs.append([c for c in chs if c[0] == 1])
    seqs.append([c for c in chs if c[0] == 0])

    def do_prologue(ch):
        typ, t0, n = ch
        aT = aTs.next()
        prologue(kb, cst, R, [src[t0 + t * 128:t0 + (t + 1) * 128, :] for t in range(n // 128)], aT,
                 pp.ap[:, 1, :, typ], pp.ap[:, 0, :, typ], [pp.b])
        return aT

    for seq in seqs:
        aT_list = {}
        aT_list[0] = do_prologue(seq[0])
        for ci, ch in enumerate(seq):
            typ, t0, n = ch
            if ci + 1 < len(seq):
                aT_list[ci + 1] = do_prologue(seq[ci + 1])
            aT = aT_list[ci]
            hl = halos.next()
            if ci == 0:
                kb.memset(hl, 0.0, sub=hl.ap[:, :, 0:1])
            else:
                pv = aT_list[ci - 1]
                pn = seq[ci - 1][2]
                kb.copy(hl.ap[:, :, 0:1], pv.ap[:, :, pn - 1:pn], reads=[pv.b], writes=[hl.b], eng="pool")
            if ci + 1 == len(seq):
                kb.memset(hl, 0.0, sub=hl.ap[:, :, 1:2])
            else:
                nx = aT_list[ci + 1]
                kb.copy(hl.ap[:, :, 1:2], nx.ap[:, :, 0:1], reads=[nx.b], writes=[hl.b], eng="pool")
            for j in range(8):
                z = zr.next()
                bc = kb.bank()
                for k in range(8):
                    kb.mm(bc, kb.ps[bc][:, 0:n], Win.ap[:, k, 1024 + j * 128:1024 + (j + 1) * 128], aT.ap[:, k, 0:n], k == 0, k == 7, reads=[Win.b, aT.b])
                bu = kb.bank()
                for k in range(8):
                    kb.mm(bu, kb.ps[bu][:, 0:n], Win.ap[:, k, 2048 + j * 128:2048 + (j + 1) * 128], aT.ap[:, k, 0:n], k == 0, k == 7, reads=[Win.b, aT.b])
                bh = kb.bank()
                for k in range(8):
                    kb.mm(bh, kb.ps[bh][:, 0:2], Win.ap[:, k, 1024 + j * 128:1024 + (j + 1) * 128], hl.ap[:, k, :], k == 0, k == 7, reads=[Win.b, hl.b])
                for k in range(8):
                    kb.mm(bh, kb.ps[bh][:, 2:4], Win.ap[:, k, 2048 + j * 128:2048 + (j + 1) * 128], hl.ap[:, k, :], k == 0, k == 7, reads=[Win.b, hl.b], skipchk=True)
                csb = csbr.next()
                kb.act(csb.ap[:, 0:n], kb.ps[bc][:, 0:n], AF.Copy, reads=[kb.pb[bc]], writes=[csb.b])
                kb.tt(z.ap[:, 1:n + 1], csb.ap[:, 0:n], kb.ps[bu][:, 0:n], ALU.mult, reads=[csb.b, kb.pb[bu]], writes=[z.b])
                chb = chbr.next()
                kb.act(chb.ap, kb.ps[bh][:, 0:2], AF.Copy, reads=[kb.pb[bh]], writes=[chb.b])
                kb.tt(z.ap[:, 0:1], chb.ap[:, 0:1], kb.ps[bh][:, 2:3], ALU.mult, reads=[chb.b, kb.pb[bh]], writes=[z.b])
                kb.tt(z.ap[:, n + 1:n + 2], chb.ap[:, 1:2], kb.ps[bh][:, 3:4], ALU.mult, reads=[chb.b, kb.pb[bh]], writes=[z.b])
                y = yr.next()
                kb.ts(y.ap[:, 0:n], z.ap[:, 0:n], ckb.ap[:, j, 0:1], ckb.ap[:, j, 3:4], ALU.mult, ALU.add, reads=[z.b, ckb.b], writes=[y.b])
                kb.stt(y.ap[:, 0:n], z.ap[:, 1:n + 1], ckb.ap[:, j, 1:2], y.ap[:, 0:n], ALU.mult, ALU.add, reads=[z.b, ckb.b, y.b], writes=[y.b])
                kb.stt(y.ap[:, 0:n], z.ap[:, 2:n + 2], ckb.ap[:, j, 2:3], y.ap[:, 0:n], ALU.mult, ALU.add, reads=[z.b, ckb.b, y.b], writes=[y.b])
                bb = kb.bank()
                for k in range(8):
                    kb.mm(bb, kb.ps[bb][:, 0:n], Win.ap[:, k, j * 128:(j + 1) * 128], aT.ap[:, k, 0:n], k == 0, k == 7, reads=[Win.b, aT.b])
                kb.tt(ybg.ap[:, j, 0:n], y.ap[:, 0:n], kb.ps[bb][:, 0:n], ALU.mult, reads=[y.b, kb.pb[bb]], writes=[ybg.b])
            for t in range(n // 128):
                r0 = t0 + t * 128
                out_proj_residual(kb, RR, ybg, 8, slice(t * 128, (t + 1) * 128), Wout, gbc[typ], src[r0:r0 + 128, :], dst[r0:r0 + 128, :])


def group_rstd(kb, cst, banks, ncols, gsize, sqr, ssr, rhr):
    sq = sqr.next()
    for i, bk in enumerate(banks):
        kb.act(sq.ap[:, i * ncols:(i + 1) * ncols], kb.ps[bk][:, 0:ncols], AF.Square, reads=[kb.pb[bk]], writes=[sq.b])
    ng = len(banks) * ncols // gsize
    ss = ssr.next()
    kb.op("dve", lambda e: e.tensor_reduce(out=ss.ap[:, 0:ng], in_=sq.ap[:, 0:ng * gsize].rearrange("p (g d) -> p g d", d=gsize),
                                          axis=AX.X, op=ALU.add), reads=[sq.b], writes=[ss.b])
    rh = rhr.next()
    kb.rstd(Tl(rh.ap[:, 0:ng], rh.b), ss.ap[:, 0:ng], [ss.b], gsize, cst)
    return rh


def rope_apply(kb, x, xb, out_ap, out_b, cos, sin, tb, tmpr, lead):
    R = cos.shape[-1]
    t1 = tmpr.next()
    t2 = tmpr.next()
    if lead:
        shp = [128, lead, R]
        cb = cos[:, None, :].to_broadcast(shp)
        t1a = t1.ap[:, 0:lead * R].rearrange("p (h r) -> p h r", h=lead)
        t2a = t2.ap[:, 0:lead * R].rearrange("p (h r) -> p h r", h=lead)
    else:
        cb = cos
        t1a = t1.ap[:, 0:R]
        t2a = t2.ap[:, 0:R]
    kb.tt(t1a, x, cb, ALU.mult, reads=[xb, tb], writes=[t1.b])
    q = R // 4
    for half in range(2):
        if lead:
            xv = x.rearrange("p h (g f i) -> p h g f i", g=2, f=2)[:, :, :, 1 - half, :]
            ov = t2a.rearrange("p h (g f i) -> p h g f i", g=2, f=2)[:, :, :, half, :]
            sv = sin.rearrange("p (g f i) -> p g f i", g=2, f=2)[:, :, half, :][:, None, :, :].to_broadcast([128, lead, 2, q])
        else:
            xv = x.rearrange("p (g f i) -> p g f i", g=2, f=2)[:, :, 1 - half, :]
            ov = t2a.rearrange("p (g f i) -> p g f i", g=2, f=2)[:, :, half, :]
            sv = sin.rearrange("p (g f i) -> p g f i", g=2, f=2)[:, :, half, :]
        kb.tt(ov, xv, sv, ALU.mult, reads=[xb, tb], writes=[t2.b])
    kb.tt(out_ap, t1a, t2a, ALU.add, reads=[t1.b, t2.b], writes=[out_b])


MH = 16


def phase_mla(kb, cst, io, layer, modpp, src, dst):
    QT = io["mQT"]
    KT = io["mKT"]
    kb.new_phase()
    Vaug = kb.bf([128, NT, MH, 65], "Vaug")
    markV = kb.ptr
    kb.memset(Vaug, 1.0)
    Win = kb.bf([128, 8, 1056], "mWin", dma=True)
    kb.load(Win, io["mwin"], q="pool")
    Wq = kb.bf([128, 6, 1536], "mWq", dma=True)
    kb.load(Wq, io["mwq"], q="pool")
    Wkv = kb.bf([128, 2, 2048], "mWkv", dma=True)
    kb.load(Wkv, io["mwkv"], q="pool")
    gsm = kb.f32([128, 10], "mgsm", dma=True)
    kb.load(gsm, io["mgsm"])
    grope = kb.f32([128, 64], "mgrope", dma=True)
    kb.load(grope, io["mrg"].partition_broadcast(128))
    tab = kb.f32([128, 32, 2, 32], "mrope", dma=True)
    kb.load(tab, io["ropem"])
    R = prologue_rings(kb)
    TC = 256
    aTs = kb.ring("bf", [128, 8, TC], 2, "aT")
    ss4r = kb.ring("f32", [128, 4], 2, "ss4")
    r3r = [kb.ring("f32", [128, 1], 2, "r3_%d" % i) for i in range(3)]
    nqr = kb.ring("bf", [128, 768], 2, "nq")
    nkvr = kb.ring("bf", [128, 256], 2, "nkv")
    kper = kb.ring("f32", [128, 32], 2, "kpe")
    kpe2r = kb.ring("f32", [128, 32], 2, "kpe2")
    qlTr = kb.ring("bf", [128, 6, 128], 2, "qlT")
    kvlTr = kb.ring("bf", [128, 2, 128], 2, "kvlT")
    sqr = kb.ring("f32", [128, 1024], 1, "sq")
    ssr = kb.ring("f32", [128, 16], 2, "ssh")
    rhr = kb.ring("f32", [128, 16], 2, "rh")
    qpr = kb.ring("f32", [128, 512], 2, "qp")
    tmpr = kb.ring("f32", [128, 512], 2, "ropetmp")
    Qcr = kb.ring("bf", [128, MH, 96], 2, "Qc")
    Kcr = kb.ring("bf", [128, MH, 96], 2, "Kc")
    QTs = kb.ring("bf", [128, MH, TC], 1, "QTs", dma=True)
    KTs = kb.ring("bf", [128, MH, TC], 1, "KTs", dma=True)
    pp = modpp[layer]
    identb = cst["identb"]
    for typ, t0, n in chunks_of(TC, True):
        aT = aTs.next()
        prologue(kb, cst, R, [src[t0 + t * 128:t0 + (t + 1) * 128, :] for t in range(n // 128)], aT,
                 pp.ap[:, 1, :, typ], pp.ap[:, 0, :, typ], [pp.b])
        qts = QTs.next()
        kts = KTs.next()
        for t in range(n // 128):
            gt = (t0 + t * 128) // 128
            lt = gt - 2
            cols = slice(t * 128, (t + 1) * 128)
            bks = []
            for c0, cn in ((0, 512), (512, 256), (768, 288)):
                bk = kb.bank()
                bks.append(bk)
                for k in range(8):
                    kb.mm(bk, kb.ps[bk][:, 0:cn], aT.ap[:, k, cols], Win.ap[:, k, c0:c0 + cn], k == 0, k == 7, reads=[aT.b, Win.b])
            b0, b1, b2 = bks
            jk = R["junk"].next()
            ss = ss4r.next()
            kb.act(jk.ap[:, 0:512], kb.ps[b0][:, 0:512], AF.Square, reads=[kb.pb[b0]], writes=[jk.b, ss.b], accum=ss.ap[:, 0:1])
            kb.act(jk.ap[:, 0:256], kb.ps[b1][:, 0:256], AF.Square, reads=[kb.pb[b1]], writes=[jk.b, ss.b], accum=ss.ap[:, 1:2])
            kb.act(jk.ap[:, 0:256], kb.ps[b2][:, 0:256], AF.Square, reads=[kb.pb[b2]], writes=[jk.b, ss.b], accum=ss.ap[:, 2:3])
            kb.act(jk.ap[:, 0:32], kb.ps[b2][:, 256:288], AF.Square, reads=[kb.pb[b2]], writes=[jk.b, ss.b], accum=ss.ap[:, 3:4])
            kb.tt(ss.ap[:, 0:1], ss.ap[:, 0:1], ss.ap[:, 1:2], ALU.add, reads=[ss.b], writes=[ss.b])
            rq, rkv, rkp = [r.next() for r in r3r]
            kb.rstd(rq, ss.ap[:, 0:1], [ss.b], 768, cst)
            kb.rstd(rkv, ss.ap[:, 2:3], [ss.b], 256, cst)
            kb.rstd(rkp, ss.ap[:, 3:4], [ss.b], 32, cst)
            nq = nqr.next()
            kb.ts(nq.ap[:, 0:512], kb.ps[b0][:, 0:512], rq.ap, None, ALU.mult, None, reads=[kb.pb[b0], rq.b], writes=[nq.b])
            kb.ts(nq.ap[:, 512:768], kb.ps[b1][:, 0:256], rq.ap, None, ALU.mult, None, reads=[kb.pb[b1], rq.b], writes=[nq.b])
            nkv = nkvr.next()
            kb.ts(nkv.ap, kb.ps[b2][:, 0:256], rkv.ap, None, ALU.mult, None, reads=[kb.pb[b2], rkv.b], writes=[nkv.b])
            kpe = kper.next()
            kb.stt(kpe.ap, kb.ps[b2][:, 256:288], rkp.ap, grope.ap[:, 32:64], ALU.mult, ALU.mult, reads=[kb.pb[b2], rkp.b, grope.b], writes=[kpe.b])
            if typ == 0:
                kpe2 = kpe2r.next()
                rope_apply(kb, kpe.ap, kpe.b, kpe2.ap, kpe2.b, tab.ap[:, lt, 0, :], tab.ap[:, lt, 1, :], tab.b, tmpr, 0)
                kpe = kpe2
            bt = kb.bank()
            pT = kb.psb[bt][:, 0:1024].rearrange("p (k n) -> p k n", k=8)
            for j in range(6):
                kb.tr(bt, pT[:, j, :], nq.ap[:, j * 128:(j + 1) * 128], identb.ap, reads=[nq.b, identb.b])
            for j in range(2):
                kb.tr(bt, pT[:, 6 + j, :], nkv.ap[:, j * 128:(j + 1) * 128], identb.ap, reads=[nkv.b, identb.b])
            qlT = qlTr.next()
            kvlT = kvlTr.next()
            for j in range(6):
                kb.act(qlT.ap[:, j, :], pT[:, j, :], AF.Copy, reads=[kb.pb[bt], gsm.b], writes=[qlT.b], scale=gsm.ap[:, j:j + 1])
            for j in range(2):
                kb.act(kvlT.ap[:, j, :], pT[:, 6 + j, :], AF.Copy, reads=[kb.pb[bt], gsm.b], writes=[kvlT.b], scale=gsm.ap[:, 6 + j:7 + j])
            Qc = Qcr.next()
            Kc = Kcr.next()
            bq = []
            for nn in range(2):
                bk = kb.bank()
                bq.append(bk)
                for k in range(6):
                    kb.mm(bk, kb.ps[bk][:, :], qlT.ap[:, k, :], Wq.ap[:, k, nn * 512:(nn + 1) * 512], k == 0, k == 5, reads=[qlT.b, Wq.b])
            rh = group_rstd(kb, cst, bq, 512, 64, sqr, ssr, rhr)
            for nn in range(2):
                kb.tt(Qc.ap[:, nn * 8:(nn + 1) * 8, 0:64], kb.ps[bq[nn]][:, :].rearrange("p (h d) -> p h d", d=64),
                      rh.ap[:, nn * 8:(nn + 1) * 8, None].to_broadcast([128, 8, 64]), ALU.mult, reads=[kb.pb[bq[nn]], rh.b], writes=[Qc.b])
            bk = kb.bank()
            for k in range(6):
                kb.mm(bk, kb.ps[bk][:, :], qlT.ap[:, k, :], Wq.ap[:, k, 1024:1536], k == 0, k == 5, reads=[qlT.b, Wq.b])
            rh = group_rstd(kb, cst, [bk], 512, 32, sqr, ssr, rhr)
            qp = qpr.next()
            qp3 = qp.ap.rearrange("p (h r) -> p h r", r=32)
            kb.tt(qp3, kb.ps[bk][:, :].rearrange("p (h r) -> p h r", r=32), rh.ap[:, 0:16, None].to_broadcast([128, 16, 32]), ALU.mult,
                  reads=[kb.pb[bk], rh.b], writes=[qp.b])
            kb.tt(qp3, qp3, grope.ap[:, None, 0:32].to_broadcast([128, 16, 32]), ALU.mult, reads=[qp.b, grope.b], writes=[qp.b])
            if typ == 0:
                rope_apply(kb, qp3, qp.b, Qc.ap[:, :, 64:96], Qc.b, tab.ap[:, lt, 0, :], tab.ap[:, lt, 1, :], tab.b, tmpr, 16)
            else:
                kb.copy(Qc.ap[:, :, 64:96], qp3, reads=[qp.b], writes=[Qc.b])
            bkk = []
            for nn in range(2):
                bk = kb.bank()
                bkk.append(bk)
                for k in range(2):
                    kb.mm(bk, kb.ps[bk][:, :], kvlT.ap[:, k, :], Wkv.ap[:, k, nn * 512:(nn + 1) * 512], k == 0, k == 1, reads=[kvlT.b, Wkv.b])
            rh = group_rstd(kb, cst, bkk, 512, 64, sqr, ssr, rhr)
            for nn in range(2):
                kb.tt(Kc.ap[:, nn * 8:(nn + 1) * 8, 0:64], kb.ps[bkk[nn]][:, :].rearrange("p (h d) -> p h d", d=64),
                      rh.ap[:, nn * 8:(nn + 1) * 8, None].to_broadcast([128, 8, 64]), ALU.mult, reads=[kb.pb[bkk[nn]], rh.b], writes=[Kc.b])
            kb.copy(Kc.ap[:, :, 64:96], kpe.ap[:, None, :].to_broadcast([128, 16, 32]), reads=[kpe.b], writes=[Kc.b], eng="pool")
            for nn in range(2):
                bk = kb.bank()
                for k in range(2):
                    kb.mm(bk, kb.ps[bk][:, :], kvlT.ap[:, k, :], Wkv.ap[:, k, 1024 + nn * 512:1024 + (nn + 1) * 512], k == 0, k == 1, reads=[kvlT.b, Wkv.b])
                kb.act(Vaug.ap[:, gt, nn * 8:(nn + 1) * 8, 0:64], kb.ps[bk][:, :].rearrange("p (h d) -> p h d", d=64), AF.Copy,
                       reads=[kb.pb[bk]], writes=[Vaug.b])
            for src_t, stg, gcol in ((Qc, qts, 8), (Kc, kts, 9)):
                for hh in range(2):
                    bk = kb.bank()
                    pv = kb.psb[bk][:, 0:1024].rearrange("p (h n) -> p h n", h=8)
                    for h8 in range(8):
                        kb.tr(bk, pv[0:96, h8, :], src_t.ap[:, hh * 8 + h8, :], identb.ap, reads=[src_t.b, identb.b])
                    kb.act(stg.ap[0:96, hh * 8:(hh + 1) * 8, cols], pv[0:96, :, :], AF.Copy, reads=[kb.pb[bk], gsm.b], writes=[stg.b],
                           scale=gsm.ap[0:96, gcol:gcol + 1])
        kb.store(QT[:, :, t0:t0 + n].rearrange("h p t -> p h t"), qts, sub=qts.ap[0:96, :, 0:n])
        kb.store(KT[:, :, t0:t0 + n].rearrange("h p t -> p h t"), kts, sub=kts.ap[0:96, :, 0:n])

    kb.new_phase_keep(markV)
    Osb = kb.bf([128, NT, 1024], "Osb")
    markO = kb.ptr
    KTr = kb.ring("bf", [128, T], 2, "KTh", dma=True)
    QTr = kb.ring("bf", [128, T], 2, "QTh", dma=True)
    ptr_ = kb.ring("bf", [128, 512], 3, "PT")
    recr = kb.ring("f32", [128, 4], 2, "rec")
    scale = 96.0 ** -0.5
    OB = [0, 1]
    SB = [2, 3, 4, 5, 6, 7]
    for h in range(MH):
        kth = KTr.next()
        qth = QTr.next()
        kb.load(kth, KT[h], sub=kth.ap[0:96, :])
        kb.load(qth, QT[h], sub=qth.ap[0:96, :])
        jobs = [(0, 256, [0, 1])] + [(TCX + c * 512, 512, list(range(NT))) for c in range(TL // 512)]
        for q0, qn, kts_ in jobs:
            bo = kb.bank_of("O", OB)
            nsub = qn // 128
            for idx, kt in enumerate(kts_):
                bs = kb.bank_of("S", SB)
                kb.mm(bs, kb.ps[bs][:, 0:qn], kth.ap[0:96, kt * 128:(kt + 1) * 128], qth.ap[0:96, q0:q0 + qn], True, True, reads=[kth.b, qth.b])
                pt = ptr_.next()
                kb.act(pt.ap[:, 0:qn], kb.ps[bs][:, 0:qn], AF.Exp, reads=[kb.pb[bs]], writes=[pt.b], scale=scale)
                for qs in range(nsub):
                    kb.mm(bo, kb.ps[bo][:, qs * 65:(qs + 1) * 65], pt.ap[:, qs * 128:(qs + 1) * 128], Vaug.ap[:, kt, h, :],
                          idx == 0 and qs == 0, idx == len(kts_) - 1, reads=[pt.b, Vaug.b], skipchk=True)
            rec = recr.next()
            ov = kb.ps[bo][:, 0:nsub * 65].rearrange("p (s c) -> p s c", c=65)
            kb.op("dve", lambda e, rec=rec, ov=ov, nsub=nsub: e.reciprocal(out=rec.ap[:, 0:nsub], in_=ov[:, :, 64]), reads=[kb.pb[bo]], writes=[rec.b])
            kb.tt(Osb.ap[:, q0 // 128:q0 // 128 + nsub, h * 64:(h + 1) * 64], ov[:, :, 0:64],
                  rec.ap[:, 0:nsub, None].to_broadcast([128, nsub, 64]), ALU.mult, reads=[kb.pb[bo], rec.b], writes=[Osb.b])

    kb.new_phase_keep(markO)
    attn_out_phase(kb, cst, io, layer, Osb, io["mwo"], None, src, dst)


def attn_out_phase(kb, cst, io, layer, Osb, w_dram, pre, src, dst):
    gbc = load_gate_bc(kb, io, layer, 0)
    Wo = kb.bf([128, 8, 1024], "Wo", dma=True)
    kb.load(Wo, w_dram, q="pool")
    RR = residual_rings(kb)
    OTr = kb.ring("bf", [128, 8, 128], 2, "OT")
    identb = cst["identb"]
    for t in range(NT):
        typ = 1 if t < 2 else 0
        if pre is not None:
            o_ap, o_b, scl = pre(t)
        else:
            o_ap, o_b, scl = Osb.ap[:, t, :], Osb.b, None
        bk = kb.bank()
        pT = kb.psb[bk][:, 0:1024].rearrange("p (k n) -> p k n", k=8)
        for j in range(8):
            kb.tr(bk, pT[:, j, :], o_ap[:, j * 128:(j + 1) * 128], identb.ap, reads=[o_b, identb.b])
        OT = OTr.next()
        if scl is None:
            kb.copy(OT.ap, pT, reads=[kb.pb[bk]], writes=[OT.b])
        else:
            kb.act(OT.ap, pT, AF.Copy, reads=[kb.pb[bk], scl[1]], writes=[OT.b], scale=scl[0])
        r0 = t * 128
        out_proj_residual(kb, RR, OT, 8, slice(0, 128), Wo, gbc[typ], src[r0:r0 + 128, :], dst[r0:r0 + 128, :])


DH = 8


def phase_diff(kb, cst, io, layer, modpp, src, dst):
    lam_init = 0.8 - 0.6 * math.exp(-0.3 * layer)
    QT = io["dQT"]
    KT = io["dKT"]
    kb.new_phase()
    Vaug = kb.bf([128, NT, DH, 129], "dVaug")
    markV = kb.ptr
    kb.memset(Vaug, 1.0)
    Win = kb.bf([128, 8, 3072], "dWin", dma=True)
    for q4 in range(4):
        kb.S.add("pool", lambda e, q4=q4: e.dma_start(out=Win.ap[:, q4 * 2:(q4 + 1) * 2, :], in_=io["dwin"][:, q4 * 2:(q4 + 1) * 2, :]),
                 writes=[Win.b], dma_sem=Win.sem)
    gn = kb.f32([128, 128], "dgn", dma=True)
    kb.load(gn, io["dgn"].partition_broadcast(128))
    tab = kb.f32([128, 32, 2, 64], "drope", dma=True)
    kb.load(tab, io["roped"])
    R = prologue_rings(kb, 2)
    TC = 256
    aTs = kb.ring("bf", [128, 8, TC], 2, "aT")
    sqr = kb.ring("f32", [128, 1024], 1, "sq")
    ssr = kb.ring("f32", [128, 16], 2, "ssh")
    rhr = kb.ring("f32", [128, 16], 2, "rh")
    qnr = kb.ring("f32", [128, 1024], 2, "qn")
    tmpr = kb.ring("f32", [128, 1024], 2, "ropetmp")
    QKc = kb.ring("bf", [128, 1024], 2, "QKc")
    QTs = kb.bf([128, DH, 2, TC], "dQTs", dma=True)
    KTs = kb.bf([128, DH, TC], "dKTs", dma=True)
    kb.memset(QTs, 0.0)
    pp = modpp[layer]
    identb = cst["identb"]
    for typ, t0, n in chunks_of(TC, True):
        aT = aTs.next()
        prologue(kb, cst, R, [src[t0 + t * 128:t0 + (t + 1) * 128, :] for t in range(n // 128)], aT,
                 pp.ap[:, 1, :, typ], pp.ap[:, 0, :, typ], [pp.b])
        for t in range(n // 128):
            gt = (t0 + t * 128) // 128
            lt = gt - 2
            cols = slice(t * 128, (t + 1) * 128)
            for which in range(2):
                bks = []
                for nn in range(2):
                    bk = kb.bank()
                    bks.append(bk)
                    c0 = which * 1024 + nn * 512
                    for k in range(8):
                        kb.mm(bk, kb.ps[bk][:, :], aT.ap[:, k, cols], Win.ap[:, k, c0:c0 + 512], k == 0, k == 7, reads=[aT.b, Win.b])
                rh = group_rstd(kb, cst, bks, 512, 64, sqr, ssr, rhr)
                qn = qnr.next()
                qn3 = qn.ap.rearrange("p (g d) -> p g d", d=64)
                for nn in range(2):
                    kb.tt(qn3[:, nn * 8:(nn + 1) * 8, :], kb.ps[bks[nn]][:, :].rearrange("p (g d) -> p g d", d=64),
                          rh.ap[:, nn * 8:(nn + 1) * 8, None].to_broadcast([128, 8, 64]), ALU.mult, reads=[kb.pb[bks[nn]], rh.b], writes=[qn.b])
                kb.tt(qn3, qn3, gn.ap[:, None, which * 64:(which + 1) * 64].to_broadcast([128, 16, 64]), ALU.mult,
                      reads=[qn.b, gn.b], writes=[qn.b], eng="pool")
                qc = QKc.next()
                qc3 = qc.ap.rearrange("p (g d) -> p g d", d=64)
                if typ == 0:
                    rope_apply(kb, qn3, qn.b, qc3, qc.b, tab.ap[:, lt, 0, :], tab.ap[:, lt, 1, :], tab.b, tmpr, 16)
                else:
                    kb.copy(qc3, qn3, reads=[qn.b], writes=[qc.b])
                bk = kb.bank()
                pT = kb.psb[bk][:, 0:1024].rearrange("p (h n) -> p h n", h=8)
                for h in range(DH):
                    kb.tr(bk, pT[:, h, :], qc.ap[:, h * 128:(h + 1) * 128], identb.ap, reads=[qc.b, identb.b])
                if which == 0:
                    kb.op("act", lambda e, pT=pT, cols=cols: e.activation(out=QTs.ap[0:64, :, 0, cols], in_=pT[0:64, :, :], func=AF.Copy),
                          reads=[kb.pb[bk]], writes=[QTs.b])
                    kb.copy(QTs.ap[64:128, :, 1, cols], pT[64:128, :, :], reads=[kb.pb[bk]], writes=[QTs.b])
                else:
                    kb.op("act", lambda e, pT=pT, cols=cols: e.activation(out=KTs.ap[:, :, cols], in_=pT, func=AF.Copy),
                          reads=[kb.pb[bk]], writes=[KTs.b])
            for nn in range(2):
                bk = kb.bank()
                c0 = 2048 + nn * 512
                for k in range(8):
                    kb.mm(bk, kb.ps[bk][:, :], aT.ap[:, k, cols], Win.ap[:, k, c0:c0 + 512], k == 0, k == 7, reads=[aT.b, Win.b])
                kb.copy(Vaug.ap[:, gt, nn * 4:(nn + 1) * 4, 0:128], kb.ps[bk][:, :].rearrange("p (h d) -> p h d", d=128),
                        reads=[kb.pb[bk]], writes=[Vaug.b])
        kb.store(QT[:, :, :, t0:t0 + n].rearrange("h v p t -> p h v t"), QTs, sub=QTs.ap[:, :, :, 0:n])
        kb.store(KT[:, :, t0:t0 + n].rearrange("h p t -> p h t"), KTs, sub=KTs.ap[:, :, 0:n])

    kb.new_phase_keep(markV)
    Osb = kb.bf([128, NT, 1024], "dOsb")
    markO = kb.ptr
    lrow = kb.f32([1, 256], "lrow", dma=True)
    kb.load(lrow, io["dlam"])
    lpr = kb.f32([1, 128], "lprod")
    lv = lrow.ap.rearrange("p (a b d) -> p a b d", a=2, b=2)
    kb.tt(lpr.ap.rearrange("p (a d) -> p a d", a=2), lv[:, :, 0, :], lv[:, :, 1, :], ALU.mult, reads=[lrow.b], writes=[lpr.b])
    lsum = kb.f32([1, 2], "lsum")
    kb.op("dve", lambda e: e.tensor_reduce(out=lsum.ap, in_=lpr.ap.rearrange("p (a d) -> p a d", a=2), axis=AX.X, op=ALU.add),
          reads=[lpr.b], writes=[lsum.b])
    kb.act(lsum.ap, lsum.ap, AF.Exp, reads=[lsum.b], writes=[lsum.b])
    nl1 = kb.f32([1, 2], "nl1")
    kb.tt(nl1.ap[:, 0:1], lsum.ap[:, 1:2], lsum.ap[:, 0:1], ALU.subtract, reads=[lsum.b], writes=[nl1.b])
    kb.ts(nl1.ap[:, 0:1], nl1.ap[:, 0:1], -lam_init, None, ALU.add, None, reads=[nl1.b], writes=[nl1.b])
    kb.copy(nl1.ap[:, 1:2], nl1.ap[:, 0:1], reads=[nl1.b], writes=[nl1.b])
    bk = kb.bank()
    kb.mm(bk, kb.ps[bk][:, 0:2], cst["ones"].ap[0:1, 0:128], nl1.ap[0:1, 0:2], True, True, reads=[cst["ones"].b, nl1.b])
    neglam = kb.f32([128, 2], "neglam")
    kb.copy(neglam.ap, kb.ps[bk][:, 0:2], reads=[kb.pb[bk]], writes=[neglam.b])

    KTr = kb.ring("bf", [128, T], 2, "dKTh", dma=True)
    QTr = kb.ring("bf", [128, 2, T], 2, "dQTh", dma=True)
    ptr_ = kb.ring("bf", [128, 512], 3, "PT")
    recr = kb.ring("f32", [128, 4], 2, "rec")
    o1r = kb.ring("f32", [128, 4, 128], 2, "O1n")
    o2r = kb.ring("f32", [128, 2, 128], 2, "O2n")
    scale = 64.0 ** -0.5
    OB = [0, 1, 2, 3]
    SB = [4, 5, 6, 7]
    for h in range(DH):
        kth = KTr.next()
        qth = QTr.next()
        kb.load(kth, KT[h])
        kb.load(qth, QT[h].rearrange("v p t -> p v t"))
        jobs = [(0, 256, [0, 1])] + [(TCX + c * 512, 512, list(range(NT))) for c in range(TL // 512)]
        for q0, qn_, kts_ in jobs:
            nsub = qn_ // 128
            nb = (nsub + 1) // 2
            o1 = o1r.next()
            for sv in range(2):
                bos = [kb.bank_of("O", OB) for _ in range(2)]
                for idx, kt in enumerate(kts_):
                    bs = kb.bank_of("S", SB)
                    kb.mm(bs, kb.ps[bs][:, 0:qn_], kth.ap[:, kt * 128:(kt + 1) * 128], qth.ap[:, sv, q0:q0 + qn_], True, True, reads=[kth.b, qth.b])
                    pt = ptr_.next()
                    kb.act(pt.ap[:, 0:qn_], kb.ps[bs][:, 0:qn_], AF.Exp, reads=[kb.pb[bs]], writes=[pt.b], scale=scale)
                    for qs in range(nsub):
                        bo = bos[qs // 2]
                        c0 = (qs % 2) * 129
                        kb.mm(bo, kb.ps[bo][:, c0:c0 + 129], pt.ap[:, qs * 128:(qs + 1) * 128], Vaug.ap[:, kt, h, :],
                              idx == 0 and qs % 2 == 0, idx == len(kts_) - 1, reads=[pt.b, Vaug.b], skipchk=True)
                rec = recr.next()
                for b2 in range(nb):
                    bo = bos[b2]
                    ov = kb.ps[bo][:, 0:258].rearrange("p (s c) -> p s c", c=129)
                    rsl = rec.ap[:, 2 * b2:2 * b2 + 2]
                    kb.op("dve", lambda e, rsl=rsl, ov=ov: e.reciprocal(out=rsl, in_=ov[:, :, 128]), reads=[kb.pb[bo]], writes=[rec.b])
                    if sv == 0:
                        kb.tt(o1.ap[:, 2 * b2:2 * b2 + 2, :], ov[:, :, 0:128], rsl[:, :, None].to_broadcast([128, 2, 128]), ALU.mult,
                              reads=[kb.pb[bo], rec.b], writes=[o1.b])
                    else:
                        o2 = o2r.next()
                        kb.tt(o2.ap, ov[:, :, 0:128], rsl[:, :, None].to_broadcast([128, 2, 128]), ALU.mult,
                              reads=[kb.pb[bo], rec.b], writes=[o2.b])
                        tl0 = q0 // 128 + 2 * b2
                        kb.stt(Osb.ap[:, tl0:tl0 + 2, h * 128:(h + 1) * 128], o2.ap, neglam.ap[:, 0:1], o1.ap[:, 2 * b2:2 * b2 + 2, :],
                               ALU.mult, ALU.add, reads=[o2.b, neglam.b, o1.b], writes=[Osb.b])

    kb.new_phase_keep(markO)
    dsub = kb.f32([128, 1], "dsub", dma=True)
    kb.load(dsub, io["dsub"])
    sqr = kb.ring("f32", [128, 1024], 2, "sq")
    ssr = kb.ring("f32", [128, 8], 2, "ssh")
    rhr = kb.ring("f32", [128, 8], 2, "rh")
    onr = kb.ring("bf", [128, 1024], 2, "On")

    def pre(t):
        sq = sqr.next()
        kb.act(sq.ap, Osb.ap[:, t, :], AF.Square, reads=[Osb.b], writes=[sq.b])
        ss = ssr.next()
        kb.op("dve", lambda e: e.tensor_reduce(out=ss.ap, in_=sq.ap.rearrange("p (h d) -> p h d", d=128), axis=AX.X, op=ALU.add),
              reads=[sq.b], writes=[ss.b])
        rh = rhr.next()
        kb.rstd(rh, ss.ap, [ss.b], 128, cst)
        on = onr.next()
        kb.stt(on.ap.rearrange("p (h d) -> p h d", d=128), Osb.ap[:, t, :].rearrange("p (h d) -> p h d", d=128), 1.0 - lam_init,
               rh.ap[:, :, None].to_broadcast([128, 8, 128]), ALU.mult, ALU.mult, reads=[Osb.b, rh.b], writes=[on.b])
        return on.ap, on.b, (dsub.ap[:, 0:1], dsub.b)

    attn_out_phase(kb, cst, io, layer, Osb, io["dwo"], pre, src, dst)

def build_program(n_layers=DEPTH, dbg=False, stop_after=None):
    nc = bass.Bass("TRN2", target_bir_lowering=False)
    io = {}

    def inp(name, shape):
        io[name] = nc.dram_tensor(name, list(shape), F32, kind="ExternalInput").ap()

    inp("xin", [T, D])
    inp("cT", [128, 8, 2])
    inp("ng", [128, DEPTH, 2, 8])
    inp("consts", [128, 256])
    inp("modw", [DEPTH, 128, 8, 6144])
    inp("modb", [DEPTH, 1, 6144])
    inp("wgu", [DEPTH, NF, 128, 2, 8, 128])
    inp("wd", [DEPTH, 128, NF, 1024])
    inp("cwin", [2, 128, 8, 3072])
    inp("cwout", [2, 128, 8, 1024])
    inp("ckb", [2, 128, 8, 4])
    inp("mwin", [128, 8, 1056])
    inp("mwq", [128, 6, 1536])
    inp("mwkv", [128, 2, 2048])
    inp("mwo", [128, 8, 1024])
    inp("mgsm", [128, 10])
    inp("mrg", [1, 64])
    inp("ropem", [128, 32, 2, 32])
    inp("dwin", [128, 8, 3072])
    inp("dwo", [128, 8, 1024])
    inp("dgn", [1, 128])
    inp("dsub", [128, 1])
    inp("dlam", [1, 256])
    inp("roped", [128, 32, 2, 64])
    io["dQT"] = nc.dram_tensor("dQT", [DH, 2, 128, T], BF16).ap()
    io["dKT"] = nc.dram_tensor("dKT", [DH, 128, T], BF16).ap()
    io["mQT"] = nc.dram_tensor("mQT", [MH, 96, T], BF16).ap()
    io["mKT"] = nc.dram_tensor("mKT", [MH, 96, T], BF16).ap()
    io["y"] = nc.dram_tensor("y", [TL, D], F32, kind="ExternalOutput").ap()
    hkind = "ExternalOutput" if dbg else "Internal"
    io["Ha"] = nc.dram_tensor("Ha", [T, D], F32, kind=hkind).ap()
    io["Hb"] = nc.dram_tensor("Hb", [T, D], F32, kind=hkind).ap()
    io["MR"] = nc.dram_tensor("MR", [DEPTH, 2, 6144], F32, kind=hkind).ap()

    S = Sched(nc)
    with contextlib.ExitStack() as st:
        arena = st.enter_context(nc.sbuf_tensor("arena", [128, ARENA_WORDS], F32))
        ps = [st.enter_context(nc.psum_tensor("ps%d" % i, [128, 512], F32)) for i in range(8)]
        kb = KB(nc, S, arena, ps)
        cst = {}
        cf = kb.f32([128, 256], "consts", dma=True)
        kb.load(cf, io["consts"])
        cst["identf"] = Tl(cf.ap[:, 0:128], cf.b)
        cst["ones"] = Tl(cf.ap[:, 128:256], cf.b)
        cst["identb"] = kb.bf([128, 128], "identb")
        kb.copy(cst["identb"].ap, cf.ap[:, 0:128], reads=[cf.b], writes=[cst["identb"].b])
        cst["mhalf"] = kb.f32([128, 1], "mhalf")
        kb.memset(cst["mhalf"], -0.5)
        modpp = [kb.f32([128, 4, 8, 2], "modpp%d" % i) for i in range(DEPTH)]
        kb.persist()

        phase_mods(kb, cst, io, modpp)
        cur, oth = io["xin"], io["Ha"]

        def swap(cur, oth):
            nxt = io["Hb"] if oth is io["Ha"] else io["Ha"]
            return oth, nxt

        done = False
        for i in range(n_layers):
            kind, slot = i % 3, i // 3
            last = i == DEPTH - 1
            if kind == 0:
                phase_conv(kb, cst, io, i, slot, modpp, cur, oth, with_ctx=not last)
                cur, oth = swap(cur, oth)
            elif kind == 1:
                phase_mla(kb, cst, io, i, modpp, cur, oth)
                cur, oth = swap(cur, oth)
            else:
                phase_diff(kb, cst, io, i, modpp, cur, oth)
                cur, oth = swap(cur, oth)
            if stop_after == ("mix", i):
                break
            if last:
                phase_ffn(kb, cst, io, i, modpp, cur, oth, with_ctx=False, out_lat=io["y"])
            else:
                phase_ffn(kb, cst, io, i, modpp, cur, oth, with_ctx=True)
                cur, oth = swap(cur, oth)
            if stop_after == ("ffn", i):
                break
        S.emit()
        build_program.info = dict(S.stats, sbuf_peak_words=kb.peak, n_ops=len(S.ops))
    return nc


def pp_layout(v, nchunk):
    v = np.asarray(v, np.float32)
    lead = v.shape[:-1]
    r = v.reshape(lead + (nchunk, 128))
    r = np.moveaxis(r, -1, 0)
    return np.ascontiguousarray(r)


def kmajor(w):
    K, N = w.shape
    return np.ascontiguousarray(w.reshape(K // 128, 128, N).transpose(1, 0, 2))


def rope_table(rot_dim):
    rows = TL // 64
    axis_dim = rot_dim // 2
    inv_freq = (np.float32(10000.0) ** (-np.arange(0, axis_dim, 2, dtype=np.float32) / np.float32(axis_dim))).astype(np.float32)
    row = np.repeat(np.arange(rows, dtype=np.float32), 64)
    col = np.tile(np.arange(64, dtype=np.float32), rows)
    ang_r = row[:, None] * inv_freq
    ang_c = col[:, None] * inv_freq
    ang = np.concatenate([ang_r, ang_r, ang_c, ang_c], axis=-1).astype(np.float32)
    cos = np.cos(ang).astype(np.float32)
    sin = np.sin(ang).astype(np.float32)
    hq = rot_dim // 4
    sign = np.concatenate([-np.ones(hq), np.ones(hq), -np.ones(hq), np.ones(hq)]).astype(np.float32)
    tab = np.stack([cos, sin * sign], axis=1)
    return np.ascontiguousarray(tab.reshape(32, 128, 2, rot_dim).transpose(1, 0, 2, 3))


def host_prepare(inputs):
    f = lambda a: np.asarray(a, np.float32)
    sh = {}
    consts = np.zeros((128, 256), np.float32)
    consts[:, 0:128] = np.eye(128, dtype=np.float32)
    consts[:, 128:256] = 1.0
    sh["consts"] = consts
    sh["ng"] = np.ascontiguousarray(np.stack([pp_layout(f(inputs["norm_mix_g"]), 8), pp_layout(f(inputs["norm_ffn_g"]), 8)], axis=2))
    sh["modw"] = np.ascontiguousarray(f(inputs["mod_w"]).reshape(DEPTH, 8, 128, 6144).transpose(0, 2, 1, 3))
    sh["modb"] = np.ascontiguousarray(f(inputs["mod_b"]).reshape(DEPTH, 1, 6144))
    wg = f(inputs["ffn_w_gate"]).reshape(DEPTH, 8, 128, NF, 128)
    wu = f(inputs["ffn_w_up"]).reshape(DEPTH, 8, 128, NF, 128)
    wgu = np.stack([wg, wu], axis=0)
    sh["wgu"] = np.ascontiguousarray(wgu.transpose(1, 4, 3, 0, 2, 5))
    sh["wd"] = np.ascontiguousarray(f(inputs["ffn_w_down"]).reshape(DEPTH, NF, 128, D).transpose(0, 2, 1, 3))
    sh["cwin"] = np.stack([kmajor(w) for w in f(inputs["conv_w_in"])])
    sh["cwout"] = np.stack([kmajor(w) for w in f(inputs["conv_w_out"])])
    ck = f(inputs["conv_k"])
    cb = f(inputs["conv_b"])
    ckb = np.concatenate([ck, cb[:, None, :]], axis=1)
    sh["ckb"] = np.ascontiguousarray(ckb.reshape(2, 4, 8, 128).transpose(0, 3, 2, 1))
    sh["mwin"] = kmajor(f(inputs["mla_w_in"])[0])
    permq = [h * 96 + d for h in range(16) for d in range(64)] + [h * 96 + 64 + r for h in range(16) for r in range(32)]
    sh["mwq"] = kmajor(np.ascontiguousarray(f(inputs["mla_w_q_up"])[0][:, permq]))
    permkv = [h * 128 + d for h in range(16) for d in range(64)] + [h * 128 + 64 + d for h in range(16) for d in range(64)]
    sh["mwkv"] = kmajor(np.ascontiguousarray(f(inputs["mla_w_kv_up"])[0][:, permkv]))
    sh["mwo"] = kmajor(f(inputs["mla_w_out"])[0])
    gsm = np.zeros((128, 10), np.float32)
    gsm[:, 0:6] = pp_layout(f(inputs["mla_q_norm_g"])[0], 6)
    gsm[:, 6:8] = pp_layout(f(inputs["mla_kv_norm_g"])[0], 2)
    gsm[0:64, 8] = f(inputs["mla_qn_nope_g"])[0]
    gsm[64:96, 8] = 1.0
    gsm[0:64, 9] = f(inputs["mla_kn_nope_g"])[0]
    gsm[64:96, 9] = 1.0
    sh["mgsm"] = gsm
    sh["mrg"] = np.concatenate([f(inputs["mla_qn_rope_g"])[0], f(inputs["mla_kn_rope_g"])[0]])[None, :].copy()
    sh["ropem"] = rope_table(32)
    sh["dwin"] = kmajor(f(inputs["diff_w_in"])[0])
    sh["dwo"] = kmajor(f(inputs["diff_w_out"])[0])
    sh["dgn"] = np.concatenate([f(inputs["diff_qn_g"])[0], f(inputs["diff_kn_g"])[0]])[None, :].copy()
    sh["dsub"] = np.ascontiguousarray(f(inputs["diff_sub_g"])[0][:, None])
    sh["dlam"] = np.concatenate([f(inputs["diff_lq1"])[0], f(inputs["diff_lk1"])[0], f(inputs["diff_lq2"])[0], f(inputs["diff_lk2"])[0]])[None, :].copy()
    sh["roped"] = rope_table(64)
    per_core = []
    x = f(inputs["x"])
    ctx = f(inputs["ctx"])
    c = f(inputs["c"])
    cc = f(inputs["c_ctx"])
    for b in range(x.shape[0]):
        m = dict(sh)
        m["xin"] = np.ascontiguousarray(np.concatenate([ctx[b], x[b]], axis=0))
        m["cT"] = np.ascontiguousarray(np.stack([c[b].reshape(8, 128).T, cc.reshape(8, 128).T], axis=2))
        per_core.append(m)
    return per_core


def kernel(**inputs):
    maps = host_prepare(inputs)
    nc = build_program()
    res = run_bass_kernel_spmd(nc, maps, core_ids=list(range(len(maps))))
    return np.stack([np.asarray(r["y"], np.float32) for r in res.results], axis=0)
```

```python
import contextlib
import math
import numpy as np
import concourse.bass as bass
import concourse.mybir as mybir
from concourse.bass_utils import run_bass_kernel_spmd

F32 = mybir.dt.float32
BF16 = mybir.dt.bfloat16
AF = mybir.ActivationFunctionType
ALU = mybir.AluOpType
AX = mybir.AxisListType

D = 1024
TCX = 256
TL = 4096
T = TCX + TL
NT = T // 128
FF = 2816
NF = FF // 128
EPS = 1e-6
DEPTH = 4
SAME_ENG_SYNC = True


class Buf:
    __slots__ = ("name", "w", "rc", "rd", "excl")

    def __init__(self, name="", excl=False):
        self.name = name
        self.w = None
        self.rc = {}
        self.rd = {}
        self.excl = excl


class Op:
    __slots__ = ("idx", "eng", "fn", "cdeps", "dwaits", "dma", "need_inc", "val")

    def __init__(self, idx, eng, fn):
        self.idx = idx
        self.eng = eng
        self.fn = fn
        self.cdeps = set()
        self.dwaits = {}
        self.dma = None
        self.need_inc = False
        self.val = 0


class Sched:
    ENGS = ("pe", "act", "dve", "pool", "sp")

    def __init__(self, nc, n_dma_sems=64):
        self.nc = nc
        self.ops = []
        self.n_dma_sems = n_dma_sems
        self.dma_cnt = [0] * n_dma_sems
        self.last_op = {e: None for e in self.ENGS}
        self.barrier = {e: None for e in self.ENGS}

    def _dep_on(self, op, w):
        if w is None:
            return
        if w[0] == "c":
            op.cdeps.add(w[1])
        else:
            s = w[1]
            op.dwaits[s] = max(op.dwaits.get(s, 0), self.dma_cnt[s] * 16)

    def add(self, eng, fn, reads=(), writes=(), dma_sem=None):
        op = Op(len(self.ops), eng, fn)
        if self.barrier[eng] is not None:
            cd, dw = self.barrier[eng]
            op.cdeps.update(cd)
            for s, v in dw.items():
                op.dwaits[s] = max(op.dwaits.get(s, 0), v)
            self.barrier[eng] = None
        rds = [b for b in reads if not b.excl]
        wrs = list(writes) + [b for b in reads if b.excl]
        for b in rds:
            self._dep_on(op, b.w)
        for b in wrs:
            self._dep_on(op, b.w)
            for e, i in b.rc.items():
                op.cdeps.add(i)
            for s in b.rd:
                op.dwaits[s] = max(op.dwaits.get(s, 0), self.dma_cnt[s] * 16)
        if dma_sem is not None:
            self.dma_cnt[dma_sem] += 1
            op.dma = (dma_sem, self.dma_cnt[dma_sem] * 16)
            me = ("d", dma_sem, op.dma[1])
        else:
            me = ("c", op.idx)
        for b in rds:
            if dma_sem is not None:
                b.rd[dma_sem] = op.dma[1]
            else:
                b.rc[eng] = op.idx
        for b in wrs:
            b.w = me
            b.rc = {}
            b.rd = {}
        op.cdeps.discard(op.idx)
        self.ops.append(op)
        self.last_op[eng] = op.idx
        return op

    def full_barrier(self):
        cd = set(i for i in self.last_op.values() if i is not None)
        dw = {s: c * 16 for s, c in enumerate(self.dma_cnt) if c}
        for e in self.ENGS:
            self.barrier[e] = (set(cd), dict(dw))

    def emit(self):
        nc = self.nc
        ops = self.ops

        def skip(y, op):
            return y.eng == op.eng and (y.eng == "pe" or not SAME_ENG_SYNC)

        for op in ops:
            for d in op.cdeps:
                y = ops[d]
                if y.dma is None and not skip(y, op):
                    y.need_inc = True
        cnt = {e: 0 for e in self.ENGS}
        for op in ops:
            if op.dma is None and op.need_inc:
                cnt[op.eng] += 1
                op.val = cnt[op.eng]
        self.stats = dict(cnt)
        by_eng = {e: [op for op in ops if op.eng == e] for e in self.ENGS}
        for e in self.ENGS:
            self.stats["n_" + e] = len(by_eng[e])
        with contextlib.ExitStack() as st:
            esem = {e: st.enter_context(nc.semaphore("s_" + e)) for e in self.ENGS}
            dsem = [st.enter_context(nc.semaphore("d%d" % i)) for i in range(self.n_dma_sems)]
            block = st.enter_context(nc.Block())
            final_dw = {s: c * 16 for s, c in enumerate(self.dma_cnt) if c}

            def run(e_name, eng):
                seen = {}
                for op in by_eng[e_name]:
                    waits = {}
                    for d in op.cdeps:
                        y = ops[d]
                        if y.dma is not None:
                            k = ("d", y.dma[0])
                            waits[k] = max(waits.get(k, 0), y.dma[1])
                        elif not skip(y, op):
                            k = ("e", y.eng)
                            waits[k] = max(waits.get(k, 0), y.val)
                    for s, v in op.dwaits.items():
                        k = ("d", s)
                        waits[k] = max(waits.get(k, 0), v)
                    for k, v in waits.items():
                        if seen.get(k, 0) >= v:
                            continue
                        seen[k] = v
                        eng.wait_ge(esem[k[1]] if k[0] == "e" else dsem[k[1]], v)
                    ins = op.fn(eng)
                    if op.dma is not None:
                        ins.then_inc(dsem[op.dma[0]], 16)
                    elif op.need_inc:
                        ins.then_inc(esem[e_name], 1)
                if e_name == "sp":
                    for s, v in final_dw.items():
                        eng.wait_ge(dsem[s], v)

            @block.tensor
            def _(eng):
                run("pe", eng)

            @block.scalar
            def _(eng):
                run("act", eng)

            @block.vector
            def _(eng):
                run("dve", eng)

            @block.gpsimd
            def _(eng):
                run("pool", eng)

            @block.sync
            def _(eng):
                run("sp", eng)


class Tl:
    __slots__ = ("ap", "b", "sem")

    def __init__(self, ap, b, sem=None):
        self.ap = ap
        self.b = b
        self.sem = sem


class Ring:
    def __init__(self, tiles):
        self.tiles = tiles
        self.i = 0

    def next(self):
        t = self.tiles[self.i % len(self.tiles)]
        self.i += 1
        return t


ARENA_WORDS = 52224


class KB:
    def __init__(self, nc, S, arena, ps):
        self.nc = nc
        self.S = S
        self.A = arena
        self.AB = arena.bitcast(BF16)
        self.ps = ps
        self.psb = [p.bitcast(BF16) for p in ps]
        self.pb = [Buf("psum%d" % i, excl=True) for i in range(8)]
        self.bank_i = 0
        self.bank_ctr = {}
        self.base = 0
        self.ptr = 0
        self.sem_base = 0
        self.sem_i = 0
        self.psem_i = 0
        self.peak = 0

    def persist(self):
        self.base = self.ptr
        self.sem_base = self.sem_i

    def new_phase(self):
        self.S.full_barrier()
        self.ptr = self.base
        self.sem_i = self.sem_base
        self.psem_i = 0

    def _alloc(self, words):
        words = (words + 15) // 16 * 16
        off = self.ptr
        self.ptr += words
        self.peak = max(self.peak, self.ptr)
        assert self.ptr <= ARENA_WORDS, "SBUF arena overflow %d" % self.ptr
        return off

    N_SP_SEMS = 40

    def sem(self, kind=True):
        if kind == "pool":
            s = self.N_SP_SEMS + self.psem_i
            self.psem_i += 1
            assert s < self.S.n_dma_sems
            return s
        s = self.sem_i
        self.sem_i += 1
        assert s < self.N_SP_SEMS
        return s

    @staticmethod
    def _shape(ap, shape):
        if len(shape) == 2:
            return ap
        if len(shape) == 3:
            return ap.rearrange("p (a b) -> p a b", a=shape[1])
        if len(shape) == 4:
            return ap.rearrange("p (a b c) -> p a b c", a=shape[1], b=shape[2])
        raise ValueError(shape)

    def f32(self, shape, name="", dma=False):
        n = int(np.prod(shape[1:]))
        off = self._alloc(n)
        ap = self._shape(self.A[0:shape[0], off:off + n], shape)
        return Tl(ap, Buf(name), self.sem(dma) if dma else None)

    def bf(self, shape, name="", dma=False):
        n = int(np.prod(shape[1:]))
        off = self._alloc((n + 1) // 2)
        ap = self._shape(self.AB[0:shape[0], 2 * off:2 * off + n], shape)
        return Tl(ap, Buf(name), self.sem(dma) if dma else None)

    def ring(self, kind, shape, n, name="", dma=False):
        mk = self.f32 if kind == "f32" else self.bf
        return Ring([mk(shape, "%s%d" % (name, i), dma) for i in range(n)])

    def bank(self):
        i = self.bank_i
        self.bank_i = (i + 1) % 8
        return i

    def bank_of(self, key, banks):
        c = self.bank_ctr.get(key, 0)
        self.bank_ctr[key] = c + 1
        return banks[c % len(banks)]

    def new_phase_keep(self, keep):
        self.S.full_barrier()
        self.ptr = keep
        self.sem_i = self.sem_base
        self.psem_i = 0

    def dma(self, out_ap, in_ap, reads=(), writes=(), sem=None, q="sp"):
        self.S.add(q, lambda e: e.dma_start(out=out_ap, in_=in_ap), reads=reads, writes=writes, dma_sem=sem)

    def load(self, tl, in_ap, q="sp", sub=None):
        self.dma(tl.ap if sub is None else sub, in_ap, writes=[tl.b], sem=tl.sem, q=q)

    def store(self, out_ap, tl, sub=None, q="sp"):
        self.dma(out_ap, tl.ap if sub is None else sub, reads=[tl.b], sem=tl.sem, q=q)

    def mm(self, bank, out_ap, lhsT, rhs, start, stop, reads, skipchk=False):
        self.S.add("pe", lambda e: e.matmul(out_ap, lhsT=lhsT, rhs=rhs, start=start, stop=stop, skip_group_check=skipchk),
                   reads=reads, writes=[self.pb[bank]])

    def tr(self, bank, out_ap, in_ap, ident, reads):
        self.S.add("pe", lambda e: e.transpose(out=out_ap, in_=in_ap, identity=ident), reads=reads, writes=[self.pb[bank]])

    def act(self, out_ap, in_ap, func, reads, writes, scale=None, bias=None, accum=None):
        kw = {}
        if scale is not None:
            kw["scale"] = scale
        if bias is not None:
            kw["bias"] = bias
        if accum is not None:
            kw["accum_out"] = accum
        self.S.add("act", lambda e: e.activation(out=out_ap, in_=in_ap, func=func, **kw), reads=reads, writes=writes)

    def op(self, eng, fn, reads, writes):
        self.S.add(eng, fn, reads=reads, writes=writes)

    def tt(self, out_ap, in0, in1, op, reads, writes, eng="dve"):
        self.S.add(eng, lambda e: e.tensor_tensor(out=out_ap, in0=in0, in1=in1, op=op), reads=reads, writes=writes)

    def ts(self, out_ap, in0, s1, s2, op0, op1, reads, writes, eng="dve"):
        if s2 is None:
            self.S.add(eng, lambda e: e.tensor_scalar(out=out_ap, in0=in0, scalar1=s1, scalar2=None, op0=op0), reads=reads, writes=writes)
        else:
            self.S.add(eng, lambda e: e.tensor_scalar(out=out_ap, in0=in0, scalar1=s1, scalar2=s2, op0=op0, op1=op1), reads=reads, writes=writes)

    def stt(self, out_ap, in0, scalar, in1, op0, op1, reads, writes, eng="dve"):
        self.S.add(eng, lambda e: e.scalar_tensor_tensor(out=out_ap, in0=in0, scalar=scalar, in1=in1, op0=op0, op1=op1), reads=reads, writes=writes)

    def copy(self, out_ap, in_ap, reads, writes, eng="dve"):
        self.S.add(eng, lambda e: e.tensor_copy(out=out_ap, in_=in_ap), reads=reads, writes=writes)

    def memset(self, tl, val, sub=None, eng="pool"):
        ap = tl.ap if sub is None else sub
        self.S.add(eng, lambda e: e.memset(ap, val), writes=[tl.b])

    def rstd(self, out_tl, ssq_ap, ssq_bufs, n_feat, cst):
        o = out_tl.ap
        self.ts(o, ssq_ap, 1.0 / n_feat, EPS, ALU.mult, ALU.add, reads=ssq_bufs, writes=[out_tl.b])
        mh = cst["mhalf"]
        sh = list(o.shape)
        mhb = mh.ap if len(sh) == 2 and sh[1] == 1 else mh.ap.to_broadcast(sh)
        self.tt(o, o, mhb, ALU.pow, reads=[out_tl.b, mh.b], writes=[out_tl.b], eng="pool")


def prologue(kb, cst, R, src_aps, aT, G, sh, gs_bufs):
    for t, src in enumerate(src_aps):
        hb = R["hbuf"].next()
        kb.load(hb, src)
        jk = R["junk"].next()
        ss = R["ssq"].next()
        kb.act(jk.ap, hb.ap, AF.Square, reads=[hb.b], writes=[jk.b, ss.b], accum=ss.ap)
        rs = R["rs"].next()
        kb.rstd(rs, ss.ap, [ss.b], D, cst)
        nt = R["ntile"].next()
        kb.ts(nt.ap, hb.ap, rs.ap, None, ALU.mult, None, reads=[hb.b, rs.b], writes=[nt.b])
        bk = kb.bank()
        pT = kb.psb[bk][:, 0:1024].rearrange("p (k n) -> p k n", k=8)
        for j in range(8):
            kb.tr(bk, pT[:, j, :], nt.ap[:, j * 128:(j + 1) * 128], cst["identb"].ap, reads=[nt.b, cst["identb"].b])
        for j in range(8):
            kb.act(aT.ap[:, j, t * 128:(t + 1) * 128], pT[:, j, :], AF.Identity, reads=[kb.pb[bk]] + gs_bufs, writes=[aT.b],
                   scale=G[:, j:j + 1], bias=sh[:, j:j + 1])


def prologue_rings(kb, nh=3):
    return {
        "hbuf": kb.ring("f32", [128, 1024], nh, "hbuf", dma=True),
        "junk": kb.ring("bf", [128, 1024], 2, "junk"),
        "ssq": kb.ring("f32", [128, 1], 4, "ssq"),
        "rs": kb.ring("f32", [128, 1], 4, "rs"),
        "ntile": kb.ring("bf", [128, 1024], 2, "ntile"),
    }


def residual_rings(kb):
    return {
        "hres": kb.ring("f32", [128, 1024], 3, "hres", dma=True),
        "rtmp": kb.ring("f32", [128, 512], 2, "rtmp"),
    }


def out_proj_residual(kb, RR, xT, n_k, tile_cols, W, gbc, src_ap, dst_ap):
    hr = RR["hres"].next()
    kb.load(hr, src_ap)
    for nh in range(2):
        bk = kb.bank()
        for k in range(n_k):
            kb.mm(bk, kb.ps[bk][:, :], xT.ap[:, k, tile_cols], W.ap[:, k, nh * 512:(nh + 1) * 512], k == 0, k == n_k - 1,
                  reads=[xT.b, W.b])
        tmp = RR["rtmp"].next()
        kb.tt(tmp.ap, kb.ps[bk][:, :], gbc.ap[:, nh * 512:(nh + 1) * 512], ALU.mult, reads=[kb.pb[bk], gbc.b], writes=[tmp.b])
        hs = hr.ap[:, nh * 512:(nh + 1) * 512]
        kb.tt(hs, hs, tmp.ap, ALU.add, reads=[hr.b, tmp.b], writes=[hr.b], eng="pool")
    kb.store(dst_ap, hr)


def chunks_of(tc, with_ctx=True):
    out = []
    if with_ctx:
        for t0 in range(0, TCX, min(tc, TCX)):
            out.append((1, t0, min(tc, TCX)))
    for t0 in range(TCX, T, tc):
        out.append((0, t0, tc))
    return out


def phase_mods(kb, cst, io, modpp):
    nc = kb.nc
    sT = kb.f32([128, 8, 2], "sT", dma=True)
    kb.load(sT, io["cT"])
    kb.act(sT.ap, sT.ap, AF.Silu, reads=[sT.b], writes=[sT.b])
    ng = kb.f32([128, DEPTH, 2, 8], "ng", dma=True)
    kb.load(ng, io["ng"])
    wring = kb.ring("f32", [128, 8, 512], 2, "modw", dma=True)
    mrows = kb.ring("f32", [2, 6144], 2, "mrow", dma=True)
    brows = kb.ring("f32", [1, 6144], 2, "brow", dma=True)
    for i in range(DEPTH):
        mr = mrows.next()
        br = brows.next()
        kb.load(br, io["modb"][i])
        for n in range(12):
            w = wring.next()
            kb.load(w, io["modw"][i][:, :, n * 512:(n + 1) * 512])
            bk = kb.bank()
            for k in range(8):
                kb.mm(bk, kb.ps[bk][0:2, :], sT.ap[:, k, :], w.ap[:, k, :], k == 0, False, reads=[sT.b, w.b])
            kb.mm(bk, kb.ps[bk][0:2, :], cst["ones"].ap[0:1, 0:2], br.ap[0:1, n * 512:(n + 1) * 512], False, True,
                  reads=[cst["ones"].b, br.b])
            kb.copy(mr.ap[0:2, n * 512:(n + 1) * 512], kb.ps[bk][0:2, :], reads=[kb.pb[bk]], writes=[mr.b])
        kb.store(io["MR"][i], mr)
        bk = kb.bank()
        first = True
        for v, c0 in enumerate((0, 1024, 3072, 4096)):
            for j in range(8):
                col = (v * 8 + j) * 2
                kb.mm(bk, kb.ps[bk][:, col:col + 2], mr.ap[0:2, c0 + j * 128:c0 + (j + 1) * 128], cst["identf"].ap[0:2, 0:2],
                      first, True, reads=[mr.b, cst["identf"].b], skipchk=True)
                first = False
        pp = modpp[i]
        kb.copy(pp.ap.rearrange("p v j t -> p (v j t)"), kb.ps[bk][:, 0:64], reads=[kb.pb[bk]], writes=[pp.b])
        for v, which in ((1, 0), (3, 1)):
            sc = pp.ap[:, v, :, :]
            kb.ts(sc, sc, 1.0, None, ALU.add, None, reads=[pp.b], writes=[pp.b])
            kb.tt(sc, sc, ng.ap[:, i, which, :, None].to_broadcast([128, 8, 2]), ALU.mult, reads=[pp.b, ng.b], writes=[pp.b])


def load_gate_bc(kb, io, layer, which):
    c0 = 2048 if which == 0 else 5120
    out = []
    for typ in range(2):
        g = kb.f32([128, 1024], "gbc%d" % typ, dma=True)
        kb.load(g, io["MR"][layer][typ:typ + 1, c0:c0 + 1024].partition_broadcast(128))
        out.append(g)
    return out


def phase_ffn(kb, cst, io, layer, modpp, src, dst, with_ctx, out_lat=None):
    kb.new_phase()
    gbc = load_gate_bc(kb, io, layer, 1)
    Wd = kb.bf([128, NF, 1024], "Wd", dma="pool")
    for q4 in range(2):
        kb.S.add("pool", lambda e, q4=q4: e.dma_start(out=Wd.ap[:, q4 * 11:(q4 + 1) * 11, :], in_=io["wd"][layer][:, q4 * 11:(q4 + 1) * 11, :]),
                 writes=[Wd.b], dma_sem=Wd.sem)
    R = prologue_rings(kb)
    RR = residual_rings(kb)
    TC = 1024
    aTs = kb.ring("bf", [128, 8, TC], 2, "aT")
    hT = kb.bf([128, NF, TC], "hT")
    wgu = kb.ring("bf", [128, 2, 8, 128], 3, "wgu", dma="pool")
    sgr = kb.ring("bf", [128, 512], 2, "sg")
    pp = modpp[layer]
    chs = chunks_of(TC, with_ctx)

    def do_prologue(ch):
        typ, t0, n = ch
        aT = aTs.next()
        prologue(kb, cst, R, [src[t0 + t * 128:t0 + (t + 1) * 128, :] for t in range(n // 128)], aT,
                 pp.ap[:, 3, :, typ], pp.ap[:, 2, :, typ], [pp.b])
        return aT

    aT_next = do_prologue(chs[0])
    for ci, (typ, t0, n) in enumerate(chs):
        ntile = n // 128
        aT = aT_next
        halves = [(c, min(512, n - c)) for c in range(0, n, 512)]
        for f in range(NF):
            w = wgu.next()
            kb.load(w, io["wgu"][layer][f], q="pool")
            for c0, cn in halves:
                bg = kb.bank()
                for k in range(8):
                    kb.mm(bg, kb.ps[bg][:, 0:cn], w.ap[:, 0, k, :], aT.ap[:, k, c0:c0 + cn], k == 0, k == 7, reads=[w.b, aT.b])
                bu = kb.bank()
                for k in range(8):
                    kb.mm(bu, kb.ps[bu][:, 0:cn], w.ap[:, 1, k, :], aT.ap[:, k, c0:c0 + cn], k == 0, k == 7, reads=[w.b, aT.b])
                sg = sgr.next()
                kb.act(sg.ap[:, 0:cn], kb.ps[bg][:, 0:cn], AF.Silu, reads=[kb.pb[bg]], writes=[sg.b])
                kb.tt(hT.ap[:, f, c0:c0 + cn], sg.ap[:, 0:cn], kb.ps[bu][:, 0:cn], ALU.mult, reads=[sg.b, kb.pb[bu]], writes=[hT.b])
        if ci + 1 < len(chs):
            aT_next = do_prologue(chs[ci + 1])
        for t in range(ntile):
            r0 = t0 + t * 128
            if out_lat is not None:
                d_ap = out_lat[r0 - TCX:r0 - TCX + 128, :]
            else:
                d_ap = dst[r0:r0 + 128, :]
            out_proj_residual(kb, RR, hT, NF, slice(t * 128, (t + 1) * 128), Wd, gbc[typ], src[r0:r0 + 128, :], d_ap)


def phase_conv(kb, cst, io, layer, slot, modpp, src, dst, with_ctx):
    kb.new_phase()
    gbc = load_gate_bc(kb, io, layer, 0)
    Win = kb.bf([128, 8, 3072], "cWin", dma="pool")
    for q4 in range(4):
        kb.S.add("pool", lambda e, q4=q4: e.dma_start(out=Win.ap[:, q4 * 2:(q4 + 1) * 2, :], in_=io["cwin"][slot][:, q4 * 2:(q4 + 1) * 2, :]),
                 writes=[Win.b], dma_sem=Win.sem)
    Wout = kb.bf([128, 8, 1024], "cWout", dma="pool")
    kb.load(Wout, io["cwout"][slot], q="pool")
    ckb = kb.f32([128, 8, 4], "ckb", dma=True)
    kb.load(ckb, io["ckb"][slot])
    R = prologue_rings(kb)
    RR = residual_rings(kb)
    TC = 512
    aTs = kb.ring("bf", [128, 8, TC], 3, "aT")
    halos = kb.ring("bf", [128, 8, 2], 2, "ahalo")
    zr = kb.ring("f32", [128, TC + 2], 2, "z")
    csbr = kb.ring("f32", [128, TC], 2, "csb")
    chbr = kb.ring("f32", [128, 2], 2, "chb")
    yr = kb.ring("f32", [128, TC], 2, "ycv")
    ybg = kb.bf([128, 8, TC], "ybg")
    pp = modpp[layer]
    chs = chunks_of(TC, with_ctx)
    seqs = []
    if with_ctx:
        seqs.append([c for c in chs if c[0] == 1])
    seqs.append([c for c in chs if c[0] == 0])

    def do_prologue(ch):
        typ, t0, n = ch
        aT = aTs.next()
        prologue(kb, cst, R, [src[t0 + t * 128:t0 + (t + 1) * 128, :] for t in range(n // 128)], aT,
                 pp.ap[:, 1, :, typ], pp.ap[:, 0, :, typ], [pp.b])
        return aT

    for seq in seqs:
        aT_list = {}
        aT_list[0] = do_prologue(seq[0])
        for ci, ch in enumerate(seq):
            typ, t0, n = ch
            if ci + 1 < len(seq):
                aT_list[ci + 1] = do_prologue(seq[ci + 1])
            aT = aT_list[ci]
            hl = halos.next()
            if ci == 0:
                kb.memset(hl, 0.0, sub=hl.ap[:, :, 0:1])
            else:
                pv = aT_list[ci - 1]
                pn = seq[ci - 1][2]
                kb.copy(hl.ap[:, :, 0:1], pv.ap[:, :, pn - 1:pn], reads=[pv.b], writes=[hl.b], eng="pool")
            if ci + 1 == len(seq):
                kb.memset(hl, 0.0, sub=hl.ap[:, :, 1:2])
            else:
                nx = aT_list[ci + 1]
                kb.copy(hl.ap[:, :, 1:2], nx.ap[:, :, 0:1], reads=[nx.b], writes=[hl.b], eng="pool")
            for j in range(8):
                z = zr.next()
                bc = kb.bank()
                for k in range(8):
                    kb.mm(bc, kb.ps[bc][:, 0:n], Win.ap[:, k, 1024 + j * 128:1024 + (j + 1) * 128], aT.ap[:, k, 0:n], k == 0, k == 7, reads=[Win.b, aT.b])
                bu = kb.bank()
                for k in range(8):
                    kb.mm(bu, kb.ps[bu][:, 0:n], Win.ap[:, k, 2048 + j * 128:2048 + (j + 1) * 128], aT.ap[:, k, 0:n], k == 0, k == 7, reads=[Win.b, aT.b])
                bh = kb.bank()
                for k in range(8):
                    kb.mm(bh, kb.ps[bh][:, 0:2], Win.ap[:, k, 1024 + j * 128:1024 + (j + 1) * 128], hl.ap[:, k, :], k == 0, k == 7, reads=[Win.b, hl.b])
                for k in range(8):
                    kb.mm(bh, kb.ps[bh][:, 2:4], Win.ap[:, k, 2048 + j * 128:2048 + (j + 1) * 128], hl.ap[:, k, :], k == 0, k == 7, reads=[Win.b, hl.b], skipchk=True)
                csb = csbr.next()
                kb.act(csb.ap[:, 0:n], kb.ps[bc][:, 0:n], AF.Copy, reads=[kb.pb[bc]], writes=[csb.b])
                kb.tt(z.ap[:, 1:n + 1], csb.ap[:, 0:n], kb.ps[bu][:, 0:n], ALU.mult, reads=[csb.b, kb.pb[bu]], writes=[z.b])
                chb = chbr.next()
                kb.act(chb.ap, kb.ps[bh][:, 0:2], AF.Copy, reads=[kb.pb[bh]], writes=[chb.b])
                kb.tt(z.ap[:, 0:1], chb.ap[:, 0:1], kb.ps[bh][:, 2:3], ALU.mult, reads=[chb.b, kb.pb[bh]], writes=[z.b])
                kb.tt(z.ap[:, n + 1:n + 2], chb.ap[:, 1:2], kb.ps[bh][:, 3:4], ALU.mult, reads=[chb.b, kb.pb[bh]], writes=[z.b])
                y = yr.next()
                kb.ts(y.ap[:, 0:n], z.ap[:, 0:n], ckb.ap[:, j, 0:1], ckb.ap[:, j, 3:4], ALU.mult, ALU.add, reads=[z.b, ckb.b], writes=[y.b])
                kb.stt(y.ap[:, 0:n], z.ap[:, 1:n + 1], ckb.ap[:, j, 1:2], y.ap[:, 0:n], ALU.mult, ALU.add, reads=[z.b, ckb.b, y.b], writes=[y.b])
                kb.stt(y.ap[:, 0:n], z.ap[:, 2:n + 2], ckb.ap[:, j, 2:3], y.ap[:, 0:n], ALU.mult, ALU.add, reads=[z.b, ckb.b, y.b], writes=[y.b])
                bb = kb.bank()
                for k in range(8):
                    kb.mm(bb, kb.ps[bb][:, 0:n], Win.ap[:, k, j * 128:(j + 1) * 128], aT.ap[:, k, 0:n], k == 0, k == 7, reads=[Win.b, aT.b])
                kb.tt(ybg.ap[:, j, 0:n], y.ap[:, 0:n], kb.ps[bb][:, 0:n], ALU.mult, reads=[y.b, kb.pb[bb]], writes=[ybg.b])
            for t in range(n // 128):
                r0 = t0 + t * 128
                out_proj_residual(kb, RR, ybg, 8, slice(t * 128, (t + 1) * 128), Wout, gbc[typ], src[r0:r0 + 128, :], dst[r0:r0 + 128, :])


def group_rstd(kb, cst, banks, ncols, gsize, sqr, ssr, rhr):
    sq = sqr.next()
    for i, bk in enumerate(banks):
        kb.act(sq.ap[:, i * ncols:(i + 1) * ncols], kb.ps[bk][:, 0:ncols], AF.Square, reads=[kb.pb[bk]], writes=[sq.b])
    ng = len(banks) * ncols // gsize
    ss = ssr.next()
    kb.op("dve", lambda e: e.tensor_reduce(out=ss.ap[:, 0:ng], in_=sq.ap[:, 0:ng * gsize].rearrange("p (g d) -> p g d", d=gsize),
                                          axis=AX.X, op=ALU.add), reads=[sq.b], writes=[ss.b])
    rh = rhr.next()
    kb.rstd(Tl(rh.ap[:, 0:ng], rh.b), ss.ap[:, 0:ng], [ss.b], gsize, cst)
    return rh


def rope_apply(kb, x, xb, out_ap, out_b, cos, sin, tb, tmpr, lead):
    R = cos.shape[-1]
    t1 = tmpr.next()
    t2 = tmpr.next()
    if lead:
        shp = [128, lead, R]
        cb = cos[:, None, :].to_broadcast(shp)
        t1a = t1.ap[:, 0:lead * R].rearrange("p (h r) -> p h r", h=lead)
        t2a = t2.ap[:, 0:lead * R].rearrange("p (h r) -> p h r", h=lead)
    else:
        cb = cos
        t1a = t1.ap[:, 0:R]
        t2a = t2.ap[:, 0:R]
    kb.tt(t1a, x, cb, ALU.mult, reads=[xb, tb], writes=[t1.b])
    q = R // 4
    for half in range(2):
        if lead:
            xv = x.rearrange("p h (g f i) -> p h g f i", g=2, f=2)[:, :, :, 1 - half, :]
            ov = t2a.rearrange("p h (g f i) -> p h g f i", g=2, f=2)[:, :, :, half, :]
            sv = sin.rearrange("p (g f i) -> p g f i", g=2, f=2)[:, :, half, :][:, None, :, :].to_broadcast([128, lead, 2, q])
        else:
            xv = x.rearrange("p (g f i) -> p g f i", g=2, f=2)[:, :, 1 - half, :]
            ov = t2a.rearrange("p (g f i) -> p g f i", g=2, f=2)[:, :, half, :]
            sv = sin.rearrange("p (g f i) -> p g f i", g=2, f=2)[:, :, half, :]
        kb.tt(ov, xv, sv, ALU.mult, reads=[xb, tb], writes=[t2.b])
    kb.tt(out_ap, t1a, t2a, ALU.add, reads=[t1.b, t2.b], writes=[out_b])


MH = 16


def run_pipelined(gens, interval):
    pending = list(gens)
    active = []
    step = 0
    while pending or active:
        if pending and step % interval == 0:
            active.append(pending.pop(0))
        for g in list(active):
            try:
                next(g)
            except StopIteration:
                active.remove(g)
        step += 1


def attention_core(kb, items, load_head, s_operands, v_operand, ncol, scale, o_banks, s_banks, ptr_, epilogue, PD=2):
    n = len(items)
    sb = [None] * n
    heads = {}
    jobs = {}
    for i in range(n + PD):
        if i < n:
            h, jk, q0, qn, idx, kt, nk = items[i]
            if h not in heads:
                heads[h] = load_head(h)
            lhsT, rhs, rd = s_operands(heads[h], jk, q0, qn, kt)
            bs = kb.bank_of("S", s_banks)
            sb[i] = bs
            kb.mm(bs, kb.ps[bs][:, 0:qn], lhsT, rhs, True, True, reads=rd)
        j = i - PD
        if j < 0:
            continue
        h, jk, q0, qn, idx, kt, nk = items[j]
        nsub = qn // 128
        per_bank = 512 // ncol
        if idx == 0:
            jobs[jk] = [kb.bank_of("O", o_banks) for _ in range((nsub + per_bank - 1) // per_bank)]
        bos = jobs[jk]
        bs = sb[j]
        pt = ptr_.next()
        kb.act(pt.ap[:, 0:qn], kb.ps[bs][:, 0:qn], AF.Exp, reads=[kb.pb[bs]], writes=[pt.b], scale=scale)
        vap, vb = v_operand(h, kt)
        for qs in range(nsub):
            bo = bos[qs // per_bank]
            c0 = (qs % per_bank) * ncol
            kb.mm(bo, kb.ps[bo][:, c0:c0 + ncol], pt.ap[:, qs * 128:(qs + 1) * 128], vap,
                  idx == 0 and qs % per_bank == 0, idx == nk - 1, reads=[pt.b, vb], skipchk=True)
        if idx == nk - 1:
            epilogue(h, jk, q0, qn, bos)
            del jobs[jk]


def phase_mla(kb, cst, io, layer, modpp, src, dst):
    QT = io["mQT"]
    KT = io["mKT"]
    kb.new_phase()
    Vaug = kb.bf([128, NT, MH, 65], "Vaug")
    markV = kb.ptr
    kb.memset(Vaug, 1.0)
    Win = kb.bf([128, 8, 1056], "mWin", dma="pool")
    kb.load(Win, io["mwin"], q="pool")
    Wq = kb.bf([128, 6, 1536], "mWq", dma="pool")
    kb.load(Wq, io["mwq"], q="pool")
    Wkv = kb.bf([128, 2, 2048], "mWkv", dma="pool")
    kb.load(Wkv, io["mwkv"], q="pool")
    gsm = kb.f32([128, 10], "mgsm", dma=True)
    kb.load(gsm, io["mgsm"])
    grope = kb.f32([128, 64], "mgrope", dma=True)
    kb.load(grope, io["mrg"].partition_broadcast(128))
    tab = kb.f32([128, 32, 2, 32], "mrope", dma=True)
    kb.load(tab, io["ropem"])
    R = prologue_rings(kb, 2)
    aTs = kb.ring("bf", [128, 8, 128], 2, "aT")
    ss4r = kb.ring("f32", [128, 4], 2, "ss4")
    r3r = [kb.ring("f32", [128, 1], 2, "r3_%d" % i) for i in range(3)]
    nqr = kb.ring("bf", [128, 768], 2, "nq")
    nkvr = kb.ring("bf", [128, 256], 2, "nkv")
    kper = kb.ring("f32", [128, 32], 2, "kpe")
    kpe2r = kb.ring("f32", [128, 32], 2, "kpe2")
    qlTr = kb.ring("bf", [128, 6, 128], 2, "qlT")
    kvlTr = kb.ring("bf", [128, 2, 128], 2, "kvlT")
    sqr = kb.ring("f32", [128, 1024], 2, "sq")
    ssr = kb.ring("f32", [128, 16], 3, "ssh")
    rhr = kb.ring("f32", [128, 16], 3, "rh")
    qpr = kb.ring("f32", [128, 512], 2, "qp")
    tmpr = kb.ring("f32", [128, 512], 2, "ropetmp")
    Qcr = kb.ring("bf", [128, MH, 96], 2, "Qc")
    Kcr = kb.ring("bf", [128, MH, 96], 2, "Kc")
    QTs = kb.ring("bf", [128, MH, 128], 2, "QTs", dma=True)
    KTs = kb.ring("bf", [128, MH, 128], 2, "KTs", dma=True)
    pp = modpp[layer]
    identb = cst["identb"]

    def tile_gen(gt):
        typ = 1 if gt < 2 else 0
        lt = gt - 2
        r0 = gt * 128
        aT = aTs.next()
        prologue(kb, cst, R, [src[r0:r0 + 128, :]], aT, pp.ap[:, 1, :, typ], pp.ap[:, 0, :, typ], [pp.b])
        yield
        bks = []
        for c0, cn in ((0, 512), (512, 256), (768, 288)):
            bk = kb.bank()
            bks.append(bk)
            for k in range(8):
                kb.mm(bk, kb.ps[bk][:, 0:cn], aT.ap[:, k, :], Win.ap[:, k, c0:c0 + cn], k == 0, k == 7, reads=[aT.b, Win.b])
        b0, b1, b2 = bks
        jk = R["junk"].next()
        ss = ss4r.next()
        kb.act(jk.ap[:, 0:512], kb.ps[b0][:, 0:512], AF.Square, reads=[kb.pb[b0]], writes=[jk.b, ss.b], accum=ss.ap[:, 0:1])
        kb.act(jk.ap[:, 0:256], kb.ps[b1][:, 0:256], AF.Square, reads=[kb.pb[b1]], writes=[jk.b, ss.b], accum=ss.ap[:, 1:2])
        kb.act(jk.ap[:, 0:256], kb.ps[b2][:, 0:256], AF.Square, reads=[kb.pb[b2]], writes=[jk.b, ss.b], accum=ss.ap[:, 2:3])
        kb.act(jk.ap[:, 0:32], kb.ps[b2][:, 256:288], AF.Square, reads=[kb.pb[b2]], writes=[jk.b, ss.b], accum=ss.ap[:, 3:4])
        kb.tt(ss.ap[:, 0:1], ss.ap[:, 0:1], ss.ap[:, 1:2], ALU.add, reads=[ss.b], writes=[ss.b])
        rq, rkv, rkp = [r.next() for r in r3r]
        kb.rstd(rq, ss.ap[:, 0:1], [ss.b], 768, cst)
        kb.rstd(rkv, ss.ap[:, 2:3], [ss.b], 256, cst)
        kb.rstd(rkp, ss.ap[:, 3:4], [ss.b], 32, cst)
        nq = nqr.next()
        kb.ts(nq.ap[:, 0:512], kb.ps[b0][:, 0:512], rq.ap, None, ALU.mult, None, reads=[kb.pb[b0], rq.b], writes=[nq.b])
        kb.ts(nq.ap[:, 512:768], kb.ps[b1][:, 0:256], rq.ap, None, ALU.mult, None, reads=[kb.pb[b1], rq.b], writes=[nq.b])
        nkv = nkvr.next()
        kb.ts(nkv.ap, kb.ps[b2][:, 0:256], rkv.ap, None, ALU.mult, None, reads=[kb.pb[b2], rkv.b], writes=[nkv.b])
        kpe = kper.next()
        kb.stt(kpe.ap, kb.ps[b2][:, 256:288], rkp.ap, grope.ap[:, 32:64], ALU.mult, ALU.mult, reads=[kb.pb[b2], rkp.b, grope.b], writes=[kpe.b])
        if typ == 0:
            kpe2 = kpe2r.next()
            rope_apply(kb, kpe.ap, kpe.b, kpe2.ap, kpe2.b, tab.ap[:, lt, 0, :], tab.ap[:, lt, 1, :], tab.b, tmpr, 0)
            kpe = kpe2
        yield
        bt = kb.bank()
        pT = kb.psb[bt][:, 0:1024].rearrange("p (k n) -> p k n", k=8)
        for j in range(6):
            kb.tr(bt, pT[:, j, :], nq.ap[:, j * 128:(j + 1) * 128], identb.ap, reads=[nq.b, identb.b])
        for j in range(2):
            kb.tr(bt, pT[:, 6 + j, :], nkv.ap[:, j * 128:(j + 1) * 128], identb.ap, reads=[nkv.b, identb.b])
        qlT = qlTr.next()
        kvlT = kvlTr.next()
        for j in range(6):
            kb.act(qlT.ap[:, j, :], pT[:, j, :], AF.Copy, reads=[kb.pb[bt], gsm.b], writes=[qlT.b], scale=gsm.ap[:, j:j + 1])
        for j in range(2):
            kb.act(kvlT.ap[:, j, :], pT[:, 6 + j, :], AF.Copy, reads=[kb.pb[bt], gsm.b], writes=[kvlT.b], scale=gsm.ap[:, 6 + j:7 + j])
        yield
        Qc = Qcr.next()
        Kc = Kcr.next()
        bq = []
        for nn in range(2):
            bk = kb.bank()
            bq.append(bk)
            for k in range(6):
                kb.mm(bk, kb.ps[bk][:, :], qlT.ap[:, k, :], Wq.ap[:, k, nn * 512:(nn + 1) * 512], k == 0, k == 5, reads=[qlT.b, Wq.b])
        rh = group_rstd(kb, cst, bq, 512, 64, sqr, ssr, rhr)
        for nn in range(2):
            kb.tt(Qc.ap[:, nn * 8:(nn + 1) * 8, 0:64], kb.ps[bq[nn]][:, :].rearrange("p (h d) -> p h d", d=64),
                  rh.ap[:, nn * 8:(nn + 1) * 8, None].to_broadcast([128, 8, 64]), ALU.mult, reads=[kb.pb[bq[nn]], rh.b], writes=[Qc.b])
        bk = kb.bank()
        for k in range(6):
            kb.mm(bk, kb.ps[bk][:, :], qlT.ap[:, k, :], Wq.ap[:, k, 1024:1536], k == 0, k == 5, reads=[qlT.b, Wq.b])
        rh = group_rstd(kb, cst, [bk], 512, 32, sqr, ssr, rhr)
        qp = qpr.next()
        qp3 = qp.ap.rearrange("p (h r) -> p h r", r=32)
        kb.tt(qp3, kb.ps[bk][:, :].rearrange("p (h r) -> p h r", r=32), rh.ap[:, 0:16, None].to_broadcast([128, 16, 32]), ALU.mult,
              reads=[kb.pb[bk], rh.b], writes=[qp.b])
        kb.tt(qp3, qp3, grope.ap[:, None, 0:32].to_broadcast([128, 16, 32]), ALU.mult, reads=[qp.b, grope.b], writes=[qp.b], eng="pool")
        if typ == 0:
            rope_apply(kb, qp3, qp.b, Qc.ap[:, :, 64:96], Qc.b, tab.ap[:, lt, 0, :], tab.ap[:, lt, 1, :], tab.b, tmpr, 16)
        else:
            kb.copy(Qc.ap[:, :, 64:96], qp3, reads=[qp.b], writes=[Qc.b])
        yield
        bkk = []
        for nn in range(2):
            bk = kb.bank()
            bkk.append(bk)
            for k in range(2):
                kb.mm(bk, kb.ps[bk][:, :], kvlT.ap[:, k, :], Wkv.ap[:, k, nn * 512:(nn + 1) * 512], k == 0, k == 1, reads=[kvlT.b, Wkv.b])
        rh = group_rstd(kb, cst, bkk, 512, 64, sqr, ssr, rhr)
        for nn in range(2):
            kb.tt(Kc.ap[:, nn * 8:(nn + 1) * 8, 0:64], kb.ps[bkk[nn]][:, :].rearrange("p (h d) -> p h d", d=64),
                  rh.ap[:, nn * 8:(nn + 1) * 8, None].to_broadcast([128, 8, 64]), ALU.mult, reads=[kb.pb[bkk[nn]], rh.b], writes=[Kc.b])
        kb.copy(Kc.ap[:, :, 64:96], kpe.ap[:, None, :].to_broadcast([128, 16, 32]), reads=[kpe.b], writes=[Kc.b], eng="pool")
        for nn in range(2):
            bk = kb.bank()
            for k in range(2):
                kb.mm(bk, kb.ps[bk][:, :], kvlT.ap[:, k, :], Wkv.ap[:, k, 1024 + nn * 512:1024 + (nn + 1) * 512], k == 0, k == 1, reads=[kvlT.b, Wkv.b])
            kb.act(Vaug.ap[:, gt, nn * 8:(nn + 1) * 8, 0:64], kb.ps[bk][:, :].rearrange("p (h d) -> p h d", d=64), AF.Copy,
                   reads=[kb.pb[bk]], writes=[Vaug.b])
        yield
        qts = QTs.next()
        kts = KTs.next()
        for src_t, stg, gcol in ((Qc, qts, 8), (Kc, kts, 9)):
            for hh in range(2):
                bk = kb.bank()
                pv = kb.psb[bk][:, 0:1024].rearrange("p (h n) -> p h n", h=8)
                for h8 in range(8):
                    kb.tr(bk, pv[0:96, h8, :], src_t.ap[:, hh * 8 + h8, :], identb.ap, reads=[src_t.b, identb.b])
                kb.act(stg.ap[0:96, hh * 8:(hh + 1) * 8, :], pv[0:96, :, :], AF.Copy, reads=[kb.pb[bk], gsm.b], writes=[stg.b],
                       scale=gsm.ap[0:96, gcol:gcol + 1])
        kb.store(QT[:, :, r0:r0 + 128].rearrange("h p t -> p h t"), qts, sub=qts.ap[0:96, :, :])
        kb.store(KT[:, :, r0:r0 + 128].rearrange("h p t -> p h t"), kts, sub=kts.ap[0:96, :, :])

    run_pipelined([tile_gen(gt) for gt in range(NT)], 2)

    kb.new_phase_keep(markV)
    Osb = kb.bf([128, NT, 1024], "Osb")
    markO = kb.ptr
    KTr = kb.ring("bf", [128, T], 2, "KTh", dma=True)
    QTr = kb.ring("bf", [128, T], 2, "QTh", dma=True)
    ptr_ = kb.ring("bf", [128, 512], 4, "PT")
    recr = kb.ring("f32", [128, 4], 2, "rec")
    items = []
    for h in range(MH):
        jobs = [(0, 256, [0, 1])] + [(TCX + c * 512, 512, list(range(NT))) for c in range(TL // 512)]
        for q0, qn, kts_ in jobs:
            for idx, kt in enumerate(kts_):
                items.append((h, (h, q0), q0, qn, idx, kt, len(kts_)))

    def load_head(h):
        kth = KTr.next()
        qth = QTr.next()
        kb.load(kth, KT[h], sub=kth.ap[0:96, :])
        kb.load(qth, QT[h], sub=qth.ap[0:96, :])
        return kth, qth

    def s_operands(hd, jk, q0, qn, kt):
        kth, qth = hd
        return kth.ap[0:96, kt * 128:(kt + 1) * 128], qth.ap[0:96, q0:q0 + qn], [kth.b, qth.b]

    def v_operand(h, kt):
        return Vaug.ap[:, kt, h, :], Vaug.b

    def epilogue(h, jk, q0, qn, bos):
        bo = bos[0]
        nsub = qn // 128
        rec = recr.next()
        ov = kb.ps[bo][:, 0:nsub * 65].rearrange("p (s c) -> p s c", c=65)
        kb.op("dve", lambda e: e.reciprocal(out=rec.ap[:, 0:nsub], in_=ov[:, :, 64]), reads=[kb.pb[bo]], writes=[rec.b])
        kb.tt(Osb.ap[:, q0 // 128:q0 // 128 + nsub, h * 64:(h + 1) * 64], ov[:, :, 0:64],
              rec.ap[:, 0:nsub, None].to_broadcast([128, nsub, 64]), ALU.mult, reads=[kb.pb[bo], rec.b], writes=[Osb.b])

    attention_core(kb, items, load_head, s_operands, v_operand, 65, 96.0 ** -0.5, [0, 1], [2, 3, 4, 5, 6, 7], ptr_, epilogue)

    kb.new_phase_keep(markO)
    attn_out_phase(kb, cst, io, layer, Osb, io["mwo"], None, src, dst)


def attn_out_phase(kb, cst, io, layer, Osb, w_dram, pre, src, dst):
    gbc = load_gate_bc(kb, io, layer, 0)
    Wo = kb.bf([128, 8, 1024], "Wo", dma="pool")
    kb.load(Wo, w_dram, q="pool")
    RR = residual_rings(kb)
    OTr = kb.ring("bf", [128, 8, 128], 2, "OT")
    identb = cst["identb"]
    for t in range(NT):
        typ = 1 if t < 2 else 0
        if pre is not None:
            o_ap, o_b, scl = pre(t)
        else:
            o_ap, o_b, scl = Osb.ap[:, t, :], Osb.b, None
        bk = kb.bank()
        pT = kb.psb[bk][:, 0:1024].rearrange("p (k n) -> p k n", k=8)
        for j in range(8):
            kb.tr(bk, pT[:, j, :], o_ap[:, j * 128:(j + 1) * 128], identb.ap, reads=[o_b, identb.b])
        OT = OTr.next()
        if scl is None:
            kb.copy(OT.ap, pT, reads=[kb.pb[bk]], writes=[OT.b])
        else:
            kb.act(OT.ap, pT, AF.Copy, reads=[kb.pb[bk], scl[1]], writes=[OT.b], scale=scl[0])
        r0 = t * 128
        out_proj_residual(kb, RR, OT, 8, slice(0, 128), Wo, gbc[typ], src[r0:r0 + 128, :], dst[r0:r0 + 128, :])


DH = 8


def phase_diff(kb, cst, io, layer, modpp, src, dst):
    lam_init = 0.8 - 0.6 * math.exp(-0.3 * layer)
    QT = io["dQT"]
    KT = io["dKT"]
    kb.new_phase()
    Vaug = kb.bf([128, NT, DH, 129], "dVaug")
    markV = kb.ptr
    kb.memset(Vaug, 1.0)
    Win = kb.bf([128, 8, 3072], "dWin", dma="pool")
    for q4 in range(4):
        kb.S.add("pool", lambda e, q4=q4: e.dma_start(out=Win.ap[:, q4 * 2:(q4 + 1) * 2, :], in_=io["dwin"][:, q4 * 2:(q4 + 1) * 2, :]),
                 writes=[Win.b], dma_sem=Win.sem)
    gn = kb.f32([128, 128], "dgn", dma=True)
    kb.load(gn, io["dgn"].partition_broadcast(128))
    tab = kb.f32([128, 32, 2, 64], "drope", dma=True)
    kb.load(tab, io["roped"])
    R = prologue_rings(kb, 2)
    aTs = kb.ring("bf", [128, 8, 128], 3, "aT")
    sqr = kb.ring("f32", [128, 1024], 1, "sq")
    ssr = kb.ring("f32", [128, 16], 3, "ssh")
    rhr = kb.ring("f32", [128, 16], 3, "rh")
    qnr = kb.ring("f32", [128, 1024], 2, "qn")
    tmpr = kb.ring("f32", [128, 1024], 2, "ropetmp")
    QKc = kb.ring("bf", [128, 1024], 2, "QKc")
    QTs = kb.ring("bf", [128, DH, 2, 128], 2, "dQTs", dma=True)
    KTs = kb.ring("bf", [128, DH, 128], 2, "dKTs", dma=True)
    for tl in QTs.tiles:
        kb.memset(tl, 0.0)
    pp = modpp[layer]
    identb = cst["identb"]

    def tile_gen(gt):
        typ = 1 if gt < 2 else 0
        lt = gt - 2
        r0 = gt * 128
        aT = aTs.next()
        prologue(kb, cst, R, [src[r0:r0 + 128, :]], aT, pp.ap[:, 1, :, typ], pp.ap[:, 0, :, typ], [pp.b])
        yield
        qts = QTs.next()
        kts = KTs.next()
        for which in range(2):
            bks = []
            for nn in range(2):
                bk = kb.bank()
                bks.append(bk)
                c0 = which * 1024 + nn * 512
                for k in range(8):
                    kb.mm(bk, kb.ps[bk][:, :], aT.ap[:, k, :], Win.ap[:, k, c0:c0 + 512], k == 0, k == 7, reads=[aT.b, Win.b])
            rh = group_rstd(kb, cst, bks, 512, 64, sqr, ssr, rhr)
            qn = qnr.next()
            qn3 = qn.ap.rearrange("p (g d) -> p g d", d=64)
            for nn in range(2):
                kb.tt(qn3[:, nn * 8:(nn + 1) * 8, :], kb.ps[bks[nn]][:, :].rearrange("p (g d) -> p g d", d=64),
                      rh.ap[:, nn * 8:(nn + 1) * 8, None].to_broadcast([128, 8, 64]), ALU.mult, reads=[kb.pb[bks[nn]], rh.b], writes=[qn.b])
            kb.tt(qn3, qn3, gn.ap[:, None, which * 64:(which + 1) * 64].to_broadcast([128, 16, 64]), ALU.mult,
                  reads=[qn.b, gn.b], writes=[qn.b], eng="pool")
            yield
            qc = QKc.next()
            qc3 = qc.ap.rearrange("p (g d) -> p g d", d=64)
            if typ == 0:
                rope_apply(kb, qn3, qn.b, qc3, qc.b, tab.ap[:, lt, 0, :], tab.ap[:, lt, 1, :], tab.b, tmpr, 16)
            else:
                kb.copy(qc3, qn3, reads=[qn.b], writes=[qc.b])
            bk = kb.bank()
            pT = kb.psb[bk][:, 0:1024].rearrange("p (h n) -> p h n", h=8)
            for h in range(DH):
                kb.tr(bk, pT[:, h, :], qc.ap[:, h * 128:(h + 1) * 128], identb.ap, reads=[qc.b, identb.b])
            if which == 0:
                kb.op("act", lambda e, pT=pT: e.activation(out=qts.ap[0:64, :, 0, :], in_=pT[0:64, :, :], func=AF.Copy),
                      reads=[kb.pb[bk]], writes=[qts.b])
                kb.copy(qts.ap[64:128, :, 1, :], pT[64:128, :, :], reads=[kb.pb[bk]], writes=[qts.b])
            else:
                kb.op("act", lambda e, pT=pT: e.activation(out=kts.ap, in_=pT, func=AF.Copy), reads=[kb.pb[bk]], writes=[kts.b])
            yield
        for nn in range(2):
            bk = kb.bank()
            c0 = 2048 + nn * 512
            for k in range(8):
                kb.mm(bk, kb.ps[bk][:, :], aT.ap[:, k, :], Win.ap[:, k, c0:c0 + 512], k == 0, k == 7, reads=[aT.b, Win.b])
            kb.copy(Vaug.ap[:, gt, nn * 4:(nn + 1) * 4, 0:128], kb.ps[bk][:, :].rearrange("p (h d) -> p h d", d=128),
                    reads=[kb.pb[bk]], writes=[Vaug.b])
        kb.store(QT[:, :, :, r0:r0 + 128].rearrange("h v p t -> p h v t"), qts)
        kb.store(KT[:, :, r0:r0 + 128].rearrange("h p t -> p h t"), kts)

    run_pipelined([tile_gen(gt) for gt in range(NT)], 2)

    kb.new_phase_keep(markV)
    Osb = kb.bf([128, NT, 1024], "dOsb")
    markO = kb.ptr
    lrow = kb.f32([1, 256], "lrow", dma=True)
    kb.load(lrow, io["dlam"])
    lpr = kb.f32([1, 128], "lprod")
    lv = lrow.ap.rearrange("p (a b d) -> p a b d", a=2, b=2)
    kb.tt(lpr.ap.rearrange("p (a d) -> p a d", a=2), lv[:, :, 0, :], lv[:, :, 1, :], ALU.mult, reads=[lrow.b], writes=[lpr.b])
    lsum = kb.f32([1, 2], "lsum")
    kb.op("dve", lambda e: e.tensor_reduce(out=lsum.ap, in_=lpr.ap.rearrange("p (a d) -> p a d", a=2), axis=AX.X, op=ALU.add),
          reads=[lpr.b], writes=[lsum.b])
    kb.act(lsum.ap, lsum.ap, AF.Exp, reads=[lsum.b], writes=[lsum.b])
    nl1 = kb.f32([1, 2], "nl1")
    kb.tt(nl1.ap[:, 0:1], lsum.ap[:, 1:2], lsum.ap[:, 0:1], ALU.subtract, reads=[lsum.b], writes=[nl1.b])
    kb.ts(nl1.ap[:, 0:1], nl1.ap[:, 0:1], -lam_init, None, ALU.add, None, reads=[nl1.b], writes=[nl1.b])
    kb.copy(nl1.ap[:, 1:2], nl1.ap[:, 0:1], reads=[nl1.b], writes=[nl1.b])
    bk = kb.bank()
    kb.mm(bk, kb.ps[bk][:, 0:2], cst["ones"].ap[0:1, 0:128], nl1.ap[0:1, 0:2], True, True, reads=[cst["ones"].b, nl1.b])
    neglam = kb.f32([128, 2], "neglam")
    kb.copy(neglam.ap, kb.ps[bk][:, 0:2], reads=[kb.pb[bk]], writes=[neglam.b])

    KTr = kb.ring("bf", [128, T], 2, "dKTh", dma=True)
    QTr = kb.ring("bf", [128, 2, T], 2, "dQTh", dma=True)
    ptr_ = kb.ring("bf", [128, 512], 4, "PT")
    recr = kb.ring("f32", [128, 4], 2, "rec")
    o1r = kb.ring("f32", [128, 4, 128], 2, "O1n")
    o2r = kb.ring("f32", [128, 3, 128], 2, "O2n")
    items = []
    for h in range(DH):
        jobs = [(0, 256, [0, 1])] + [(TCX + c * 512, 512, list(range(NT))) for c in range(TL // 512)]
        for q0, qn_, kts_ in jobs:
            for sv in range(2):
                for idx, kt in enumerate(kts_):
                    items.append((h, (h, q0, sv), q0, qn_, idx, kt, len(kts_)))

    def load_head(h):
        kth = KTr.next()
        qth = QTr.next()
        kb.load(kth, KT[h])
        kb.load(qth, QT[h].rearrange("v p t -> p v t"))
        return kth, qth

    def s_operands(hd, jk, q0, qn, kt):
        kth, qth = hd
        return kth.ap[:, kt * 128:(kt + 1) * 128], qth.ap[:, jk[2], q0:q0 + qn], [kth.b, qth.b]

    def v_operand(h, kt):
        return Vaug.ap[:, kt, h, :], Vaug.b

    o1s = {}

    def epilogue(h, jk, q0, qn, bos):
        sv = jk[2]
        nsub = qn // 128
        if sv == 0:
            o1s[(h, q0)] = o1r.next()
        o1 = o1s[(h, q0)]
        rec = recr.next()
        for b2, bo in enumerate(bos):
            s0 = b2 * 3
            ns = min(3, nsub - s0)
            ov = kb.ps[bo][:, 0:ns * 129].rearrange("p (s c) -> p s c", c=129)
            rsl = rec.ap[:, s0:s0 + ns]
            kb.op("dve", lambda e, rsl=rsl, ov=ov: e.reciprocal(out=rsl, in_=ov[:, :, 128]), reads=[kb.pb[bo]], writes=[rec.b])
            if sv == 0:
                kb.tt(o1.ap[:, s0:s0 + ns, :], ov[:, :, 0:128], rsl[:, :, None].to_broadcast([128, ns, 128]), ALU.mult,
                      reads=[kb.pb[bo], rec.b], writes=[o1.b])
            else:
                o2 = o2r.next()
                kb.tt(o2.ap[:, 0:ns, :], ov[:, :, 0:128], rsl[:, :, None].to_broadcast([128, ns, 128]), ALU.mult,
                      reads=[kb.pb[bo], rec.b], writes=[o2.b])
                tl0 = q0 // 128 + s0
                kb.stt(Osb.ap[:, tl0:tl0 + ns, h * 128:(h + 1) * 128], o2.ap[:, 0:ns, :], neglam.ap[:, 0:1], o1.ap[:, s0:s0 + ns, :],
                       ALU.mult, ALU.add, reads=[o2.b, neglam.b, o1.b], writes=[Osb.b])
        if sv == 1:
            del o1s[(h, q0)]

    attention_core(kb, items, load_head, s_operands, v_operand, 129, 64.0 ** -0.5, [0, 1, 2, 3], [4, 5, 6, 7], ptr_, epilogue)

    kb.new_phase_keep(markO)
    dsub = kb.f32([128, 1], "dsub", dma=True)
    kb.load(dsub, io["dsub"])
    sqr = kb.ring("f32", [128, 1024], 2, "sq")
    ssr = kb.ring("f32", [128, 8], 2, "ssh")
    rhr = kb.ring("f32", [128, 8], 2, "rh")
    onr = kb.ring("bf", [128, 1024], 2, "On")

    def pre(t):
        sq = sqr.next()
        kb.act(sq.ap, Osb.ap[:, t, :], AF.Square, reads=[Osb.b], writes=[sq.b])
        ss = ssr.next()
        kb.op("dve", lambda e: e.tensor_reduce(out=ss.ap, in_=sq.ap.rearrange("p (h d) -> p h d", d=128), axis=AX.X, op=ALU.add),
              reads=[sq.b], writes=[ss.b])
        rh = rhr.next()
        kb.rstd(rh, ss.ap, [ss.b], 128, cst)
        on = onr.next()
        kb.stt(on.ap.rearrange("p (h d) -> p h d", d=128), Osb.ap[:, t, :].rearrange("p (h d) -> p h d", d=128), 1.0 - lam_init,
               rh.ap[:, :, None].to_broadcast([128, 8, 128]), ALU.mult, ALU.mult, reads=[Osb.b, rh.b], writes=[on.b])
        return on.ap, on.b, (dsub.ap[:, 0:1], dsub.b)

    attn_out_phase(kb, cst, io, layer, Osb, io["dwo"], pre, src, dst)

def build_program(n_layers=DEPTH, dbg=False, stop_after=None):
    nc = bass.Bass("TRN2", target_bir_lowering=False)
    io = {}

    def inp(name, shape):
        io[name] = nc.dram_tensor(name, list(shape), F32, kind="ExternalInput").ap()

    inp("xin", [T, D])
    inp("cT", [128, 8, 2])
    inp("ng", [128, DEPTH, 2, 8])
    inp("consts", [128, 256])
    inp("modw", [DEPTH, 128, 8, 6144])
    inp("modb", [DEPTH, 1, 6144])
    inp("wgu", [DEPTH, NF, 128, 2, 8, 128])
    inp("wd", [DEPTH, 128, NF, 1024])
    inp("cwin", [2, 128, 8, 3072])
    inp("cwout", [2, 128, 8, 1024])
    inp("ckb", [2, 128, 8, 4])
    inp("mwin", [128, 8, 1056])
    inp("mwq", [128, 6, 1536])
    inp("mwkv", [128, 2, 2048])
    inp("mwo", [128, 8, 1024])
    inp("mgsm", [128, 10])
    inp("mrg", [1, 64])
    inp("ropem", [128, 32, 2, 32])
    inp("dwin", [128, 8, 3072])
    inp("dwo", [128, 8, 1024])
    inp("dgn", [1, 128])
    inp("dsub", [128, 1])
    inp("dlam", [1, 256])
    inp("roped", [128, 32, 2, 64])
    io["dQT"] = nc.dram_tensor("dQT", [DH, 2, 128, T], BF16).ap()
    io["dKT"] = nc.dram_tensor("dKT", [DH, 128, T], BF16).ap()
    io["mQT"] = nc.dram_tensor("mQT", [MH, 96, T], BF16).ap()
    io["mKT"] = nc.dram_tensor("mKT", [MH, 96, T], BF16).ap()
    io["y"] = nc.dram_tensor("y", [TL, D], F32, kind="ExternalOutput").ap()
    hkind = "ExternalOutput" if dbg else "Internal"
    io["Ha"] = nc.dram_tensor("Ha", [T, D], F32, kind=hkind).ap()
    io["Hb"] = nc.dram_tensor("Hb", [T, D], F32, kind=hkind).ap()
    io["MR"] = nc.dram_tensor("MR", [DEPTH, 2, 6144], F32, kind=hkind).ap()

    S = Sched(nc)
    with contextlib.ExitStack() as st:
        arena = st.enter_context(nc.sbuf_tensor("arena", [128, ARENA_WORDS], F32))
        ps = [st.enter_context(nc.psum_tensor("ps%d" % i, [128, 512], F32)) for i in range(8)]
        kb = KB(nc, S, arena, ps)
        cst = {}
        cf = kb.f32([128, 256], "consts", dma=True)
        kb.load(cf, io["consts"])
        cst["identf"] = Tl(cf.ap[:, 0:128], cf.b)
        cst["ones"] = Tl(cf.ap[:, 128:256], cf.b)
        cst["identb"] = kb.bf([128, 128], "identb")
        kb.copy(cst["identb"].ap, cf.ap[:, 0:128], reads=[cf.b], writes=[cst["identb"].b])
        cst["mhalf"] = kb.f32([128, 1], "mhalf")
        kb.memset(cst["mhalf"], -0.5)
        modpp = [kb.f32([128, 4, 8, 2], "modpp%d" % i) for i in range(DEPTH)]
        kb.persist()

        phase_mods(kb, cst, io, modpp)
        cur, oth = io["xin"], io["Ha"]

        def swap(cur, oth):
            nxt = io["Hb"] if oth is io["Ha"] else io["Ha"]
            return oth, nxt

        done = False
        for i in range(n_layers):
            kind, slot = i % 3, i // 3
            last = i == DEPTH - 1
            if kind == 0:
                phase_conv(kb, cst, io, i, slot, modpp, cur, oth, with_ctx=not last)
                cur, oth = swap(cur, oth)
            elif kind == 1:
                phase_mla(kb, cst, io, i, modpp, cur, oth)
                cur, oth = swap(cur, oth)
            else:
                phase_diff(kb, cst, io, i, modpp, cur, oth)
                cur, oth = swap(cur, oth)
            if stop_after == ("mix", i):
                break
            if last:
                phase_ffn(kb, cst, io, i, modpp, cur, oth, with_ctx=False, out_lat=io["y"])
            else:
                phase_ffn(kb, cst, io, i, modpp, cur, oth, with_ctx=True)
                cur, oth = swap(cur, oth)
            if stop_after == ("ffn", i):
                break
        S.emit()
        build_program.info = dict(S.stats, sbuf_peak_words=kb.peak, n_ops=len(S.ops))
    return nc


def pp_layout(v, nchunk):
    v = np.asarray(v, np.float32)
    lead = v.shape[:-1]
    r = v.reshape(lead + (nchunk, 128))
    r = np.moveaxis(r, -1, 0)
    return np.ascontiguousarray(r)


def kmajor(w):
    K, N = w.shape
    return np.ascontiguousarray(w.reshape(K // 128, 128, N).transpose(1, 0, 2))


def rope_table(rot_dim):
    rows = TL // 64
    axis_dim = rot_dim // 2
    inv_freq = (np.float32(10000.0) ** (-np.arange(0, axis_dim, 2, dtype=np.float32) / np.float32(axis_dim))).astype(np.float32)
    row = np.repeat(np.arange(rows, dtype=np.float32), 64)
    col = np.tile(np.arange(64, dtype=np.float32), rows)
    ang_r = row[:, None] * inv_freq
    ang_c = col[:, None] * inv_freq
    ang = np.concatenate([ang_r, ang_r, ang_c, ang_c], axis=-1).astype(np.float32)
    cos = np.cos(ang).astype(np.float32)
    sin = np.sin(ang).astype(np.float32)
    hq = rot_dim // 4
    sign = np.concatenate([-np.ones(hq), np.ones(hq), -np.ones(hq), np.ones(hq)]).astype(np.float32)
    tab = np.stack([cos, sin * sign], axis=1)
    return np.ascontiguousarray(tab.reshape(32, 128, 2, rot_dim).transpose(1, 0, 2, 3))


def host_prepare(inputs):
    f = lambda a: np.asarray(a, np.float32)
    sh = {}
    consts = np.zeros((128, 256), np.float32)
    consts[:, 0:128] = np.eye(128, dtype=np.float32)
    consts[:, 128:256] = 1.0
    sh["consts"] = consts
    sh["ng"] = np.ascontiguousarray(np.stack([pp_layout(f(inputs["norm_mix_g"]), 8), pp_layout(f(inputs["norm_ffn_g"]), 8)], axis=2))
    sh["modw"] = np.ascontiguousarray(f(inputs["mod_w"]).reshape(DEPTH, 8, 128, 6144).transpose(0, 2, 1, 3))
    sh["modb"] = np.ascontiguousarray(f(inputs["mod_b"]).reshape(DEPTH, 1, 6144))
    wg = f(inputs["ffn_w_gate"]).reshape(DEPTH, 8, 128, NF, 128)
    wu = f(inputs["ffn_w_up"]).reshape(DEPTH, 8, 128, NF, 128)
    wgu = np.stack([wg, wu], axis=0)
    sh["wgu"] = np.ascontiguousarray(wgu.transpose(1, 4, 3, 0, 2, 5))
    sh["wd"] = np.ascontiguousarray(f(inputs["ffn_w_down"]).reshape(DEPTH, NF, 128, D).transpose(0, 2, 1, 3))
    sh["cwin"] = np.stack([kmajor(w) for w in f(inputs["conv_w_in"])])
    sh["cwout"] = np.stack([kmajor(w) for w in f(inputs["conv_w_out"])])
    ck = f(inputs["conv_k"])
    cb = f(inputs["conv_b"])
    ckb = np.concatenate([ck, cb[:, None, :]], axis=1)
    sh["ckb"] = np.ascontiguousarray(ckb.reshape(2, 4, 8, 128).transpose(0, 3, 2, 1))
    sh["mwin"] = kmajor(f(inputs["mla_w_in"])[0])
    permq = [h * 96 + d for h in range(16) for d in range(64)] + [h * 96 + 64 + r for h in range(16) for r in range(32)]
    sh["mwq"] = kmajor(np.ascontiguousarray(f(inputs["mla_w_q_up"])[0][:, permq]))
    permkv = [h * 128 + d for h in range(16) for d in range(64)] + [h * 128 + 64 + d for h in range(16) for d in range(64)]
    sh["mwkv"] = kmajor(np.ascontiguousarray(f(inputs["mla_w_kv_up"])[0][:, permkv]))
    sh["mwo"] = kmajor(f(inputs["mla_w_out"])[0])
    gsm = np.zeros((128, 10), np.float32)
    gsm[:, 0:6] = pp_layout(f(inputs["mla_q_norm_g"])[0], 6)
    gsm[:, 6:8] = pp_layout(f(inputs["mla_kv_norm_g"])[0], 2)
    gsm[0:64, 8] = f(inputs["mla_qn_nope_g"])[0]
    gsm[64:96, 8] = 1.0
    gsm[0:64, 9] = f(inputs["mla_kn_nope_g"])[0]
    gsm[64:96, 9] = 1.0
    sh["mgsm"] = gsm
    sh["mrg"] = np.concatenate([f(inputs["mla_qn_rope_g"])[0], f(inputs["mla_kn_rope_g"])[0]])[None, :].copy()
    sh["ropem"] = rope_table(32)
    sh["dwin"] = kmajor(f(inputs["diff_w_in"])[0])
    sh["dwo"] = kmajor(f(inputs["diff_w_out"])[0])
    sh["dgn"] = np.concatenate([f(inputs["diff_qn_g"])[0], f(inputs["diff_kn_g"])[0]])[None, :].copy()
    sh["dsub"] = np.ascontiguousarray(f(inputs["diff_sub_g"])[0][:, None])
    sh["dlam"] = np.concatenate([f(inputs["diff_lq1"])[0], f(inputs["diff_lk1"])[0], f(inputs["diff_lq2"])[0], f(inputs["diff_lk2"])[0]])[None, :].copy()
    sh["roped"] = rope_table(64)
    per_core = []
    x = f(inputs["x"])
    ctx = f(inputs["ctx"])
    c = f(inputs["c"])
    cc = f(inputs["c_ctx"])
    for b in range(x.shape[0]):
        m = dict(sh)
        m["xin"] = np.ascontiguousarray(np.concatenate([ctx[b], x[b]], axis=0))
        m["cT"] = np.ascontiguousarray(np.stack([c[b].reshape(8, 128).T, cc.reshape(8, 128).T], axis=2))
        per_core.append(m)
    return per_core


def kernel(**inputs):
    maps = host_prepare(inputs)
    nc = build_program()
    res = run_bass_kernel_spmd(nc, maps, core_ids=list(range(len(maps))))
    return np.stack([np.asarray(r["y"], np.float32) for r in res.results], axis=0)
```

```python
import contextlib
import math
import numpy as np
import concourse.bass as bass
import concourse.mybir as mybir
from concourse.bass_utils import run_bass_kernel_spmd

F32 = mybir.dt.float32
BF16 = mybir.dt.bfloat16
AF = mybir.ActivationFunctionType
ALU = mybir.AluOpType
AX = mybir.AxisListType

D = 1024
TCX = 256
TL = 4096
T = TCX + TL
NT = T // 128
FF = 2816
NF = FF // 128
EPS = 1e-6
DEPTH = 4
SAME_ENG_SYNC = True


class Buf:
    __slots__ = ("name", "w", "rc", "rd", "excl")

    def __init__(self, name="", excl=False):
        self.name = name
        self.w = None
        self.rc = {}
        self.rd = {}
        self.excl = excl


class Op:
    __slots__ = ("idx", "eng", "fn", "cdeps", "dwaits", "dma", "need_inc", "val")

    def __init__(self, idx, eng, fn):
        self.idx = idx
        self.eng = eng
        self.fn = fn
        self.cdeps = {}
        self.dwaits = {}
        self.dma = None
        self.need_inc = False
        self.val = 0


class Sched:
    ENGS = ("pe", "act", "dve", "pool", "sp")

    def __init__(self, nc, n_dma_sems=64):
        self.nc = nc
        self.ops = []
        self.n_dma_sems = n_dma_sems
        self.dma_cnt = [0] * n_dma_sems
        self.last_op = {e: None for e in self.ENGS}
        self.barrier = {e: None for e in self.ENGS}

    def _dep_on(self, op, w, raw):
        if w is None:
            return
        if w[0] == "c":
            op.cdeps[w[1]] = op.cdeps.get(w[1], False) or raw
        else:
            s = w[1]
            op.dwaits[s] = max(op.dwaits.get(s, 0), self.dma_cnt[s] * 16)

    def add(self, eng, fn, reads=(), writes=(), dma_sem=None):
        op = Op(len(self.ops), eng, fn)
        if self.barrier[eng] is not None:
            cd, dw = self.barrier[eng]
            for i in cd:
                op.cdeps[i] = True
            for s, v in dw.items():
                op.dwaits[s] = max(op.dwaits.get(s, 0), v)
            self.barrier[eng] = None
        rds = [b for b in reads if not b.excl]
        wrs = list(writes) + [b for b in reads if b.excl]
        for b in rds:
            self._dep_on(op, b.w, True)
        for b in wrs:
            self._dep_on(op, b.w, False)
            for e, i in b.rc.items():
                op.cdeps.setdefault(i, False)
            for s in b.rd:
                op.dwaits[s] = max(op.dwaits.get(s, 0), self.dma_cnt[s] * 16)
        if dma_sem is not None:
            self.dma_cnt[dma_sem] += 1
            op.dma = (dma_sem, self.dma_cnt[dma_sem] * 16)
            me = ("d", dma_sem, op.dma[1])
        else:
            me = ("c", op.idx)
        for b in rds:
            if dma_sem is not None:
                b.rd[dma_sem] = op.dma[1]
            else:
                b.rc[eng] = op.idx
        for b in wrs:
            b.w = me
            b.rc = {}
            b.rd = {}
        op.cdeps.pop(op.idx, None)
        self.ops.append(op)
        self.last_op[eng] = op.idx
        return op

    def full_barrier(self):
        cd = set(i for i in self.last_op.values() if i is not None)
        dw = {s: c * 16 for s, c in enumerate(self.dma_cnt) if c}
        for e in self.ENGS:
            self.barrier[e] = (set(cd), dict(dw))

    def emit(self):
        nc = self.nc
        ops = self.ops

        def skip(y, op, raw):
            return y.eng == op.eng and (y.eng == "pe" or not SAME_ENG_SYNC or not raw)

        for op in ops:
            for d, raw in op.cdeps.items():
                y = ops[d]
                if y.dma is None and not skip(y, op, raw):
                    y.need_inc = True
        cnt = {e: 0 for e in self.ENGS}
        for op in ops:
            if op.dma is None and op.need_inc:
                cnt[op.eng] += 1
                op.val = cnt[op.eng]
        self.stats = dict(cnt)
        by_eng = {e: [op for op in ops if op.eng == e] for e in self.ENGS}
        for e in self.ENGS:
            self.stats["n_" + e] = len(by_eng[e])
        with contextlib.ExitStack() as st:
            esem = {e: st.enter_context(nc.semaphore("s_" + e)) for e in self.ENGS}
            dsem = [st.enter_context(nc.semaphore("d%d" % i)) for i in range(self.n_dma_sems)]
            block = st.enter_context(nc.Block())
            final_dw = {s: c * 16 for s, c in enumerate(self.dma_cnt) if c}

            def run(e_name, eng):
                seen = {}
                for op in by_eng[e_name]:
                    waits = {}
                    for d, raw in op.cdeps.items():
                        y = ops[d]
                        if y.dma is not None:
                            k = ("d", y.dma[0])
                            waits[k] = max(waits.get(k, 0), y.dma[1])
                        elif not skip(y, op, raw):
                            k = ("e", y.eng)
                            waits[k] = max(waits.get(k, 0), y.val)
                    for s, v in op.dwaits.items():
                        k = ("d", s)
                        waits[k] = max(waits.get(k, 0), v)
                    for k, v in waits.items():
                        if seen.get(k, 0) >= v:
                            continue
                        seen[k] = v
                        eng.wait_ge(esem[k[1]] if k[0] == "e" else dsem[k[1]], v)
                    ins = op.fn(eng)
                    if op.dma is not None:
                        ins.then_inc(dsem[op.dma[0]], 16)
                    elif op.need_inc:
                        ins.then_inc(esem[e_name], 1)
                if e_name == "sp":
                    for s, v in final_dw.items():
                        eng.wait_ge(dsem[s], v)

            @block.tensor
            def _(eng):
                run("pe", eng)

            @block.scalar
            def _(eng):
                run("act", eng)

            @block.vector
            def _(eng):
                run("dve", eng)

            @block.gpsimd
            def _(eng):
                run("pool", eng)

            @block.sync
            def _(eng):
                run("sp", eng)


class Tl:
    __slots__ = ("ap", "b", "sem")

    def __init__(self, ap, b, sem=None):
        self.ap = ap
        self.b = b
        self.sem = sem


class Ring:
    def __init__(self, tiles):
        self.tiles = tiles
        self.i = 0

    def next(self):
        t = self.tiles[self.i % len(self.tiles)]
        self.i += 1
        return t


ARENA_WORDS = 52224


class KB:
    def __init__(self, nc, S, arena, ps):
        self.nc = nc
        self.S = S
        self.A = arena
        self.AB = arena.bitcast(BF16)
        self.ps = ps
        self.psb = [p.bitcast(BF16) for p in ps]
        self.pb = [Buf("psum%d" % i, excl=True) for i in range(8)]
        self.bank_i = 0
        self.bank_ctr = {}
        self.base = 0
        self.ptr = 0
        self.sem_base = 0
        self.sem_i = 0
        self.psem_i = 0
        self.peak = 0

    def persist(self):
        self.base = self.ptr
        self.sem_base = self.sem_i

    def new_phase(self):
        self.S.full_barrier()
        self.ptr = self.base
        self.sem_i = self.sem_base
        self.psem_i = 0

    def _alloc(self, words):
        words = (words + 15) // 16 * 16
        off = self.ptr
        self.ptr += words
        self.peak = max(self.peak, self.ptr)
        assert self.ptr <= ARENA_WORDS, "SBUF arena overflow %d" % self.ptr
        return off

    N_SP_SEMS = 40

    def sem(self, kind=True):
        if kind == "pool":
            s = self.N_SP_SEMS + self.psem_i
            self.psem_i += 1
            assert s < self.S.n_dma_sems
            return s
        s = self.sem_i
        self.sem_i += 1
        assert s < self.N_SP_SEMS
        return s

    @staticmethod
    def _shape(ap, shape):
        if len(shape) == 2:
            return ap
        if len(shape) == 3:
            return ap.rearrange("p (a b) -> p a b", a=shape[1])
        if len(shape) == 4:
            return ap.rearrange("p (a b c) -> p a b c", a=shape[1], b=shape[2])
        raise ValueError(shape)

    def f32(self, shape, name="", dma=False):
        n = int(np.prod(shape[1:]))
        off = self._alloc(n)
        ap = self._shape(self.A[0:shape[0], off:off + n], shape)
        return Tl(ap, Buf(name), self.sem(dma) if dma else None)

    def bf(self, shape, name="", dma=False):
        n = int(np.prod(shape[1:]))
        off = self._alloc((n + 1) // 2)
        ap = self._shape(self.AB[0:shape[0], 2 * off:2 * off + n], shape)
        return Tl(ap, Buf(name), self.sem(dma) if dma else None)

    def ring(self, kind, shape, n, name="", dma=False):
        mk = self.f32 if kind == "f32" else self.bf
        return Ring([mk(shape, "%s%d" % (name, i), dma) for i in range(n)])

    def bank(self):
        i = self.bank_i
        self.bank_i = (i + 1) % 8
        return i

    def bank_of(self, key, banks):
        c = self.bank_ctr.get(key, 0)
        self.bank_ctr[key] = c + 1
        return banks[c % len(banks)]

    def new_phase_keep(self, keep):
        self.S.full_barrier()
        self.ptr = keep
        self.sem_i = self.sem_base
        self.psem_i = 0

    def dma(self, out_ap, in_ap, reads=(), writes=(), sem=None, q="sp"):
        self.S.add(q, lambda e: e.dma_start(out=out_ap, in_=in_ap), reads=reads, writes=writes, dma_sem=sem)

    def load(self, tl, in_ap, q="sp", sub=None):
        self.dma(tl.ap if sub is None else sub, in_ap, writes=[tl.b], sem=tl.sem, q=q)

    def store(self, out_ap, tl, sub=None, q="sp"):
        self.dma(out_ap, tl.ap if sub is None else sub, reads=[tl.b], sem=tl.sem, q=q)

    def mm(self, bank, out_ap, lhsT, rhs, start, stop, reads, skipchk=False):
        self.S.add("pe", lambda e: e.matmul(out_ap, lhsT=lhsT, rhs=rhs, start=start, stop=stop, skip_group_check=skipchk),
                   reads=reads, writes=[self.pb[bank]])

    def tr(self, bank, out_ap, in_ap, ident, reads):
        self.S.add("pe", lambda e: e.transpose(out=out_ap, in_=in_ap, identity=ident), reads=reads, writes=[self.pb[bank]])

    def act(self, out_ap, in_ap, func, reads, writes, scale=None, bias=None, accum=None):
        kw = {}
        if scale is not None:
            kw["scale"] = scale
        if bias is not None:
            kw["bias"] = bias
        if accum is not None:
            kw["accum_out"] = accum
        self.S.add("act", lambda e: e.activation(out=out_ap, in_=in_ap, func=func, **kw), reads=reads, writes=writes)

    def op(self, eng, fn, reads, writes):
        self.S.add(eng, fn, reads=reads, writes=writes)

    def tt(self, out_ap, in0, in1, op, reads, writes, eng="dve"):
        self.S.add(eng, lambda e: e.tensor_tensor(out=out_ap, in0=in0, in1=in1, op=op), reads=reads, writes=writes)

    def ts(self, out_ap, in0, s1, s2, op0, op1, reads, writes, eng="dve"):
        if s2 is None:
            self.S.add(eng, lambda e: e.tensor_scalar(out=out_ap, in0=in0, scalar1=s1, scalar2=None, op0=op0), reads=reads, writes=writes)
        else:
            self.S.add(eng, lambda e: e.tensor_scalar(out=out_ap, in0=in0, scalar1=s1, scalar2=s2, op0=op0, op1=op1), reads=reads, writes=writes)

    def stt(self, out_ap, in0, scalar, in1, op0, op1, reads, writes, eng="dve"):
        self.S.add(eng, lambda e: e.scalar_tensor_tensor(out=out_ap, in0=in0, scalar=scalar, in1=in1, op0=op0, op1=op1), reads=reads, writes=writes)

    def copy(self, out_ap, in_ap, reads, writes, eng="dve"):
        self.S.add(eng, lambda e: e.tensor_copy(out=out_ap, in_=in_ap), reads=reads, writes=writes)

    def memset(self, tl, val, sub=None, eng="pool"):
        ap = tl.ap if sub is None else sub
        self.S.add(eng, lambda e: e.memset(ap, val), writes=[tl.b])

    def rstd(self, out_tl, ssq_ap, ssq_bufs, n_feat, cst):
        o = out_tl.ap
        self.ts(o, ssq_ap, 1.0 / n_feat, EPS, ALU.mult, ALU.add, reads=ssq_bufs, writes=[out_tl.b])
        mh = cst["mhalf"]
        sh = list(o.shape)
        mhb = mh.ap if len(sh) == 2 and sh[1] == 1 else mh.ap.to_broadcast(sh)
        self.tt(o, o, mhb, ALU.pow, reads=[out_tl.b, mh.b], writes=[out_tl.b], eng="pool")


def prologue(kb, cst, R, src_aps, aT, G, sh, gs_bufs):
    for t, src in enumerate(src_aps):
        hb = R["hbuf"].next()
        kb.load(hb, src)
        jk = R["junk"].next()
        ss = R["ssq"].next()
        kb.act(jk.ap, hb.ap, AF.Square, reads=[hb.b], writes=[jk.b, ss.b], accum=ss.ap)
        rs = R["rs"].next()
        kb.rstd(rs, ss.ap, [ss.b], D, cst)
        nt = R["ntile"].next()
        kb.ts(nt.ap, hb.ap, rs.ap, None, ALU.mult, None, reads=[hb.b, rs.b], writes=[nt.b])
        bk = kb.bank()
        pT = kb.psb[bk][:, 0:1024].rearrange("p (k n) -> p k n", k=8)
        for j in range(8):
            kb.tr(bk, pT[:, j, :], nt.ap[:, j * 128:(j + 1) * 128], cst["identb"].ap, reads=[nt.b, cst["identb"].b])
        for j in range(8):
            kb.act(aT.ap[:, j, t * 128:(t + 1) * 128], pT[:, j, :], AF.Identity, reads=[kb.pb[bk]] + gs_bufs, writes=[aT.b],
                   scale=G[:, j:j + 1], bias=sh[:, j:j + 1])


def prologue_rings(kb, nh=3):
    return {
        "hbuf": kb.ring("f32", [128, 1024], nh, "hbuf", dma=True),
        "junk": kb.ring("bf", [128, 1024], 2, "junk"),
        "ssq": kb.ring("f32", [128, 1], 4, "ssq"),
        "rs": kb.ring("f32", [128, 1], 4, "rs"),
        "ntile": kb.ring("bf", [128, 1024], 2, "ntile"),
    }


def residual_rings(kb):
    return {
        "hres": kb.ring("f32", [128, 1024], 3, "hres", dma=True),
        "rtmp": kb.ring("f32", [128, 512], 2, "rtmp"),
    }


def out_proj_residual(kb, RR, xT, n_k, tile_cols, W, gbc, src_ap, dst_ap):
    hr = RR["hres"].next()
    kb.load(hr, src_ap)
    for nh in range(2):
        bk = kb.bank()
        for k in range(n_k):
            kb.mm(bk, kb.ps[bk][:, :], xT.ap[:, k, tile_cols], W.ap[:, k, nh * 512:(nh + 1) * 512], k == 0, k == n_k - 1,
                  reads=[xT.b, W.b])
        tmp = RR["rtmp"].next()
        kb.tt(tmp.ap, kb.ps[bk][:, :], gbc.ap[:, nh * 512:(nh + 1) * 512], ALU.mult, reads=[kb.pb[bk], gbc.b], writes=[tmp.b])
        hs = hr.ap[:, nh * 512:(nh + 1) * 512]
        kb.tt(hs, hs, tmp.ap, ALU.add, reads=[hr.b, tmp.b], writes=[hr.b], eng="pool")
    kb.store(dst_ap, hr)


def chunks_of(tc, with_ctx=True):
    out = []
    if with_ctx:
        for t0 in range(0, TCX, min(tc, TCX)):
            out.append((1, t0, min(tc, TCX)))
    for t0 in range(TCX, T, tc):
        out.append((0, t0, tc))
    return out


def phase_mods(kb, cst, io, modpp):
    nc = kb.nc
    sT = kb.f32([128, 8, 2], "sT", dma=True)
    kb.load(sT, io["cT"])
    kb.act(sT.ap, sT.ap, AF.Silu, reads=[sT.b], writes=[sT.b])
    sTb = kb.bf([128, 8, 2], "sTb")
    kb.copy(sTb.ap, sT.ap, reads=[sT.b], writes=[sTb.b])
    ng = kb.f32([128, DEPTH, 2, 8], "ng", dma=True)
    kb.load(ng, io["ng"])
    wring = kb.ring("bf", [128, 8, 512], 3, "modw", dma="pool")
    mrows = kb.ring("f32", [2, 6144], 2, "mrow", dma=True)
    brows = kb.ring("f32", [2, 6144], 2, "brow", dma=True)
    for i in range(DEPTH):
        mr = mrows.next()
        br = brows.next()
        kb.load(br, io["modb"][i].partition_broadcast(2))
        for n in range(12):
            w = wring.next()
            kb.load(w, io["modw"][i][:, :, n * 512:(n + 1) * 512], q="pool")
            bk = kb.bank()
            for k in range(8):
                kb.mm(bk, kb.ps[bk][0:2, :], sTb.ap[:, k, :], w.ap[:, k, :], k == 0, k == 7, reads=[sTb.b, w.b])
            kb.tt(mr.ap[0:2, n * 512:(n + 1) * 512], kb.ps[bk][0:2, :], br.ap[0:2, n * 512:(n + 1) * 512], ALU.add,
                  reads=[kb.pb[bk], br.b], writes=[mr.b])
        kb.store(io["MR"][i], mr)
        bk = kb.bank()
        first = True
        for v, c0 in enumerate((0, 1024, 3072, 4096)):
            for j in range(8):
                col = (v * 8 + j) * 2
                kb.mm(bk, kb.ps[bk][:, col:col + 2], mr.ap[0:2, c0 + j * 128:c0 + (j + 1) * 128], cst["identf"].ap[0:2, 0:2],
                      first, True, reads=[mr.b, cst["identf"].b], skipchk=True)
                first = False
        pp = modpp[i]
        kb.copy(pp.ap.rearrange("p v j t -> p (v j t)"), kb.ps[bk][:, 0:64], reads=[kb.pb[bk]], writes=[pp.b])
        for v, which in ((1, 0), (3, 1)):
            sc = pp.ap[:, v, :, :]
            kb.ts(sc, sc, 1.0, None, ALU.add, None, reads=[pp.b], writes=[pp.b])
            kb.tt(sc, sc, ng.ap[:, i, which, :, None].to_broadcast([128, 8, 2]), ALU.mult, reads=[pp.b, ng.b], writes=[pp.b])


def load_gate_bc(kb, io, layer, which):
    c0 = 2048 if which == 0 else 5120
    out = []
    for typ in range(2):
        g = kb.f32([128, 1024], "gbc%d" % typ, dma=True)
        kb.load(g, io["MR"][layer][typ:typ + 1, c0:c0 + 1024].partition_broadcast(128))
        out.append(g)
    return out


def phase_ffn(kb, cst, io, layer, modpp, src, dst, with_ctx, out_lat=None):
    kb.new_phase()
    gbc = load_gate_bc(kb, io, layer, 1)
    Wd = kb.bf([128, NF, 1024], "Wd", dma="pool")
    for q4 in range(2):
        kb.S.add("pool", lambda e, q4=q4: e.dma_start(out=Wd.ap[:, q4 * 11:(q4 + 1) * 11, :], in_=io["wd"][layer][:, q4 * 11:(q4 + 1) * 11, :]),
                 writes=[Wd.b], dma_sem=Wd.sem)
    R = prologue_rings(kb)
    RR = residual_rings(kb)
    TC = 1024
    aTs = kb.ring("bf", [128, 8, TC], 2, "aT")
    hT = kb.bf([128, NF, TC], "hT")
    wgu = kb.ring("bf", [128, 2, 8, 128], 3, "wgu", dma="pool")
    sgr = kb.ring("bf", [128, 512], 2, "sg")
    pp = modpp[layer]
    chs = chunks_of(TC, with_ctx)

    def do_prologue(ch):
        typ, t0, n = ch
        aT = aTs.next()
        prologue(kb, cst, R, [src[t0 + t * 128:t0 + (t + 1) * 128, :] for t in range(n // 128)], aT,
                 pp.ap[:, 3, :, typ], pp.ap[:, 2, :, typ], [pp.b])
        return aT

    aT_next = do_prologue(chs[0])
    for ci, (typ, t0, n) in enumerate(chs):
        ntile = n // 128
        aT = aT_next
        halves = [(c, min(512, n - c)) for c in range(0, n, 512)]
        for f in range(NF):
            w = wgu.next()
            kb.load(w, io["wgu"][layer][f], q="pool")
            for c0, cn in halves:
                bg = kb.bank()
                for k in range(8):
                    kb.mm(bg, kb.ps[bg][:, 0:cn], w.ap[:, 0, k, :], aT.ap[:, k, c0:c0 + cn], k == 0, k == 7, reads=[w.b, aT.b])
                bu = kb.bank()
                for k in range(8):
                    kb.mm(bu, kb.ps[bu][:, 0:cn], w.ap[:, 1, k, :], aT.ap[:, k, c0:c0 + cn], k == 0, k == 7, reads=[w.b, aT.b])
                sg = sgr.next()
                kb.act(sg.ap[:, 0:cn], kb.ps[bg][:, 0:cn], AF.Silu, reads=[kb.pb[bg]], writes=[sg.b])
                kb.tt(hT.ap[:, f, c0:c0 + cn], sg.ap[:, 0:cn], kb.ps[bu][:, 0:cn], ALU.mult, reads=[sg.b, kb.pb[bu]], writes=[hT.b])
        if ci + 1 < len(chs):
            aT_next = do_prologue(chs[ci + 1])
        for t in range(ntile):
            r0 = t0 + t * 128
            if out_lat is not None:
                d_ap = out_lat[r0 - TCX:r0 - TCX + 128, :]
            else:
                d_ap = dst[r0:r0 + 128, :]
            out_proj_residual(kb, RR, hT, NF, slice(t * 128, (t + 1) * 128), Wd, gbc[typ], src[r0:r0 + 128, :], d_ap)


def phase_conv(kb, cst, io, layer, slot, modpp, src, dst, with_ctx):
    kb.new_phase()
    gbc = load_gate_bc(kb, io, layer, 0)
    Win = kb.bf([128, 8, 3072], "cWin", dma="pool")
    for q4 in range(4):
        kb.S.add("pool", lambda e, q4=q4: e.dma_start(out=Win.ap[:, q4 * 2:(q4 + 1) * 2, :], in_=io["cwin"][slot][:, q4 * 2:(q4 + 1) * 2, :]),
                 writes=[Win.b], dma_sem=Win.sem)
    Wout = kb.bf([128, 8, 1024], "cWout", dma="pool")
    kb.load(Wout, io["cwout"][slot], q="pool")
    ckb = kb.f32([128, 8, 4], "ckb", dma=True)
    kb.load(ckb, io["ckb"][slot])
    R = prologue_rings(kb)
    RR = residual_rings(kb)
    TC = 512
    aTs = kb.ring("bf", [128, 8, TC], 3, "aT")
    halos = kb.ring("bf", [128, 8, 2], 2, "ahalo")
    zr = kb.ring("f32", [128, TC + 2], 5, "z")
    csbr = kb.ring("f32", [128, TC], 3, "csb")
    bsbr = kb.ring("f32", [128, TC], 7, "bsb")
    chbr = kb.ring("f32", [128, 2], 3, "chb")
    yr = kb.ring("f32", [128, TC], 5, "ycv")
    ybg = kb.bf([128, 8, TC], "ybg")
    pp = modpp[layer]
    chs = chunks_of(TC, with_ctx)
    seqs = []
    if with_ctx:
        seqs.append([c for c in chs if c[0] == 1])
    seqs.append([c for c in chs if c[0] == 0])

    def do_prologue(ch):
        typ, t0, n = ch
        aT = aTs.next()
        prologue(kb, cst, R, [src[t0 + t * 128:t0 + (t + 1) * 128, :] for t in range(n // 128)], aT,
                 pp.ap[:, 1, :, typ], pp.ap[:, 0, :, typ], [pp.b])
        return aT

    for seq in seqs:
        aT_list = {}
        aT_list[0] = do_prologue(seq[0])
        for ci, ch in enumerate(seq):
            typ, t0, n = ch
            if ci + 1 < len(seq):
                aT_list[ci + 1] = do_prologue(seq[ci + 1])
            aT = aT_list[ci]
            hl = halos.next()
            if ci == 0:
                kb.memset(hl, 0.0, sub=hl.ap[:, :, 0:1])
            else:
                pv = aT_list[ci - 1]
                pn = seq[ci - 1][2]
                kb.copy(hl.ap[:, :, 0:1], pv.ap[:, :, pn - 1:pn], reads=[pv.b], writes=[hl.b], eng="pool")
            if ci + 1 == len(seq):
                kb.memset(hl, 0.0, sub=hl.ap[:, :, 1:2])
            else:
                nx = aT_list[ci + 1]
                kb.copy(hl.ap[:, :, 1:2], nx.ap[:, :, 0:1], reads=[nx.b], writes=[hl.b], eng="pool")
            def j_gen(j, aT=aT, hl=hl, n=n):
                z = zr.next()
                bc = kb.bank()
                for k in range(8):
                    kb.mm(bc, kb.ps[bc][:, 0:n], Win.ap[:, k, 1024 + j * 128:1024 + (j + 1) * 128], aT.ap[:, k, 0:n], k == 0, k == 7, reads=[Win.b, aT.b])
                bu = kb.bank()
                for k in range(8):
                    kb.mm(bu, kb.ps[bu][:, 0:n], Win.ap[:, k, 2048 + j * 128:2048 + (j + 1) * 128], aT.ap[:, k, 0:n], k == 0, k == 7, reads=[Win.b, aT.b])
                bh = kb.bank()
                for k in range(8):
                    kb.mm(bh, kb.ps[bh][:, 0:2], Win.ap[:, k, 1024 + j * 128:1024 + (j + 1) * 128], hl.ap[:, k, :], k == 0, k == 7, reads=[Win.b, hl.b])
                for k in range(8):
                    kb.mm(bh, kb.ps[bh][:, 2:4], Win.ap[:, k, 2048 + j * 128:2048 + (j + 1) * 128], hl.ap[:, k, :], k == 0, k == 7, reads=[Win.b, hl.b], skipchk=True)
                bb = kb.bank()
                for k in range(8):
                    kb.mm(bb, kb.ps[bb][:, 0:n], Win.ap[:, k, j * 128:(j + 1) * 128], aT.ap[:, k, 0:n], k == 0, k == 7, reads=[Win.b, aT.b])
                csb = csbr.next()
                kb.act(csb.ap[:, 0:n], kb.ps[bc][:, 0:n], AF.Copy, reads=[kb.pb[bc]], writes=[csb.b])
                chb = chbr.next()
                kb.act(chb.ap, kb.ps[bh][:, 0:2], AF.Copy, reads=[kb.pb[bh]], writes=[chb.b])
                bsb = bsbr.next()
                kb.act(bsb.ap[:, 0:n], kb.ps[bb][:, 0:n], AF.Copy, reads=[kb.pb[bb]], writes=[bsb.b])
                yield
                kb.tt(z.ap[:, 1:n + 1], csb.ap[:, 0:n], kb.ps[bu][:, 0:n], ALU.mult, reads=[csb.b, kb.pb[bu]], writes=[z.b])
                kb.tt(z.ap[:, 0:1], chb.ap[:, 0:1], kb.ps[bh][:, 2:3], ALU.mult, reads=[chb.b, kb.pb[bh]], writes=[z.b])
                kb.tt(z.ap[:, n + 1:n + 2], chb.ap[:, 1:2], kb.ps[bh][:, 3:4], ALU.mult, reads=[chb.b, kb.pb[bh]], writes=[z.b])
                yield
                y = yr.next()
                kb.ts(y.ap[:, 0:n], z.ap[:, 0:n], ckb.ap[:, j, 0:1], ckb.ap[:, j, 3:4], ALU.mult, ALU.add, reads=[z.b, ckb.b], writes=[y.b])
                yield
                kb.stt(y.ap[:, 0:n], z.ap[:, 1:n + 1], ckb.ap[:, j, 1:2], y.ap[:, 0:n], ALU.mult, ALU.add, reads=[z.b, ckb.b, y.b], writes=[y.b])
                yield
                kb.stt(y.ap[:, 0:n], z.ap[:, 2:n + 2], ckb.ap[:, j, 2:3], y.ap[:, 0:n], ALU.mult, ALU.add, reads=[z.b, ckb.b, y.b], writes=[y.b])
                yield
                kb.tt(ybg.ap[:, j, 0:n], y.ap[:, 0:n], bsb.ap[:, 0:n], ALU.mult, reads=[y.b, bsb.b], writes=[ybg.b])

            run_pipelined([j_gen(j) for j in range(8)], 1)
            for t in range(n // 128):
                r0 = t0 + t * 128
                out_proj_residual(kb, RR, ybg, 8, slice(t * 128, (t + 1) * 128), Wout, gbc[typ], src[r0:r0 + 128, :], dst[r0:r0 + 128, :])


def group_rstd(kb, cst, banks, ncols, gsize, sqr, ssr, rhr):
    sq = sqr.next()
    for i, bk in enumerate(banks):
        kb.act(sq.ap[:, i * ncols:(i + 1) * ncols], kb.ps[bk][:, 0:ncols], AF.Square, reads=[kb.pb[bk]], writes=[sq.b])
    ng = len(banks) * ncols // gsize
    ss = ssr.next()
    kb.op("dve", lambda e: e.tensor_reduce(out=ss.ap[:, 0:ng], in_=sq.ap[:, 0:ng * gsize].rearrange("p (g d) -> p g d", d=gsize),
                                          axis=AX.X, op=ALU.add), reads=[sq.b], writes=[ss.b])
    rh = rhr.next()
    kb.rstd(Tl(rh.ap[:, 0:ng], rh.b), ss.ap[:, 0:ng], [ss.b], gsize, cst)
    return rh


def rope_apply(kb, x, xb, out_ap, out_b, cos, sin, tb, tmpr, lead):
    R = cos.shape[-1]
    t1 = tmpr.next()
    t2 = tmpr.next()
    if lead:
        shp = [128, lead, R]
        cb = cos[:, None, :].to_broadcast(shp)
        t1a = t1.ap[:, 0:lead * R].rearrange("p (h r) -> p h r", h=lead)
        t2a = t2.ap[:, 0:lead * R].rearrange("p (h r) -> p h r", h=lead)
    else:
        cb = cos
        t1a = t1.ap[:, 0:R]
        t2a = t2.ap[:, 0:R]
    kb.tt(t1a, x, cb, ALU.mult, reads=[xb, tb], writes=[t1.b])
    q = R // 4
    for half in range(2):
        if lead:
            xv = x.rearrange("p h (g f i) -> p h g f i", g=2, f=2)[:, :, :, 1 - half, :]
            ov = t2a.rearrange("p h (g f i) -> p h g f i", g=2, f=2)[:, :, :, half, :]
            sv = sin.rearrange("p (g f i) -> p g f i", g=2, f=2)[:, :, half, :][:, None, :, :].to_broadcast([128, lead, 2, q])
        else:
            xv = x.rearrange("p (g f i) -> p g f i", g=2, f=2)[:, :, 1 - half, :]
            ov = t2a.rearrange("p (g f i) -> p g f i", g=2, f=2)[:, :, half, :]
            sv = sin.rearrange("p (g f i) -> p g f i", g=2, f=2)[:, :, half, :]
        kb.tt(ov, xv, sv, ALU.mult, reads=[xb, tb], writes=[t2.b])
    kb.tt(out_ap, t1a, t2a, ALU.add, reads=[t1.b, t2.b], writes=[out_b])


MH = 16


def run_pipelined(gens, interval):
    pending = list(gens)
    active = []
    step = 0
    while pending or active:
        if pending and step % interval == 0:
            active.append(pending.pop(0))
        for g in list(active):
            try:
                next(g)
            except StopIteration:
                active.remove(g)
        step += 1


def attention_core(kb, items, load_head, s_operands, v_operand, ncol, scale, o_banks, s_banks, ptr_, epilogue, PD=2):
    n = len(items)
    sb = [None] * n
    heads = {}
    jobs = {}
    for i in range(n + PD):
        if i < n:
            h, jk, q0, qn, idx, kt, nk = items[i]
            if h not in heads:
                heads[h] = load_head(h)
            lhsT, rhs, rd = s_operands(heads[h], jk, q0, qn, kt)
            bs = kb.bank_of("S", s_banks)
            sb[i] = bs
            kb.mm(bs, kb.ps[bs][:, 0:qn], lhsT, rhs, True, True, reads=rd)
        j = i - PD
        if j < 0:
            continue
        h, jk, q0, qn, idx, kt, nk = items[j]
        nsub = qn // 128
        per_bank = 512 // ncol
        if idx == 0:
            jobs[jk] = [kb.bank_of("O", o_banks) for _ in range((nsub + per_bank - 1) // per_bank)]
        bos = jobs[jk]
        bs = sb[j]
        pt = ptr_.next()
        kb.act(pt.ap[:, 0:qn], kb.ps[bs][:, 0:qn], AF.Exp, reads=[kb.pb[bs]], writes=[pt.b], scale=scale)
        vap, vb = v_operand(h, kt)
        for qs in range(nsub):
            bo = bos[qs // per_bank]
            c0 = (qs % per_bank) * ncol
            kb.mm(bo, kb.ps[bo][:, c0:c0 + ncol], pt.ap[:, qs * 128:(qs + 1) * 128], vap,
                  idx == 0 and qs % per_bank == 0, idx == nk - 1, reads=[pt.b, vb], skipchk=True)
        if idx == nk - 1:
            epilogue(h, jk, q0, qn, bos)
            del jobs[jk]


def phase_mla(kb, cst, io, layer, modpp, src, dst):
    QT = io["mQT"]
    KT = io["mKT"]
    kb.new_phase()
    Vaug = kb.bf([128, NT, MH, 65], "Vaug")
    markV = kb.ptr
    kb.memset(Vaug, 1.0)
    Win = kb.bf([128, 8, 1056], "mWin", dma="pool")
    kb.load(Win, io["mwin"], q="pool")
    Wq = kb.bf([128, 6, 1536], "mWq", dma="pool")
    kb.load(Wq, io["mwq"], q="pool")
    Wkv = kb.bf([128, 2, 2048], "mWkv", dma="pool")
    kb.load(Wkv, io["mwkv"], q="pool")
    gsm = kb.f32([128, 10], "mgsm", dma=True)
    kb.load(gsm, io["mgsm"])
    grope = kb.f32([128, 64], "mgrope", dma=True)
    kb.load(grope, io["mrg"].partition_broadcast(128))
    tab = kb.f32([128, 32, 2, 32], "mrope", dma=True)
    kb.load(tab, io["ropem"])
    R = prologue_rings(kb, 2)
    aTs = kb.ring("bf", [128, 8, 128], 2, "aT")
    ss4r = kb.ring("f32", [128, 4], 2, "ss4")
    r3r = [kb.ring("f32", [128, 1], 2, "r3_%d" % i) for i in range(3)]
    nqr = kb.ring("bf", [128, 768], 2, "nq")
    nkvr = kb.ring("bf", [128, 256], 2, "nkv")
    kper = kb.ring("f32", [128, 32], 2, "kpe")
    kpe2r = kb.ring("f32", [128, 32], 2, "kpe2")
    qlTr = kb.ring("bf", [128, 6, 128], 2, "qlT")
    kvlTr = kb.ring("bf", [128, 2, 128], 2, "kvlT")
    sqr = kb.ring("f32", [128, 1024], 2, "sq")
    ssr = kb.ring("f32", [128, 16], 3, "ssh")
    rhr = kb.ring("f32", [128, 16], 3, "rh")
    qpr = kb.ring("f32", [128, 512], 2, "qp")
    tmpr = kb.ring("f32", [128, 512], 2, "ropetmp")
    Qcr = kb.ring("bf", [128, MH, 96], 2, "Qc")
    Kcr = kb.ring("bf", [128, MH, 96], 2, "Kc")
    QTs = kb.ring("bf", [128, MH, 128], 2, "QTs", dma=True)
    KTs = kb.ring("bf", [128, MH, 128], 2, "KTs", dma=True)
    pp = modpp[layer]
    identb = cst["identb"]

    def tile_gen(gt):
        typ = 1 if gt < 2 else 0
        lt = gt - 2
        r0 = gt * 128
        aT = aTs.next()
        prologue(kb, cst, R, [src[r0:r0 + 128, :]], aT, pp.ap[:, 1, :, typ], pp.ap[:, 0, :, typ], [pp.b])
        yield
        bks = []
        for c0, cn in ((0, 512), (512, 256), (768, 288)):
            bk = kb.bank()
            bks.append(bk)
            for k in range(8):
                kb.mm(bk, kb.ps[bk][:, 0:cn], aT.ap[:, k, :], Win.ap[:, k, c0:c0 + cn], k == 0, k == 7, reads=[aT.b, Win.b])
        b0, b1, b2 = bks
        jk = R["junk"].next()
        ss = ss4r.next()
        kb.act(jk.ap[:, 0:512], kb.ps[b0][:, 0:512], AF.Square, reads=[kb.pb[b0]], writes=[jk.b, ss.b], accum=ss.ap[:, 0:1])
        kb.act(jk.ap[:, 0:256], kb.ps[b1][:, 0:256], AF.Square, reads=[kb.pb[b1]], writes=[jk.b, ss.b], accum=ss.ap[:, 1:2])
        kb.act(jk.ap[:, 0:256], kb.ps[b2][:, 0:256], AF.Square, reads=[kb.pb[b2]], writes=[jk.b, ss.b], accum=ss.ap[:, 2:3])
        kb.act(jk.ap[:, 0:32], kb.ps[b2][:, 256:288], AF.Square, reads=[kb.pb[b2]], writes=[jk.b, ss.b], accum=ss.ap[:, 3:4])
        kb.tt(ss.ap[:, 0:1], ss.ap[:, 0:1], ss.ap[:, 1:2], ALU.add, reads=[ss.b], writes=[ss.b])
        rq, rkv, rkp = [r.next() for r in r3r]
        kb.rstd(rq, ss.ap[:, 0:1], [ss.b], 768, cst)
        kb.rstd(rkv, ss.ap[:, 2:3], [ss.b], 256, cst)
        kb.rstd(rkp, ss.ap[:, 3:4], [ss.b], 32, cst)
        nq = nqr.next()
        kb.ts(nq.ap[:, 0:512], kb.ps[b0][:, 0:512], rq.ap, None, ALU.mult, None, reads=[kb.pb[b0], rq.b], writes=[nq.b])
        kb.ts(nq.ap[:, 512:768], kb.ps[b1][:, 0:256], rq.ap, None, ALU.mult, None, reads=[kb.pb[b1], rq.b], writes=[nq.b])
        nkv = nkvr.next()
        kb.ts(nkv.ap, kb.ps[b2][:, 0:256], rkv.ap, None, ALU.mult, None, reads=[kb.pb[b2], rkv.b], writes=[nkv.b])
        kpe = kper.next()
        kb.stt(kpe.ap, kb.ps[b2][:, 256:288], rkp.ap, grope.ap[:, 32:64], ALU.mult, ALU.mult, reads=[kb.pb[b2], rkp.b, grope.b], writes=[kpe.b])
        if typ == 0:
            kpe2 = kpe2r.next()
            rope_apply(kb, kpe.ap, kpe.b, kpe2.ap, kpe2.b, tab.ap[:, lt, 0, :], tab.ap[:, lt, 1, :], tab.b, tmpr, 0)
            kpe = kpe2
        yield
        bt = kb.bank()
        pT = kb.psb[bt][:, 0:1024].rearrange("p (k n) -> p k n", k=8)
        for j in range(6):
            kb.tr(bt, pT[:, j, :], nq.ap[:, j * 128:(j + 1) * 128], identb.ap, reads=[nq.b, identb.b])
        for j in range(2):
            kb.tr(bt, pT[:, 6 + j, :], nkv.ap[:, j * 128:(j + 1) * 128], identb.ap, reads=[nkv.b, identb.b])
        qlT = qlTr.next()
        kvlT = kvlTr.next()
        for j in range(6):
            kb.act(qlT.ap[:, j, :], pT[:, j, :], AF.Copy, reads=[kb.pb[bt], gsm.b], writes=[qlT.b], scale=gsm.ap[:, j:j + 1])
        for j in range(2):
            kb.act(kvlT.ap[:, j, :], pT[:, 6 + j, :], AF.Copy, reads=[kb.pb[bt], gsm.b], writes=[kvlT.b], scale=gsm.ap[:, 6 + j:7 + j])
        yield
        Qc = Qcr.next()
        Kc = Kcr.next()
        bq = []
        for nn in range(2):
            bk = kb.bank()
            bq.append(bk)
            for k in range(6):
                kb.mm(bk, kb.ps[bk][:, :], qlT.ap[:, k, :], Wq.ap[:, k, nn * 512:(nn + 1) * 512], k == 0, k == 5, reads=[qlT.b, Wq.b])
        rh = group_rstd(kb, cst, bq, 512, 64, sqr, ssr, rhr)
        for nn in range(2):
            kb.tt(Qc.ap[:, nn * 8:(nn + 1) * 8, 0:64], kb.ps[bq[nn]][:, :].rearrange("p (h d) -> p h d", d=64),
                  rh.ap[:, nn * 8:(nn + 1) * 8, None].to_broadcast([128, 8, 64]), ALU.mult, reads=[kb.pb[bq[nn]], rh.b], writes=[Qc.b])
        bk = kb.bank()
        for k in range(6):
            kb.mm(bk, kb.ps[bk][:, :], qlT.ap[:, k, :], Wq.ap[:, k, 1024:1536], k == 0, k == 5, reads=[qlT.b, Wq.b])
        rh = group_rstd(kb, cst, [bk], 512, 32, sqr, ssr, rhr)
        qp = qpr.next()
        qp3 = qp.ap.rearrange("p (h r) -> p h r", r=32)
        kb.tt(qp3, kb.ps[bk][:, :].rearrange("p (h r) -> p h r", r=32), rh.ap[:, 0:16, None].to_broadcast([128, 16, 32]), ALU.mult,
              reads=[kb.pb[bk], rh.b], writes=[qp.b])
        kb.tt(qp3, qp3, grope.ap[:, None, 0:32].to_broadcast([128, 16, 32]), ALU.mult, reads=[qp.b, grope.b], writes=[qp.b], eng="pool")
        if typ == 0:
            rope_apply(kb, qp3, qp.b, Qc.ap[:, :, 64:96], Qc.b, tab.ap[:, lt, 0, :], tab.ap[:, lt, 1, :], tab.b, tmpr, 16)
        else:
            kb.copy(Qc.ap[:, :, 64:96], qp3, reads=[qp.b], writes=[Qc.b])
        yield
        bkk = []
        for nn in range(2):
            bk = kb.bank()
            bkk.append(bk)
            for k in range(2):
                kb.mm(bk, kb.ps[bk][:, :], kvlT.ap[:, k, :], Wkv.ap[:, k, nn * 512:(nn + 1) * 512], k == 0, k == 1, reads=[kvlT.b, Wkv.b])
        rh = group_rstd(kb, cst, bkk, 512, 64, sqr, ssr, rhr)
        for nn in range(2):
            kb.tt(Kc.ap[:, nn * 8:(nn + 1) * 8, 0:64], kb.ps[bkk[nn]][:, :].rearrange("p (h d) -> p h d", d=64),
                  rh.ap[:, nn * 8:(nn + 1) * 8, None].to_broadcast([128, 8, 64]), ALU.mult, reads=[kb.pb[bkk[nn]], rh.b], writes=[Kc.b])
        kb.copy(Kc.ap[:, :, 64:96], kpe.ap[:, None, :].to_broadcast([128, 16, 32]), reads=[kpe.b], writes=[Kc.b], eng="pool")
        for nn in range(2):
            bk = kb.bank()
            for k in range(2):
                kb.mm(bk, kb.ps[bk][:, :], kvlT.ap[:, k, :], Wkv.ap[:, k, 1024 + nn * 512:1024 + (nn + 1) * 512], k == 0, k == 1, reads=[kvlT.b, Wkv.b])
            kb.act(Vaug.ap[:, gt, nn * 8:(nn + 1) * 8, 0:64], kb.ps[bk][:, :].rearrange("p (h d) -> p h d", d=64), AF.Copy,
                   reads=[kb.pb[bk]], writes=[Vaug.b])
        yield
        qts = QTs.next()
        kts = KTs.next()
        for src_t, stg, gcol in ((Qc, qts, 8), (Kc, kts, 9)):
            for hh in range(2):
                bk = kb.bank()
                pv = kb.psb[bk][:, 0:1024].rearrange("p (h n) -> p h n", h=8)
                for h8 in range(8):
                    kb.tr(bk, pv[0:96, h8, :], src_t.ap[:, hh * 8 + h8, :], identb.ap, reads=[src_t.b, identb.b])
                kb.act(stg.ap[0:96, hh * 8:(hh + 1) * 8, :], pv[0:96, :, :], AF.Copy, reads=[kb.pb[bk], gsm.b], writes=[stg.b],
                       scale=gsm.ap[0:96, gcol:gcol + 1])
        kb.store(QT[:, :, r0:r0 + 128].rearrange("h p t -> p h t"), qts, sub=qts.ap[0:96, :, :])
        kb.store(KT[:, :, r0:r0 + 128].rearrange("h p t -> p h t"), kts, sub=kts.ap[0:96, :, :])

    run_pipelined([tile_gen(gt) for gt in range(NT)], 2)

    kb.new_phase_keep(markV)
    Osb = kb.bf([128, NT, 1024], "Osb")
    markO = kb.ptr
    KTr = kb.ring("bf", [128, T], 2, "KTh", dma=True)
    QTr = kb.ring("bf", [128, T], 2, "QTh", dma=True)
    ptr_ = kb.ring("bf", [128, 512], 4, "PT")
    recr = kb.ring("f32", [128, 4], 2, "rec")
    items = []
    for h in range(MH):
        jobs = [(0, 256, [0, 1])] + [(TCX + c * 512, 512, list(range(NT))) for c in range(TL // 512)]
        for q0, qn, kts_ in jobs:
            for idx, kt in enumerate(kts_):
                items.append((h, (h, q0), q0, qn, idx, kt, len(kts_)))

    def load_head(h):
        kth = KTr.next()
        qth = QTr.next()
        kb.load(kth, KT[h], sub=kth.ap[0:96, :])
        kb.load(qth, QT[h], sub=qth.ap[0:96, :])
        return kth, qth

    def s_operands(hd, jk, q0, qn, kt):
        kth, qth = hd
        return kth.ap[0:96, kt * 128:(kt + 1) * 128], qth.ap[0:96, q0:q0 + qn], [kth.b, qth.b]

    def v_operand(h, kt):
        return Vaug.ap[:, kt, h, :], Vaug.b

    def epilogue(h, jk, q0, qn, bos):
        bo = bos[0]
        nsub = qn // 128
        rec = recr.next()
        ov = kb.ps[bo][:, 0:nsub * 65].rearrange("p (s c) -> p s c", c=65)
        kb.op("dve", lambda e: e.reciprocal(out=rec.ap[:, 0:nsub], in_=ov[:, :, 64]), reads=[kb.pb[bo]], writes=[rec.b])
        kb.tt(Osb.ap[:, q0 // 128:q0 // 128 + nsub, h * 64:(h + 1) * 64], ov[:, :, 0:64],
              rec.ap[:, 0:nsub, None].to_broadcast([128, nsub, 64]), ALU.mult, reads=[kb.pb[bo], rec.b], writes=[Osb.b])

    attention_core(kb, items, load_head, s_operands, v_operand, 65, 96.0 ** -0.5, [0, 1], [2, 3, 4, 5, 6, 7], ptr_, epilogue)

    kb.new_phase_keep(markO)
    attn_out_phase(kb, cst, io, layer, Osb, io["mwo"], None, src, dst)


def attn_out_phase(kb, cst, io, layer, Osb, w_dram, pre, src, dst):
    gbc = load_gate_bc(kb, io, layer, 0)
    Wo = kb.bf([128, 8, 1024], "Wo", dma="pool")
    kb.load(Wo, w_dram, q="pool")
    RR = residual_rings(kb)
    OTr = kb.ring("bf", [128, 8, 128], 2, "OT")
    identb = cst["identb"]
    for t in range(NT):
        typ = 1 if t < 2 else 0
        if pre is not None:
            o_ap, o_b, scl = pre(t)
        else:
            o_ap, o_b, scl = Osb.ap[:, t, :], Osb.b, None
        bk = kb.bank()
        pT = kb.psb[bk][:, 0:1024].rearrange("p (k n) -> p k n", k=8)
        for j in range(8):
            kb.tr(bk, pT[:, j, :], o_ap[:, j * 128:(j + 1) * 128], identb.ap, reads=[o_b, identb.b])
        OT = OTr.next()
        if scl is None:
            kb.copy(OT.ap, pT, reads=[kb.pb[bk]], writes=[OT.b])
        else:
            kb.act(OT.ap, pT, AF.Copy, reads=[kb.pb[bk], scl[1]], writes=[OT.b], scale=scl[0])
        r0 = t * 128
        out_proj_residual(kb, RR, OT, 8, slice(0, 128), Wo, gbc[typ], src[r0:r0 + 128, :], dst[r0:r0 + 128, :])


DH = 8


def phase_diff(kb, cst, io, layer, modpp, src, dst):
    lam_init = 0.8 - 0.6 * math.exp(-0.3 * layer)
    QT = io["dQT"]
    KT = io["dKT"]
    kb.new_phase()
    Vaug = kb.bf([128, NT, DH, 129], "dVaug")
    markV = kb.ptr
    kb.memset(Vaug, 1.0)
    Win = kb.bf([128, 8, 3072], "dWin", dma="pool")
    for q4 in range(4):
        kb.S.add("pool", lambda e, q4=q4: e.dma_start(out=Win.ap[:, q4 * 2:(q4 + 1) * 2, :], in_=io["dwin"][:, q4 * 2:(q4 + 1) * 2, :]),
                 writes=[Win.b], dma_sem=Win.sem)
    gn = kb.f32([128, 128], "dgn", dma=True)
    kb.load(gn, io["dgn"].partition_broadcast(128))
    tab = kb.f32([128, 32, 2, 64], "drope", dma=True)
    kb.load(tab, io["roped"])
    R = prologue_rings(kb, 2)
    aTs = kb.ring("bf", [128, 8, 128], 3, "aT")
    sqr = kb.ring("f32", [128, 1024], 1, "sq")
    ssr = kb.ring("f32", [128, 16], 3, "ssh")
    rhr = kb.ring("f32", [128, 16], 3, "rh")
    qnr = kb.ring("f32", [128, 1024], 2, "qn")
    tmpr = kb.ring("f32", [128, 1024], 2, "ropetmp")
    QKc = kb.ring("bf", [128, 1024], 2, "QKc")
    QTs = kb.ring("bf", [128, DH, 2, 128], 2, "dQTs", dma=True)
    KTs = kb.ring("bf", [128, DH, 128], 2, "dKTs", dma=True)
    for tl in QTs.tiles:
        kb.memset(tl, 0.0)
    pp = modpp[layer]
    identb = cst["identb"]

    def tile_gen(gt):
        typ = 1 if gt < 2 else 0
        lt = gt - 2
        r0 = gt * 128
        aT = aTs.next()
        prologue(kb, cst, R, [src[r0:r0 + 128, :]], aT, pp.ap[:, 1, :, typ], pp.ap[:, 0, :, typ], [pp.b])
        yield
        qts = QTs.next()
        kts = KTs.next()
        for which in range(2):
            bks = []
            for nn in range(2):
                bk = kb.bank()
                bks.append(bk)
                c0 = which * 1024 + nn * 512
                for k in range(8):
                    kb.mm(bk, kb.ps[bk][:, :], aT.ap[:, k, :], Win.ap[:, k, c0:c0 + 512], k == 0, k == 7, reads=[aT.b, Win.b])
            rh = group_rstd(kb, cst, bks, 512, 64, sqr, ssr, rhr)
            qn = qnr.next()
            qn3 = qn.ap.rearrange("p (g d) -> p g d", d=64)
            for nn in range(2):
                kb.tt(qn3[:, nn * 8:(nn + 1) * 8, :], kb.ps[bks[nn]][:, :].rearrange("p (g d) -> p g d", d=64),
                      rh.ap[:, nn * 8:(nn + 1) * 8, None].to_broadcast([128, 8, 64]), ALU.mult, reads=[kb.pb[bks[nn]], rh.b], writes=[qn.b])
            kb.tt(qn3, qn3, gn.ap[:, None, which * 64:(which + 1) * 64].to_broadcast([128, 16, 64]), ALU.mult,
                  reads=[qn.b, gn.b], writes=[qn.b], eng="pool")
            yield
            qc = QKc.next()
            qc3 = qc.ap.rearrange("p (g d) -> p g d", d=64)
            if typ == 0:
                rope_apply(kb, qn3, qn.b, qc3, qc.b, tab.ap[:, lt, 0, :], tab.ap[:, lt, 1, :], tab.b, tmpr, 16)
            else:
                kb.copy(qc3, qn3, reads=[qn.b], writes=[qc.b])
            bk = kb.bank()
            pT = kb.psb[bk][:, 0:1024].rearrange("p (h n) -> p h n", h=8)
            for h in range(DH):
                kb.tr(bk, pT[:, h, :], qc.ap[:, h * 128:(h + 1) * 128], identb.ap, reads=[qc.b, identb.b])
            if which == 0:
                kb.op("act", lambda e, pT=pT: e.activation(out=qts.ap[0:64, :, 0, :], in_=pT[0:64, :, :], func=AF.Copy),
                      reads=[kb.pb[bk]], writes=[qts.b])
                kb.copy(qts.ap[64:128, :, 1, :], pT[64:128, :, :], reads=[kb.pb[bk]], writes=[qts.b])
            else:
                kb.op("act", lambda e, pT=pT: e.activation(out=kts.ap, in_=pT, func=AF.Copy), reads=[kb.pb[bk]], writes=[kts.b])
            yield
        for nn in range(2):
            bk = kb.bank()
            c0 = 2048 + nn * 512
            for k in range(8):
                kb.mm(bk, kb.ps[bk][:, :], aT.ap[:, k, :], Win.ap[:, k, c0:c0 + 512], k == 0, k == 7, reads=[aT.b, Win.b])
            kb.copy(Vaug.ap[:, gt, nn * 4:(nn + 1) * 4, 0:128], kb.ps[bk][:, :].rearrange("p (h d) -> p h d", d=128),
                    reads=[kb.pb[bk]], writes=[Vaug.b])
        kb.store(QT[:, :, :, r0:r0 + 128].rearrange("h v p t -> p h v t"), qts)
        kb.store(KT[:, :, r0:r0 + 128].rearrange("h p t -> p h t"), kts)

    run_pipelined([tile_gen(gt) for gt in range(NT)], 2)

    kb.new_phase_keep(markV)
    Osb = kb.bf([128, NT, 1024], "dOsb")
    markO = kb.ptr
    lrow = kb.f32([1, 256], "lrow", dma=True)
    kb.load(lrow, io["dlam"])
    lpr = kb.f32([1, 128], "lprod")
    lv = lrow.ap.rearrange("p (a b d) -> p a b d", a=2, b=2)
    kb.tt(lpr.ap.rearrange("p (a d) -> p a d", a=2), lv[:, :, 0, :], lv[:, :, 1, :], ALU.mult, reads=[lrow.b], writes=[lpr.b])
    lsum = kb.f32([1, 2], "lsum")
    kb.op("dve", lambda e: e.tensor_reduce(out=lsum.ap, in_=lpr.ap.rearrange("p (a d) -> p a d", a=2), axis=AX.X, op=ALU.add),
          reads=[lpr.b], writes=[lsum.b])
    kb.act(lsum.ap, lsum.ap, AF.Exp, reads=[lsum.b], writes=[lsum.b])
    nl1 = kb.f32([1, 2], "nl1")
    kb.tt(nl1.ap[:, 0:1], lsum.ap[:, 1:2], lsum.ap[:, 0:1], ALU.subtract, reads=[lsum.b], writes=[nl1.b])
    kb.ts(nl1.ap[:, 0:1], nl1.ap[:, 0:1], -lam_init, None, ALU.add, None, reads=[nl1.b], writes=[nl1.b])
    kb.copy(nl1.ap[:, 1:2], nl1.ap[:, 0:1], reads=[nl1.b], writes=[nl1.b])
    bk = kb.bank()
    kb.mm(bk, kb.ps[bk][:, 0:2], cst["ones"].ap[0:1, 0:128], nl1.ap[0:1, 0:2], True, True, reads=[cst["ones"].b, nl1.b])
    neglam = kb.f32([128, 2], "neglam")
    kb.copy(neglam.ap, kb.ps[bk][:, 0:2], reads=[kb.pb[bk]], writes=[neglam.b])

    KTr = kb.ring("bf", [128, T], 2, "dKTh", dma=True)
    QTr = kb.ring("bf", [128, 2, T], 2, "dQTh", dma=True)
    ptr_ = kb.ring("bf", [128, 512], 4, "PT")
    recr = kb.ring("f32", [128, 4], 2, "rec")
    o1r = kb.ring("f32", [128, 4, 128], 2, "O1n")
    o2r = kb.ring("f32", [128, 3, 128], 2, "O2n")
    items = []
    for h in range(DH):
        jobs = [(0, 256, [0, 1])] + [(TCX + c * 512, 512, list(range(NT))) for c in range(TL // 512)]
        for q0, qn_, kts_ in jobs:
            for sv in range(2):
                for idx, kt in enumerate(kts_):
                    items.append((h, (h, q0, sv), q0, qn_, idx, kt, len(kts_)))

    def load_head(h):
        kth = KTr.next()
        qth = QTr.next()
        kb.load(kth, KT[h])
        kb.load(qth, QT[h].rearrange("v p t -> p v t"))
        return kth, qth

    def s_operands(hd, jk, q0, qn, kt):
        kth, qth = hd
        return kth.ap[:, kt * 128:(kt + 1) * 128], qth.ap[:, jk[2], q0:q0 + qn], [kth.b, qth.b]

    def v_operand(h, kt):
        return Vaug.ap[:, kt, h, :], Vaug.b

    o1s = {}

    def epilogue(h, jk, q0, qn, bos):
        sv = jk[2]
        nsub = qn // 128
        if sv == 0:
            o1s[(h, q0)] = o1r.next()
        o1 = o1s[(h, q0)]
        rec = recr.next()
        for b2, bo in enumerate(bos):
            s0 = b2 * 3
            ns = min(3, nsub - s0)
            ov = kb.ps[bo][:, 0:ns * 129].rearrange("p (s c) -> p s c", c=129)
            rsl = rec.ap[:, s0:s0 + ns]
            kb.op("dve", lambda e, rsl=rsl, ov=ov: e.reciprocal(out=rsl, in_=ov[:, :, 128]), reads=[kb.pb[bo]], writes=[rec.b])
            if sv == 0:
                kb.tt(o1.ap[:, s0:s0 + ns, :], ov[:, :, 0:128], rsl[:, :, None].to_broadcast([128, ns, 128]), ALU.mult,
                      reads=[kb.pb[bo], rec.b], writes=[o1.b])
            else:
                o2 = o2r.next()
                kb.tt(o2.ap[:, 0:ns, :], ov[:, :, 0:128], rsl[:, :, None].to_broadcast([128, ns, 128]), ALU.mult,
                      reads=[kb.pb[bo], rec.b], writes=[o2.b])
                tl0 = q0 // 128 + s0
                kb.stt(Osb.ap[:, tl0:tl0 + ns, h * 128:(h + 1) * 128], o2.ap[:, 0:ns, :], neglam.ap[:, 0:1], o1.ap[:, s0:s0 + ns, :],
                       ALU.mult, ALU.add, reads=[o2.b, neglam.b, o1.b], writes=[Osb.b])
        if sv == 1:
            del o1s[(h, q0)]

    attention_core(kb, items, load_head, s_operands, v_operand, 129, 64.0 ** -0.5, [0, 1, 2, 3], [4, 5, 6, 7], ptr_, epilogue)

    kb.new_phase_keep(markO)
    dsub = kb.f32([128, 1], "dsub", dma=True)
    kb.load(dsub, io["dsub"])
    sqr = kb.ring("f32", [128, 1024], 2, "sq")
    ssr = kb.ring("f32", [128, 8], 2, "ssh")
    rhr = kb.ring("f32", [128, 8], 2, "rh")
    onr = kb.ring("bf", [128, 1024], 2, "On")

    def pre(t):
        sq = sqr.next()
        kb.act(sq.ap, Osb.ap[:, t, :], AF.Square, reads=[Osb.b], writes=[sq.b])
        ss = ssr.next()
        kb.op("dve", lambda e: e.tensor_reduce(out=ss.ap, in_=sq.ap.rearrange("p (h d) -> p h d", d=128), axis=AX.X, op=ALU.add),
              reads=[sq.b], writes=[ss.b])
        rh = rhr.next()
        kb.rstd(rh, ss.ap, [ss.b], 128, cst)
        on = onr.next()
        kb.stt(on.ap.rearrange("p (h d) -> p h d", d=128), Osb.ap[:, t, :].rearrange("p (h d) -> p h d", d=128), 1.0 - lam_init,
               rh.ap[:, :, None].to_broadcast([128, 8, 128]), ALU.mult, ALU.mult, reads=[Osb.b, rh.b], writes=[on.b])
        return on.ap, on.b, (dsub.ap[:, 0:1], dsub.b)

    attn_out_phase(kb, cst, io, layer, Osb, io["dwo"], pre, src, dst)

def build_program(n_layers=DEPTH, dbg=False, stop_after=None):
    nc = bass.Bass("TRN2", target_bir_lowering=False)
    io = {}

    def inp(name, shape):
        io[name] = nc.dram_tensor(name, list(shape), F32, kind="ExternalInput").ap()

    inp("xin", [T, D])
    inp("cT", [128, 8, 2])
    inp("ng", [128, DEPTH, 2, 8])
    inp("consts", [128, 256])
    inp("modw", [DEPTH, 128, 8, 6144])
    inp("modb", [DEPTH, 1, 6144])
    inp("wgu", [DEPTH, NF, 128, 2, 8, 128])
    inp("wd", [DEPTH, 128, NF, 1024])
    inp("cwin", [2, 128, 8, 3072])
    inp("cwout", [2, 128, 8, 1024])
    inp("ckb", [2, 128, 8, 4])
    inp("mwin", [128, 8, 1056])
    inp("mwq", [128, 6, 1536])
    inp("mwkv", [128, 2, 2048])
    inp("mwo", [128, 8, 1024])
    inp("mgsm", [128, 10])
    inp("mrg", [1, 64])
    inp("ropem", [128, 32, 2, 32])
    inp("dwin", [128, 8, 3072])
    inp("dwo", [128, 8, 1024])
    inp("dgn", [1, 128])
    inp("dsub", [128, 1])
    inp("dlam", [1, 256])
    inp("roped", [128, 32, 2, 64])
    io["dQT"] = nc.dram_tensor("dQT", [DH, 2, 128, T], BF16).ap()
    io["dKT"] = nc.dram_tensor("dKT", [DH, 128, T], BF16).ap()
    io["mQT"] = nc.dram_tensor("mQT", [MH, 96, T], BF16).ap()
    io["mKT"] = nc.dram_tensor("mKT", [MH, 96, T], BF16).ap()
    io["y"] = nc.dram_tensor("y", [TL, D], F32, kind="ExternalOutput").ap()
    hkind = "ExternalOutput" if dbg else "Internal"
    io["Ha"] = nc.dram_tensor("Ha", [T, D], F32, kind=hkind).ap()
    io["Hb"] = nc.dram_tensor("Hb", [T, D], F32, kind=hkind).ap()
    io["MR"] = nc.dram_tensor("MR", [DEPTH, 2, 6144], F32, kind=hkind).ap()

    S = Sched(nc)
    with contextlib.ExitStack() as st:
        arena = st.enter_context(nc.sbuf_tensor("arena", [128, ARENA_WORDS], F32))
        ps = [st.enter_context(nc.psum_tensor("ps%d" % i, [128, 512], F32)) for i in range(8)]
        kb = KB(nc, S, arena, ps)
        cst = {}
        cf = kb.f32([128, 256], "consts", dma=True)
        kb.load(cf, io["consts"])
        cst["identf"] = Tl(cf.ap[:, 0:128], cf.b)
        cst["ones"] = Tl(cf.ap[:, 128:256], cf.b)
        cst["identb"] = kb.bf([128, 128], "identb")
        kb.copy(cst["identb"].ap, cf.ap[:, 0:128], reads=[cf.b], writes=[cst["identb"].b])
        cst["mhalf"] = kb.f32([128, 1], "mhalf")
        kb.memset(cst["mhalf"], -0.5)
        modpp = [kb.f32([128, 4, 8, 2], "modpp%d" % i) for i in range(DEPTH)]
        kb.persist()

        phase_mods(kb, cst, io, modpp)
        cur, oth = io["xin"], io["Ha"]

        def swap(cur, oth):
            nxt = io["Hb"] if oth is io["Ha"] else io["Ha"]
            return oth, nxt

        done = False
        for i in range(n_layers):
            kind, slot = i % 3, i // 3
            last = i == DEPTH - 1
            if kind == 0:
                phase_conv(kb, cst, io, i, slot, modpp, cur, oth, with_ctx=not last)
                cur, oth = swap(cur, oth)
            elif kind == 1:
                phase_mla(kb, cst, io, i, modpp, cur, oth)
                cur, oth = swap(cur, oth)
            else:
                phase_diff(kb, cst, io, i, modpp, cur, oth)
                cur, oth = swap(cur, oth)
            if stop_after == ("mix", i):
                break
            if last:
                phase_ffn(kb, cst, io, i, modpp, cur, oth, with_ctx=False, out_lat=io["y"])
            else:
                phase_ffn(kb, cst, io, i, modpp, cur, oth, with_ctx=True)
                cur, oth = swap(cur, oth)
            if stop_after == ("ffn", i):
                break
        S.emit()
        build_program.info = dict(S.stats, sbuf_peak_words=kb.peak, n_ops=len(S.ops))
    return nc


def pp_layout(v, nchunk):
    v = np.asarray(v, np.float32)
    lead = v.shape[:-1]
    r = v.reshape(lead + (nchunk, 128))
    r = np.moveaxis(r, -1, 0)
    return np.ascontiguousarray(r)


def kmajor(w):
    K, N = w.shape
    return np.ascontiguousarray(w.reshape(K // 128, 128, N).transpose(1, 0, 2))


def rope_table(rot_dim):
    rows = TL // 64
    axis_dim = rot_dim // 2
    inv_freq = (np.float32(10000.0) ** (-np.arange(0, axis_dim, 2, dtype=np.float32) / np.float32(axis_dim))).astype(np.float32)
    row = np.repeat(np.arange(rows, dtype=np.float32), 64)
    col = np.tile(np.arange(64, dtype=np.float32), rows)
    ang_r = row[:, None] * inv_freq
    ang_c = col[:, None] * inv_freq
    ang = np.concatenate([ang_r, ang_r, ang_c, ang_c], axis=-1).astype(np.float32)
    cos = np.cos(ang).astype(np.float32)
    sin = np.sin(ang).astype(np.float32)
    hq = rot_dim // 4
    sign = np.concatenate([-np.ones(hq), np.ones(hq), -np.ones(hq), np.ones(hq)]).astype(np.float32)
    tab = np.stack([cos, sin * sign], axis=1)
    return np.ascontiguousarray(tab.reshape(32, 128, 2, rot_dim).transpose(1, 0, 2, 3))


def host_prepare(inputs):
    f = lambda a: np.asarray(a, np.float32)
    sh = {}
    consts = np.zeros((128, 256), np.float32)
    consts[:, 0:128] = np.eye(128, dtype=np.float32)
    consts[:, 128:256] = 1.0
    sh["consts"] = consts
    sh["ng"] = np.ascontiguousarray(np.stack([pp_layout(f(inputs["norm_mix_g"]), 8), pp_layout(f(inputs["norm_ffn_g"]), 8)], axis=2))
    sh["modw"] = np.ascontiguousarray(f(inputs["mod_w"]).reshape(DEPTH, 8, 128, 6144).transpose(0, 2, 1, 3))
    sh["modb"] = np.ascontiguousarray(f(inputs["mod_b"]).reshape(DEPTH, 1, 6144))
    wg = f(inputs["ffn_w_gate"]).reshape(DEPTH, 8, 128, NF, 128)
    wu = f(inputs["ffn_w_up"]).reshape(DEPTH, 8, 128, NF, 128)
    wgu = np.stack([wg, wu], axis=0)
    sh["wgu"] = np.ascontiguousarray(wgu.transpose(1, 4, 3, 0, 2, 5))
    sh["wd"] = np.ascontiguousarray(f(inputs["ffn_w_down"]).reshape(DEPTH, NF, 128, D).transpose(0, 2, 1, 3))
    sh["cwin"] = np.stack([kmajor(w) for w in f(inputs["conv_w_in"])])
    sh["cwout"] = np.stack([kmajor(w) for w in f(inputs["conv_w_out"])])
    ck = f(inputs["conv_k"])
    cb = f(inputs["conv_b"])
    ckb = np.concatenate([ck, cb[:, None, :]], axis=1)
    sh["ckb"] = np.ascontiguousarray(ckb.reshape(2, 4, 8, 128).transpose(0, 3, 2, 1))
    sh["mwin"] = kmajor(f(inputs["mla_w_in"])[0])
    permq = [h * 96 + d for h in range(16) for d in range(64)] + [h * 96 + 64 + r for h in range(16) for r in range(32)]
    sh["mwq"] = kmajor(np.ascontiguousarray(f(inputs["mla_w_q_up"])[0][:, permq]))
    permkv = [h * 128 + d for h in range(16) for d in range(64)] + [h * 128 + 64 + d for h in range(16) for d in range(64)]
    sh["mwkv"] = kmajor(np.ascontiguousarray(f(inputs["mla_w_kv_up"])[0][:, permkv]))
    sh["mwo"] = kmajor(f(inputs["mla_w_out"])[0])
    gsm = np.zeros((128, 10), np.float32)
    gsm[:, 0:6] = pp_layout(f(inputs["mla_q_norm_g"])[0], 6)
    gsm[:, 6:8] = pp_layout(f(inputs["mla_kv_norm_g"])[0], 2)
    gsm[0:64, 8] = f(inputs["mla_qn_nope_g"])[0]
    gsm[64:96, 8] = 1.0
    gsm[0:64, 9] = f(inputs["mla_kn_nope_g"])[0]
    gsm[64:96, 9] = 1.0
    sh["mgsm"] = gsm
    sh["mrg"] = np.concatenate([f(inputs["mla_qn_rope_g"])[0], f(inputs["mla_kn_rope_g"])[0]])[None, :].copy()
    sh["ropem"] = rope_table(32)
    sh["dwin"] = kmajor(f(inputs["diff_w_in"])[0])
    sh["dwo"] = kmajor(f(inputs["diff_w_out"])[0])
    sh["dgn"] = np.concatenate([f(inputs["diff_qn_g"])[0], f(inputs["diff_kn_g"])[0]])[None, :].copy()
    sh["dsub"] = np.ascontiguousarray(f(inputs["diff_sub_g"])[0][:, None])
    sh["dlam"] = np.concatenate([f(inputs["diff_lq1"])[0], f(inputs["diff_lk1"])[0], f(inputs["diff_lq2"])[0], f(inputs["diff_lk2"])[0]])[None, :].copy()
    sh["roped"] = rope_table(64)
    per_core = []
    x = f(inputs["x"])
    ctx = f(inputs["ctx"])
    c = f(inputs["c"])
    cc = f(inputs["c_ctx"])
    for b in range(x.shape[0]):
        m = dict(sh)
        m["xin"] = np.ascontiguousarray(np.concatenate([ctx[b], x[b]], axis=0))
        m["cT"] = np.ascontiguousarray(np.stack([c[b].reshape(8, 128).T, cc.reshape(8, 128).T], axis=2))
        per_core.append(m)
    return per_core


def kernel(**inputs):
    maps = host_prepare(inputs)
    nc = build_program()
    res = run_bass_kernel_spmd(nc, maps, core_ids=list(range(len(maps))))
    return np.stack([np.asarray(r["y"], np.float32) for r in res.results], axis=0)
```
